# Optimizing a Trainium2 kernel written in Bass

```python
import math
import jax, jax.numpy as jnp
from jax import lax
import numpy as np

D_MODEL = 1024
BATCH = 16
SEQ = 2048
DEPTH = 4
DEC_BATCH = 8
DEC_SEQ = 4096
PAST_LEN = 128

GRID_W = 64
HEAD_DIM = 64
RMS_EPS = 1e-6
NEG_INF = -1e30

NA_HEADS = 8
NA_WIN_ROWS = 8
NA_WIN_COLS = 16
NA_COL_BLOCK = 16
NA_KEY_COLS = NA_COL_BLOCK + NA_WIN_COLS

DIFF_HEADS = 4
DIFF_Q_BLOCK = 128

DIL_PAIRS = ((128, 1), (512, 4), (2048, 16))
DIL_GROUPS = 3
DIL_HEADS = 4
DIL_Q_BLOCK = 64

A_W = NA_HEADS * HEAD_DIM
B_QK = DIFF_HEADS * 2 * HEAD_DIM
B_W = DIFF_HEADS * 2 * HEAD_DIM
C_QKV = DIL_GROUPS * DIL_HEADS * HEAD_DIM
C_W = DIL_HEADS * HEAD_DIM
N_BRANCH = 3
IN_SPLITS = (A_W, A_W, A_W, A_W, B_QK, B_QK, B_W, B_W, C_QKV, C_QKV, C_QKV, C_W, N_BRANCH * D_MODEL)
IN_WIDTH = 4 * A_W + 2 * B_QK + 2 * B_W + 3 * C_QKV + C_W + N_BRANCH * D_MODEL

kernel_name = 'hybrid_natten_diff_dilated_encoder'


def rms_norm(x, g):
    xf = x.astype(jnp.float32)
    y = xf * lax.rsqrt(jnp.mean(xf * xf, axis=-1, keepdims=True) + RMS_EPS)
    return (y * g.astype(jnp.float32)).astype(x.dtype)


def alibi_slopes(n):
    return jnp.asarray(2.0 ** (-8.0 * np.arange(1, n + 1) / n), dtype=jnp.float32)


def neighborhood_attention(q, k, v, rpb):
    b, l, h, dh = q.shape
    rows = l // GRID_W
    kh = min(NA_WIN_ROWS, rows)
    n_cb = GRID_W // NA_COL_BLOCK
    qcol = np.arange(GRID_W).reshape(n_cb, NA_COL_BLOCK)
    qstart = np.clip(qcol - NA_WIN_COLS // 2, 0, GRID_W - NA_WIN_COLS)
    kstart = np.clip(np.arange(n_cb) * NA_COL_BLOCK - NA_WIN_COLS // 2, 0, GRID_W - NA_KEY_COLS)
    kcol = kstart[:, None] + np.arange(NA_KEY_COLS)
    col_ok = (kcol[:, None, :] >= qstart[:, :, None]) & (kcol[:, None, :] < qstart[:, :, None] + NA_WIN_COLS)
    col_idx = np.clip(kcol[:, None, :] - qcol[:, :, None] + NA_WIN_COLS - 1, 0, 2 * NA_WIN_COLS - 2)
    mask = jnp.asarray(np.broadcast_to(col_ok[:, None, :, None, :], (n_cb, 1, NA_COL_BLOCK, kh, NA_KEY_COLS)).reshape(n_cb, 1, NA_COL_BLOCK, kh * NA_KEY_COLS))
    row_start = np.clip(np.arange(rows) - kh // 2, 0, rows - kh).astype(np.int32)
    qg = (q * dh ** -0.5).reshape(b, rows, n_cb, NA_COL_BLOCK, h, dh)
    kg = k.reshape(b, rows, GRID_W, h, dh)
    vg = v.reshape(b, rows, GRID_W, h, dh)
    rpb32 = rpb.astype(jnp.float32)

    def one_row(args):
        r, rs = args

        def gather_keys(x):
            x = lax.dynamic_slice_in_dim(x, rs, kh, axis=1)[:, :, kcol]
            return jnp.moveaxis(x, 1, 2).reshape(b, n_cb, kh * NA_KEY_COLS, h, dh)

        kb = gather_keys(kg)
        vb = gather_keys(vg)
        qr = lax.dynamic_index_in_dim(qg, r, axis=1, keepdims=False)
        s = jnp.einsum('bcqhe,bckhe->bchqk', qr, kb).astype(jnp.float32)
        row_off = rs + jnp.arange(kh, dtype=jnp.int32) - r
        bias = rpb32[:, row_off + NA_WIN_ROWS - 1][:, :, col_idx]
        bias = jnp.transpose(bias, (2, 0, 3, 1, 4)).reshape(n_cb, h, NA_COL_BLOCK, kh * NA_KEY_COLS)
        p = jax.nn.softmax(jnp.where(mask, s + bias, NEG_INF), axis=-1).astype(v.dtype)
        return jnp.einsum('bchqk,bckhe->bcqhe', p, vb)

    out = lax.map(one_row, (jnp.arange(rows, dtype=jnp.int32), jnp.asarray(row_start)))
    return jnp.moveaxis(out, 0, 1).reshape(b, l, h * dh)


def diff_attention(q, k, v, lam, slopes, g_diff, lam_init):
    b, l, h, _, dh = q.shape
    nq = l // DIFF_Q_BLOCK
    qb = jnp.moveaxis((q * dh ** -0.5).reshape(b, nq, DIFF_Q_BLOCK, h, 2, dh), 1, 0)
    kpos = jnp.arange(l, dtype=jnp.int32)

    def one_block(args):
        i, qi = args
        s = jnp.einsum('bqhme,bkhme->bhmqk', qi, k).astype(jnp.float32)
        qpos = i * DIFF_Q_BLOCK + jnp.arange(DIFF_Q_BLOCK, dtype=jnp.int32)
        dist = jnp.abs(qpos[:, None] - kpos[None, :]).astype(jnp.float32)
        a = jax.nn.softmax(s - slopes[None, :, None, None, None] * dist, axis=-1)
        w = (a[:, :, 0] - lam * a[:, :, 1]).astype(v.dtype)
        return jnp.einsum('bhqk,bkhe->bqhe', w, v)

    o = lax.map(one_block, (jnp.arange(nq, dtype=jnp.int32), qb))
    o = jnp.moveaxis(o, 0, 1).reshape(b, l, h, 2 * dh)
    o = rms_norm(o, g_diff) * (1.0 - lam_init)
    return o.reshape(b, l, h * 2 * dh)


def dilated_group_attention(q, k, v, window, dilation, slopes):
    b, l, h, dh = q.shape
    n = window // (2 * dilation)
    m = l // dilation
    nb = -(-m // DIL_Q_BLOCK)
    mp = nb * DIL_Q_BLOCK
    span = DIL_Q_BLOCK + 2 * n

    def to_classes(x):
        return jnp.moveaxis(x.reshape(b, m, dilation, h, dh), 2, 1)

    qc = jnp.pad(to_classes(q * dh ** -0.5), ((0, 0), (0, 0), (0, mp - m), (0, 0), (0, 0)))
    qc = qc.reshape(b, dilation, nb, DIL_Q_BLOCK, h, dh)
    pad_k = ((0, 0), (0, 0), (n, mp - m + n), (0, 0), (0, 0))
    idx = np.arange(nb)[:, None] * DIL_Q_BLOCK + np.arange(span)[None, :]
    kb = jnp.pad(to_classes(k), pad_k)[:, :, idx]
    vb = jnp.pad(to_classes(v), pad_k)[:, :, idx]
    s = jnp.einsum('bdnqhe,bdnkhe->bdnhqk', qc, kb).astype(jnp.float32)
    rel = (np.arange(span)[None, :] - n) - np.arange(DIL_Q_BLOCK)[:, None]
    mk = idx - n
    ok = (np.abs(rel)[None] <= n) & (mk[:, None, :] >= 0) & (mk[:, None, :] < m)
    bias = -slopes[:, None, None] * jnp.asarray(dilation * np.abs(rel), dtype=jnp.float32)
    s = jnp.where(jnp.asarray(ok)[None, None, :, None], s + bias[None, None, None], NEG_INF)
    lse = jax.nn.logsumexp(s, axis=-1)
    p = jnp.exp(s - lse[..., None]).astype(v.dtype)
    o = jnp.einsum('bdnhqk,bdnkhe->bdnqhe', p, vb).reshape(b, dilation, mp, h, dh)[:, :, :m]
    o = jnp.moveaxis(o, 1, 2).reshape(b, l, h, dh)
    lse = jnp.moveaxis(lse, -1, -2).reshape(b, dilation, mp, h)[:, :, :m]
    lse = jnp.moveaxis(lse, 1, 2).reshape(b, l, h)
    return o, lse


def dilated_attention(q, k, v, slopes):
    b, l, _, dh = q.shape
    outs, lses = [], []
    for g, (window, dilation) in enumerate(DIL_PAIRS):
        sl = slice(g * DIL_HEADS, (g + 1) * DIL_HEADS)
        o, lse = dilated_group_attention(q[:, :, sl], k[:, :, sl], v[:, :, sl], window, dilation, slopes[sl])
        outs.append(o)
        lses.append(lse)
    wts = jax.nn.softmax(jnp.stack(lses, axis=0), axis=0)
    o = jnp.sum(wts[..., None] * jnp.stack(outs, axis=0).astype(jnp.float32), axis=0)
    return o.astype(q.dtype).reshape(b, l, DIL_HEADS * dh)


def encoder_layer(x, lam_init, g_norm, w_in, b_gate, rpb, lam_qk, g_diff, w_br_a, w_br_b, w_br_c, w_out):
    b, t, _ = x.shape
    h = rms_norm(x, g_norm)
    u = h @ w_in
    offs = [int(o) for o in np.cumsum(IN_SPLITS)[:-1]]
    qa, ka, va, za, qb, kb, vb, zb, qc, kc, vc, zc, gl = jnp.split(u, offs, axis=-1)
    ya = neighborhood_attention(qa.reshape(b, t, NA_HEADS, HEAD_DIM), ka.reshape(b, t, NA_HEADS, HEAD_DIM),
                                va.reshape(b, t, NA_HEADS, HEAD_DIM), rpb)
    lq = lam_qk.astype(jnp.float32)
    lam = jnp.exp(jnp.sum(lq[0] * lq[1])) - jnp.exp(jnp.sum(lq[2] * lq[3])) + lam_init
    yb = diff_attention(qb.reshape(b, t, DIFF_HEADS, 2, HEAD_DIM), kb.reshape(b, t, DIFF_HEADS, 2, HEAD_DIM),
                        vb.reshape(b, t, DIFF_HEADS, 2 * HEAD_DIM), lam, alibi_slopes(DIFF_HEADS), g_diff, lam_init)
    nh_c = DIL_GROUPS * DIL_HEADS
    yc = dilated_attention(qc.reshape(b, t, nh_c, HEAD_DIM), kc.reshape(b, t, nh_c, HEAD_DIM),
                           vc.reshape(b, t, nh_c, HEAD_DIM), alibi_slopes(nh_c))
    pa = (ya * jax.nn.silu(za)) @ w_br_a
    pb = (yb * jax.nn.silu(zb)) @ w_br_b
    pc = (yc * jax.nn.silu(zc)) @ w_br_c
    gates = jax.nn.sigmoid((gl.reshape(b, t, N_BRANCH, D_MODEL) + b_gate).astype(jnp.float32)).astype(x.dtype)
    merged = gates[:, :, 0] * pa + gates[:, :, 1] * pb + gates[:, :, 2] * pc
    return x + merged @ w_out


def trunk(x, g_norm, w_in, b_gate, rpb, lam_qk, g_diff, w_br_a, w_br_b, w_br_c, w_out, g_final):
    for l in range(DEPTH):
        lam_init = 0.8 - 0.6 * math.exp(-0.3 * l)
        x = encoder_layer(x, lam_init, g_norm[l], w_in[l], b_gate[l], rpb[l], lam_qk[l], g_diff[l],
                          w_br_a[l], w_br_b[l], w_br_c[l], w_out[l])
    return rms_norm(x, g_final)


def setup_inputs(seed: int = 0) -> dict:
    key = jax.random.key(seed)
    ks = jax.random.split(key, 13)
    f32 = jnp.float32
    nrm = lambda k_, shape: jax.random.normal(k_, shape, dtype=f32)
    return {
        'x_prompt': nrm(ks[0], (BATCH, SEQ, D_MODEL)),
        'x_sample': nrm(ks[1], (DEC_BATCH, DEC_SEQ, D_MODEL)),
        'g_norm': 1.0 + 0.05 * nrm(ks[2], (DEPTH, D_MODEL)),
        'w_in': nrm(ks[3], (DEPTH, D_MODEL, IN_WIDTH)) * D_MODEL ** -0.5,
        'b_gate': 0.1 * nrm(ks[4], (DEPTH, N_BRANCH, D_MODEL)),
        'rpb': 0.1 * nrm(ks[5], (DEPTH, NA_HEADS, 2 * NA_WIN_ROWS - 1, 2 * NA_WIN_COLS - 1)),
        'lam_qk': 0.1 * nrm(ks[6], (DEPTH, 4, HEAD_DIM)),
        'g_diff': 1.0 + 0.05 * nrm(ks[7], (DEPTH, 2 * HEAD_DIM)),
        'w_br_a': nrm(ks[8], (DEPTH, A_W, D_MODEL)) * A_W ** -0.5,
        'w_br_b': nrm(ks[9], (DEPTH, B_W, D_MODEL)) * B_W ** -0.5,
        'w_br_c': nrm(ks[10], (DEPTH, C_W, D_MODEL)) * C_W ** -0.5,
        'w_out': nrm(ks[11], (DEPTH, D_MODEL, D_MODEL)) * D_MODEL ** -0.5,
        'g_final': 1.0 + 0.05 * nrm(ks[12], (D_MODEL,)),
    }


def reference(x_prompt, x_sample, g_norm, w_in, b_gate, rpb, lam_qk, g_diff, w_br_a, w_br_b, w_br_c, w_out, g_final):
    y_prompt = trunk(x_prompt, g_norm, w_in, b_gate, rpb, lam_qk, g_diff, w_br_a, w_br_b, w_br_c, w_out, g_final)
    y_sample = trunk(x_sample, g_norm, w_in, b_gate, rpb, lam_qk, g_diff, w_br_a, w_br_b, w_br_c, w_out, g_final)
    return (y_prompt, y_sample)
```

```python
import contextlib
import math
import numpy as np
import ml_dtypes
import concourse.bass as bass
import concourse.mybir as mybir
from concourse.bass_utils import run_bass_kernel_spmd

F32 = mybir.dt.float32
BF16 = mybir.dt.bfloat16
AF = mybir.ActivationFunctionType
ALU = mybir.AluOpType

D = 1024
DEPTH = 4
NTOK = 8192
SEQS = [(0, 2048), (2048, 2048), (4096, 4096)]
INW = 9728
QA, KA, VA, ZA, QB, KB, VB, ZB, QC, KC, VC, ZC, GL = 0, 512, 1024, 1536, 2048, 2560, 3072, 3584, 4096, 4864, 5632, 6400, 6656
NEG = -30000.0
EPS = 1e-6
DIL = ((128, 1), (512, 4), (2048, 16))
EPOCH = 30000

FM_BLOCKS = []
for i in range(4): FM_BLOCKS.append((QA + 128 * i, 128, False))
for i in range(4): FM_BLOCKS.append((KA + 128 * i, 128, False))
for i in range(4): FM_BLOCKS.append((ZA + 128 * i, 128, True))
for i in range(8): FM_BLOCKS.append((QB + 64 * i, 64, False))
for i in range(8): FM_BLOCKS.append((KB + 64 * i, 64, False))
for i in range(4): FM_BLOCKS.append((ZB + 128 * i, 128, True))
for i in range(6): FM_BLOCKS.append((QC + 128 * i, 128, False))
for i in range(6): FM_BLOCKS.append((KC + 128 * i, 128, False))
for i in range(2): FM_BLOCKS.append((ZC + 128 * i, 128, True))
B_QA, B_KA, B_ZA, B_QB, B_KB, B_ZB, B_QC, B_KC, B_ZC = 0, 4, 8, 12, 20, 28, 32, 38, 44
NB = len(FM_BLOCKS)
TM_GROUPS = [(VA, 512, 0, 0), (VB, 512, 1, 0), (VC, 512, 2, 0), (VC + 512, 256, 2, 512)]


class _Skip(Exception):
    pass


def _maybe_skip(st, cond):
    if cond:
        st.push(lambda et, ev, tb: et is _Skip)
        raise _Skip()


class Tk:
    __slots__ = ("name", "w", "r", "dsem")

    def __init__(self, name):
        self.name = name
        self.w = None
        self.r = {}
        self.dsem = None


class Eng:
    def __init__(self, name, h):
        self.name = name
        self.h = h
        self.count = 0
        self.sems = []
        self.waited = {}


class Sched:
    def __init__(self, nc, es):
        self.nc = nc
        self.es = es
        self.E = {n: Eng(n, h) for n, h in (("pe", nc.tensor), ("act", nc.scalar), ("dve", nc.vector),
                                            ("pool", nc.gpsimd), ("sp", nc.sync))}
        self.nsem = 0
        self.dfree = []
        self.dcount = {}
        self.live_dsem = []
        self.rr = 0

    def new_sem(self, name):
        self.nsem += 1
        assert self.nsem < 140, "too many semaphores"
        return self.es.enter_context(self.nc.semaphore(f"{name}{self.nsem}"))

    def eng_sem(self, ename, idx):
        e = self.E[ename]
        k = (idx - 1) // EPOCH
        while len(e.sems) <= k:
            e.sems.append(self.new_sem("e" + ename))
        return e.sems[k], (idx - 1) % EPOCH + 1

    def wait_tok(self, e, tok):
        if tok is None:
            return
        if tok[0] == "e":
            if tok[1] == "pe" and e.name == "pe":
                return
            sem, val = self.eng_sem(tok[1], tok[2])
        else:
            sem, val = tok[1], tok[2]
        key = id(sem)
        if e.waited.get(key, 0) >= val:
            return
        e.h.wait_ge(sem, val)
        e.waited[key] = val

    def _deps(self, e, reads, writes):
        for t in reads:
            self.wait_tok(e, t.w)
        for t in writes:
            self.wait_tok(e, t.w)
            for tok in t.r.values():
                self.wait_tok(e, tok)

    def _mark(self, tok, reads, writes, rkey):
        for t in reads:
            t.r[rkey] = tok
        for t in writes:
            t.w = tok
            t.r = {}

    def op(self, ename, fn, reads=(), writes=(), signal=True):
        e = self.E[ename]
        self._deps(e, reads, writes)
        inst = fn(e.h)
        if not signal:
            self._mark(("e", ename, e.count + 1), reads, writes, ename)
            return
        e.count += 1
        sem, val = self.eng_sem(ename, e.count)
        inst.then_inc(sem, 1)
        self._mark(("e", ename, e.count), reads, writes, ename)

    def dma(self, out, in_, reads=(), writes=(), q="sp"):
        e = self.E[q]
        for t in reads:
            self.wait_tok(e, t.w)
        for t in writes:
            if not (t.w is not None and t.w[0] == "d" and t.dsem is not None and t.w[1] is t.dsem[0]):
                self.wait_tok(e, t.w)
            for tok in t.r.values():
                self.wait_tok(e, tok)
        t = writes[0] if writes else reads[0]
        if t.dsem is None:
            if self.dfree:
                t.dsem = self.dfree.pop()
            else:
                t.dsem = [self.new_sem("d"), 0]
            self.live_dsem.append(t.dsem)
        ds = t.dsem
        inst = e.h.dma_start(out=out, in_=in_)
        ds[1] += 16
        inst.then_inc(ds[0], 16)
        tok = ("d", ds[0], ds[1])
        self._mark(tok, reads, writes, id(ds[0]))

    def barrier(self):
        sp = self.E["sp"]
        for n, e in self.E.items():
            if n != "sp" and e.count > 0:
                self.wait_tok(sp, ("e", n, e.count))
        for ds in self.live_dsem:
            self.wait_tok(sp, ("d", ds[0], ds[1]))
        inst = sp.h.nop()
        sp.count += 1
        sem, val = self.eng_sem("sp", sp.count)
        inst.then_inc(sem, 1)
        for n, e in self.E.items():
            if n != "sp":
                self.wait_tok(e, ("e", "sp", sp.count))
        for ds in self.live_dsem:
            if ds[1] < 20000:
                self.dfree.append(ds)
        self.live_dsem = []


class Pipe:
    def __init__(self, la):
        self.la = la
        self.q = []
        self.d = []

    def push(self, front, back):
        front()
        self.q.append(back)
        if len(self.q) > self.la:
            self.q.pop(0)()
        for e in self.d:
            e[0] -= 1
        while self.d and self.d[0][0] <= 0:
            self.d.pop(0)[1]()

    def defer(self, fn, delay):
        self.d.append([delay, fn])

    def flush(self):
        while self.q:
            self.q.pop(0)()
        while self.d:
            self.d.pop(0)[1]()


def build_nc(depth=DEPTH, debug=False, stages="2ABC6"):
    nc = bass.Bass("TRN2", target_bir_lowering=False)
    dt_in = lambda name, shape, dt=F32: nc.dram_tensor(name, list(shape), dt, kind="ExternalInput").ap()
    x_in = dt_in("x", [NTOK, D])
    w_in = dt_in("w_in", [DEPTH, D, INW])
    w_bra = dt_in("w_br_a", [DEPTH, 512, D])
    w_brb = dt_in("w_br_b", [DEPTH, 512, D])
    w_brc = dt_in("w_br_c", [DEPTH, 256, D])
    w_out = dt_in("w_out", [DEPTH, D, D])
    gk_in = dt_in("gk", [DEPTH, 128, 8])
    bg_in = dt_in("bg", [DEPTH, 128, 24])
    gd_in = dt_in("gd", [128, DEPTH])
    lq_in = dt_in("lq", [DEPTH, 256])
    gf_in = dt_in("gf", [D])
    trev_in = dt_in("trev", [DEPTH, 8, 64, 15, 64])
    ident_in = dt_in("ident", [128, 128], BF16)
    qaug_in = dt_in("qaug", [5, 4096], BF16)
    kaug_in = dt_in("kaug", [4, 2, 5, 4096], BF16)
    cdiag_in = dt_in("cdiag", [128, 4, 128])
    dilb_in = dt_in("dilb", [128, 12, 256])
    y_out = nc.dram_tensor("y", [NTOK, D], F32, kind="ExternalOutput").ap()
    skind = "ExternalOutput" if debug else "Internal"
    XS = nc.dram_tensor("xs", [NTOK, D], F32, kind=skind).ap()
    HT = nc.dram_tensor("ht", [8, 128, NTOK], BF16, kind=skind).ap()
    FM = nc.dram_tensor("fm", [NB, 128, NTOK], BF16, kind=skind).ap()
    VT = [nc.dram_tensor("vta", [NTOK, 512], BF16, kind=skind).ap(),
          nc.dram_tensor("vtb", [NTOK, 512], BF16, kind=skind).ap(),
          nc.dram_tensor("vtc", [NTOK, 768], BF16, kind=skind).ap()]
    YG = nc.dram_tensor("yg", [10, 128, NTOK], BF16, kind=skind).ap()

    with contextlib.ExitStack() as es:
        S = Sched(nc, es)
        op, dma = S.op, S.dma

        uid = {"n": 0}

        def sb(st, name, shape, dt):
            uid["n"] += 1
            return st.enter_context(nc.sbuf_tensor(f"s{uid['n']}_{name}", list(shape), dt))

        pbanks = [es.enter_context(nc.psum_tensor(f"pb{i}", [128, 512], F32)) for i in range(7)]
        ptr_tk = Tk("ptr")
        pb_tk = [Tk(f"pb{i}") for i in range(7)]
        pstate = {"i": 0, "ring": list(range(7))}

        def set_ring(r):
            pstate["ring"] = list(r)
            pstate["i"] = 0

        def next_ps():
            ring = pstate["ring"]
            i = ring[pstate["i"] % len(ring)]
            pstate["i"] += 1
            return pbanks[i], pb_tk[i]

        rr = {"i": 0}

        def evac_eng():
            rr["i"] ^= 1
            return "act" if rr["i"] else "dve"

        def copy_op(ename, out, in_, reads, writes):
            if ename == "act":
                op("act", lambda h: h.copy(out=out, in_=in_), reads, writes)
            else:
                op(ename, lambda h: h.tensor_copy(out=out, in_=in_), reads, writes)

        ident = sb(es, "ident", [128, 128], BF16); ident_tk = Tk("ident")
        ones_b = sb(es, "ones_b", [128, 128], BF16); ones_tk = Tk("ones")
        eps_t = sb(es, "eps_t", [128, 1], F32); eps_tk = Tk("eps")
        gk_t = sb(es, "gk_t", [128, DEPTH, 8], F32); gk_tk = Tk("gk")
        bg_t = sb(es, "bg_t", [128, DEPTH, 24], F32); bg_tk = Tk("bg")
        gd_t = sb(es, "gd_t", [128, DEPTH], F32); gd_tk = Tk("gd")
        gdl_t = sb(es, "gdl_t", [128, DEPTH], F32); gdl_tk = Tk("gdl")
        nlam_t = sb(es, "nlam_t", [128, DEPTH], F32); nlam_tk = Tk("nlam")
        lq_t = sb(es, "lq_t", [128, DEPTH, 256], F32); lq_tk = Tk("lq")
        lqp_t = sb(es, "lqp_t", [128, 256], F32); lqp_tk = Tk("lqp")
        lqs_t = sb(es, "lqs_t", [128, 4], F32); lqs_tk = Tk("lqs")
        gf_t = sb(es, "gf_t", [128, D], F32); gf_tk = Tk("gf")
        dma(ident[:], ident_in, writes=[ident_tk])
        dma(gk_t[:], gk_in.rearrange("l p c -> p l c"), writes=[gk_tk])
        dma(bg_t[:], bg_in.rearrange("l p c -> p l c"), writes=[bg_tk])
        dma(gd_t[:], gd_in, writes=[gd_tk])
        dma(gf_t[:], gf_in.partition_broadcast(128), writes=[gf_tk])
        for l in range(DEPTH):
            dma(lq_t[:, l, :], lq_in[l].partition_broadcast(128), writes=[lq_tk])
        op("pool", lambda h: h.memset(ones_b[:], 1.0), writes=[ones_tk])
        ones_f = sb(es, "ones_f", [128, 128], F32); onesf_tk = Tk("onesf")
        op("pool", lambda h: h.memset(ones_f[:], 1.0), writes=[onesf_tk])
        op("pool", lambda h: h.memset(eps_t[:], EPS), writes=[eps_tk])
        for l in range(depth):
            lam_init = 0.8 - 0.6 * math.exp(-0.3 * l)
            op("dve", lambda h: h.tensor_tensor(out=lqp_t[:, 0:64], in0=lq_t[:, l, 0:64], in1=lq_t[:, l, 64:128], op=ALU.mult),
               [lq_tk], [lqp_tk])
            op("dve", lambda h: h.tensor_tensor(out=lqp_t[:, 64:128], in0=lq_t[:, l, 128:192], in1=lq_t[:, l, 192:256], op=ALU.mult),
               [lq_tk], [lqp_tk])
            op("dve", lambda h: h.reduce_sum(out=lqs_t[:, 0:1], in_=lqp_t[:, 0:64], axis=mybir.AxisListType.X), [lqp_tk], [lqs_tk])
            op("dve", lambda h: h.reduce_sum(out=lqs_t[:, 1:2], in_=lqp_t[:, 64:128], axis=mybir.AxisListType.X), [lqp_tk], [lqs_tk])
            op("act", lambda h: h.activation(out=lqs_t[:, 2:4], in_=lqs_t[:, 0:2], func=AF.Exp), [lqs_tk], [lqs_tk])
            op("dve", lambda h: h.scalar_tensor_tensor(out=nlam_t[:, l:l + 1], in0=lqs_t[:, 3:4], scalar=-lam_init, in1=lqs_t[:, 2:3],
                                                       op0=ALU.add, op1=ALU.subtract), [lqs_tk], [nlam_tk])
            op("dve", lambda h: h.tensor_scalar(out=gdl_t[:, l:l + 1], in0=gd_t[:, l:l + 1], scalar1=(1.0 - lam_init), scalar2=None,
                                                op0=ALU.mult), [gd_tk], [gdl_tk])
        S.barrier()

        lc_state = {"k": 0}

        def make_stg(st_stack, piece=512):
            return ([sb(st_stack, f"wstg{i}", [128, 8, piece], F32) for i in range(4)], [Tk(f"wstg{i}") for i in range(4)])

        def load_cast(stgs, dst, dst_tk, src3, nk, ncols, gl, piece=512):
            stg, stg_tk = stgs
            k = lc_state["k"]
            for c0 in range(0, ncols, piece):
                w = min(piece, ncols - c0)
                b = k % 4
                dma(stg[b][:, 0:nk, 0:w], src3[:, :, c0:c0 + w], writes=[stg_tk[b]])
                for c in range(nk):
                    en = "act" if (k * nk + c) % 2 else "dve"
                    if gl is not None:
                        if en == "act":
                            op("act", lambda h: h.activation(out=dst[:, c, c0:c0 + w], in_=stg[b][:, c, 0:w], func=AF.Copy, scale=gk_t[:, gl, c:c + 1]),
                               [stg_tk[b], gk_tk], [])
                        else:
                            op("dve", lambda h: h.tensor_scalar(out=dst[:, c, c0:c0 + w], in0=stg[b][:, c, 0:w], scalar1=gk_t[:, gl, c:c + 1],
                                                                scalar2=None, op0=ALU.mult), [stg_tk[b], gk_tk], [])
                    else:
                        if en == "act":
                            op("act", lambda h: h.copy(out=dst[:, c, c0:c0 + w], in_=stg[b][:, c, 0:w]), [stg_tk[b]], [])
                        else:
                            op("dve", lambda h: h.tensor_copy(out=dst[:, c, c0:c0 + w], in_=stg[b][:, c, 0:w]), [stg_tk[b]], [])
                k += 1
            lc_state["k"] = k

        def rms_rstd(st, xap, x_tk, junk, junk_tk, ss, ss_tk, rstd, rstd_tk):
            op("act", lambda h: h.activation(out=junk, in_=xap, func=AF.Square, accum_out=ss), [x_tk], [junk_tk, ss_tk])
            op("act", lambda h: h.activation(out=ss, in_=ss, func=AF.Sqrt, bias=eps_t[:], scale=1.0 / D), [ss_tk, eps_tk], [ss_tk])
            op("dve", lambda h: h.reciprocal(out=rstd, in_=ss), [ss_tk], [rstd_tk])

        for l in range(depth):
            x_src = x_in if l == 0 else XS
            with contextlib.ExitStack() as st:
                _maybe_skip(st, "2" not in stages)
                ptr = st.enter_context(nc.psum_tensor(f"ptr{l}", [128, 1024], BF16))
                W2 = sb(st, "W2", [128, 8, GL], BF16); W2_tk = Tk("W2")
                with contextlib.ExitStack() as st2:
                    stgs = make_stg(st2)
                    load_cast(stgs, W2, W2_tk, w_in[l, :, 0:GL].rearrange("(c p) n -> p c n", p=128), 8, GL, l)
                    S.barrier()
                NX = 8
                xt = [sb(st, f"xt{i}", [128, D], F32) for i in range(NX)]; xt_tk = [Tk(f"xt{i}") for i in range(NX)]
                hb = [sb(st, f"hb{i}", [128, D], BF16) for i in range(2)]; hb_tk = [Tk(f"hb{i}") for i in range(2)]
                junk = sb(st, "junk", [128, D], BF16); junk_tk = Tk("junk")
                ss = [sb(st, f"ss{i}", [128, 1], F32) for i in range(2)]; ss_tk = [Tk(f"ss{i}") for i in range(2)]
                rstd = [sb(st, f"rstd{i}", [128, 1], F32) for i in range(2)]; rstd_tk = [Tk(f"rstd{i}") for i in range(2)]
                hT = [sb(st, f"hT{i}", [128, 8, 512], BF16) for i in range(2)]; hT_tk = [Tk(f"hT{i}") for i in range(2)]
                GS = 8
                fst = [sb(st, f"fst{i}", [128, GS, 512], BF16) for i in range(2)]; fst_tk = [Tk(f"fst{i}") for i in range(2)]
                vst = [sb(st, f"vst{i}", [128, 1792], BF16) for i in range(2)]; vst_tk = [Tk(f"vst{i}") for i in range(2)]
                NT = NTOK // 512
                sub = 0
                fgrp = 0
                vgrp = 0

                def s2_load(T):
                    for s_ in range(4):
                        t0 = T * 512 + s_ * 128
                        xb = (T * 4 + s_) % NX
                        dma(xt[xb][:], x_src[t0:t0 + 128, :], writes=[xt_tk[xb]])

                def s2_front(T):
                    nonlocal sub
                    hb_ = T % 2
                    for s_ in range(4):
                        xb = (T * 4 + s_) % NX
                        b2 = sub % 2
                        sub += 1
                        rms_rstd(st, xt[xb][:], xt_tk[xb], junk[:], junk_tk, ss[b2][:], ss_tk[b2], rstd[b2][:], rstd_tk[b2])
                        op("dve", lambda h: h.tensor_scalar(out=hb[b2][:], in0=xt[xb][:], scalar1=rstd[b2][:], scalar2=None, op0=ALU.mult),
                           [xt_tk[xb], rstd_tk[b2]], [hb_tk[b2]])
                        for c in range(8):
                            op("pe", lambda h: h.transpose(ptr[:, c * 128:(c + 1) * 128], hb[b2][:, c * 128:(c + 1) * 128], ident[:]),
                               [hb_tk[b2], ident_tk], [ptr_tk], signal=(c == 7))
                        copy_op(evac_eng(), hT[hb_][:, :, s_ * 128:(s_ + 1) * 128], ptr[:].rearrange("p (c t) -> p c t", c=8),
                                [ptr_tk], [hT_tk[hb_]])
                    dma(HT[:, :, T * 512:(T + 1) * 512].rearrange("c p t -> p c t"), hT[hb_][:], reads=[hT_tk[hb_]])

                s2_load(0)
                s2_load(1)
                s2_front(0)
                for T in range(NT):
                    hb_ = T % 2
                    for g0 in range(0, NB, GS):
                        if g0 == 2 * GS and T + 1 < NT:
                            s2_front(T + 1)
                            if T + 2 < NT:
                                s2_load(T + 2)
                        nb = min(GS, NB - g0)
                        fb = fgrp % 2
                        fgrp += 1
                        for bi in range(nb):
                            col, M, silu = FM_BLOCKS[g0 + bi]
                            if M == 64 and (g0 + bi) % 2 == 1:
                                continue
                            MM = 128
                            ps, ps_tk = next_ps()
                            for c in range(8):
                                op("pe", lambda h: h.matmul(ps[0:MM, :], lhsT=W2[:, c, col:col + MM], rhs=hT[hb_][:, c, :],
                                                            start=(c == 0), stop=(c == 7)), [W2_tk, hT_tk[hb_]], [ps_tk], signal=(c == 7))
                            if silu:
                                op("act", lambda h: h.activation(out=fst[fb][0:M, bi, :], in_=ps[0:M, :], func=AF.Silu), [ps_tk], [fst_tk[fb]])
                            elif M == 64:
                                copy_op("act", fst[fb][0:64, bi, :], ps[0:64, :], [ps_tk], [fst_tk[fb]])
                                copy_op("dve", fst[fb][0:64, bi + 1, :], ps[64:128, :], [ps_tk], [fst_tk[fb]])
                            else:
                                copy_op(evac_eng(), fst[fb][0:M, bi, :], ps[0:M, :], [ps_tk], [fst_tk[fb]])
                        i0 = 0
                        while i0 < nb:
                            M0 = FM_BLOCKS[g0 + i0][1]
                            i1 = i0
                            while i1 < nb and FM_BLOCKS[g0 + i1][1] == M0:
                                i1 += 1
                            dma(FM[g0 + i0:g0 + i1, 0:M0, T * 512:(T + 1) * 512].rearrange("b p t -> p b t"), fst[fb][0:M0, i0:i1, :], reads=[fst_tk[fb]])
                            i0 = i1
                    for s_ in range(4):
                        vb_ = vgrp % 2
                        vgrp += 1
                        t0 = T * 512 + s_ * 128
                        off = 0
                        for (col, w, ti, dcol) in TM_GROUPS:
                            ps, ps_tk = next_ps()
                            for c in range(8):
                                op("pe", lambda h: h.matmul(ps[:, 0:w], lhsT=hT[hb_][:, c, s_ * 128:(s_ + 1) * 128], rhs=W2[:, c, col:col + w],
                                                            start=(c == 0), stop=(c == 7)), [W2_tk, hT_tk[hb_]], [ps_tk], signal=(c == 7))
                            copy_op(evac_eng(), vst[vb_][:, off:off + w], ps[:, 0:w], [ps_tk], [vst_tk[vb_]])
                            off += w
                        dma(VT[0][t0:t0 + 128, :], vst[vb_][:, 0:512], reads=[vst_tk[vb_]])
                        dma(VT[1][t0:t0 + 128, :], vst[vb_][:, 512:1024], reads=[vst_tk[vb_]])
                        dma(VT[2][t0:t0 + 128, :], vst[vb_][:, 1024:1792], reads=[vst_tk[vb_]])
                S.barrier()

            with contextlib.ExitStack() as st:
                _maybe_skip(st, "A" not in stages)
                pb7 = st.enter_context(nc.psum_tensor(f"pb7_{l}a", [128, 512], F32))
                pbanks[7:] = [pb7]
                pb_tk[7:] = [Tk("pb7")]
                LA = 4
                pipe = Pipe(LA)
                QP = sb(st, "QP", [128, NTOK], BF16); QP_tk = Tk("QP")
                KP = sb(st, "KP", [128, NTOK], BF16); KP_tk = Tk("KP")
                ZP = sb(st, "ZP", [128, NTOK], BF16); ZP_tk = Tk("ZP")
                YP = sb(st, "YP", [128, NTOK], BF16); YP_tk = Tk("YP")
                tdup = [sb(st, f"tdup{i}", [128, 15, 64], F32) for i in range(2)]; tdup_tk = [Tk(f"tdup{i}") for i in range(2)]
                BT = [sb(st, f"BT{i}", [128, 20, 512], F32) for i in range(2)]
                BT_tks = [[Tk(f"BT{i}_{j}") for j in range(20)] for i in range(2)]
                VG = [sb(st, f"VG{i}", [128, 32, 128], BF16) for i in range(2)]; VG_tk = [Tk(f"VG{i}") for i in range(2)]
                NTMP = 4
                tmp = [sb(st, f"tmp{i}", [128, 512], F32) for i in range(NTMP)]; tmp_tk = [Tk(f"tmp{i}") for i in range(NTMP)]
                NPT = LA + 2
                PT = [sb(st, f"PT{i}", [128, 512], BF16) for i in range(NPT)]; PT_tk = [Tk(f"PT{i}") for i in range(NPT)]
                rz = [sb(st, f"rz{i}", [128, 512], F32) for i in range(2)]; rz_tk = [Tk(f"rz{i}") for i in range(2)]
                g2 = [sb(st, f"g2{i}", [128, 512], F32) for i in range(2)]; g2_tk = [Tk(f"g2{i}") for i in range(2)]
                set_ring([3, 4, 5, 6, 7])
                op("pool", lambda h: h.memset(VG[0][:, :, 64:128], 1.0), writes=[VG_tk[0]])
                op("pool", lambda h: h.memset(VG[1][:, :, 0:64], 1.0), writes=[VG_tk[1]])
                tbase = {0: 0, 1: 6, 2: 14}

                def build_bias(head):
                    bb = head % 2
                    td, td_tk = tdup[bb], tdup_tk[bb]
                    dma(td[0:64], trev_in[l, head], writes=[td_tk])
                    dma(td[64:128], trev_in[l, head], writes=[td_tk])
                    for ti_ in range(20):
                        op("pool", lambda h: h.memset(BT[bb][:, ti_, :], NEG), writes=[BT_tks[bb][ti_]])
                    rows_total = 32
                    for v, (R, ch0, nch) in enumerate(((0, 0, 6), (8, 2, 8), (rows_total - 8, (rows_total - 12) // 2, 6))):
                        for j in range(nch):
                            ti = tbase[v] + j
                            for kr in range(2):
                                krow = 2 * (ch0 + j) + kr
                                ent = []
                                for qr in range(8):
                                    qrow = R + qr
                                    rs = min(max(qrow - 4, 0), rows_total - 8)
                                    ent.append(14 - (krow - qrow + 7) if rs <= krow < rs + 8 else None)
                                qr = 0
                                while qr < 8:
                                    if ent[qr] is None:
                                        qr += 1
                                        continue
                                    q1 = qr
                                    while q1 + 1 < 8 and ent[q1 + 1] is not None and ent[q1 + 1] == ent[q1] + 1:
                                        q1 += 1
                                    n = q1 - qr + 1
                                    e0 = ent[qr]
                                    pl = 64 * kr
                                    op("pool", lambda h: h.tensor_copy(out=BT[bb][pl:pl + 64, ti, qr * 64:(qr + n) * 64].rearrange("p (a b) -> p a b", b=64),
                                                                       in_=td[pl:pl + 64, e0:e0 + n, :]), [td_tk], [BT_tks[bb][ti]])
                                    qr = q1 + 1

                acnt = 0
                pcnt = 0
                build_bias(0)
                for i in range(4):
                    dma(QP[:], FM[B_QA + i], writes=[QP_tk])
                    dma(KP[:], FM[B_KA + i], writes=[KP_tk])
                    dma(ZP[:], FM[B_ZA + i], writes=[ZP_tk])
                    for hh in range(2):
                        head = 2 * i + hh
                        bb = head % 2
                        ub = 64 * hh
                        zb = 64 - ub
                        if head + 1 < 8:
                            build_bias(head + 1)
                        vt = VG[hh]
                        vt_tk = VG_tk[hh]
                        vc0 = 0 if hh == 0 else 64
                        for (s0, L) in SEQS:
                            rows = L // 64
                            nqt = rows // 8
                            dma(vt[:, 0:L // 128, vc0:vc0 + 64],
                                VT[0][s0:s0 + L, head * 64:(head + 1) * 64].rearrange("(c p) e -> p c e", p=128), writes=[vt_tk])
                            for qt in range(nqt):
                                if qt == 0:
                                    v, ch0, nch = 0, 0, 6
                                elif qt == nqt - 1:
                                    v, ch0, nch = 2, (rows - 12) // 2, 6
                                else:
                                    v, ch0, nch = 1, 4 * qt - 2, 8
                                ab = acnt % 2
                                a3 = acnt % 3
                                acnt += 1
                                acc, acc_tk = pbanks[a3], pb_tk[a3]
                                q0 = s0 + qt * 512
                                for j in range(nch):
                                    kc = ch0 + j
                                    ti = tbase[v] + j
                                    tb = pcnt % NTMP
                                    pb = pcnt % NPT
                                    pcnt += 1
                                    vq = []
                                    for qr in range(8):
                                        qrow = 8 * qt + qr
                                        rs_ = min(max(qrow - 4, 0), rows - 8)
                                        if any(rs_ <= 2 * kc + kr < rs_ + 8 for kr in range(2)):
                                            vq.append(qr)
                                    assert vq and vq == list(range(vq[0], vq[-1] + 1))
                                    cl, ch_ = 64 * vq[0], 64 * (vq[-1] + 1)

                                    def front(kc=kc, ti=ti, tb=tb, pb=pb, q0=q0, s0=s0, ub=ub, bb=bb, cl=cl, ch_=ch_, j=j, acc=acc, acc_tk=acc_tk):
                                        if j == 0:
                                            op("dve", lambda h: h.memset(acc[:], 0.0), [], [acc_tk])
                                        ps, ps_tk = next_ps()
                                        op("pe", lambda h: h.matmul(ps[:, cl:ch_], lhsT=KP[ub:ub + 64, s0 + kc * 128:s0 + (kc + 1) * 128],
                                                                    rhs=QP[ub:ub + 64, q0 + cl:q0 + ch_], start=True, stop=True), [KP_tk, QP_tk], [ps_tk])
                                        op("dve", lambda h: h.scalar_tensor_tensor(out=tmp[tb][:, cl:ch_], in0=ps[:, cl:ch_], scalar=0.125, in1=BT[bb][:, ti, cl:ch_],
                                                                                   op0=ALU.mult, op1=ALU.add), [ps_tk, BT_tks[bb][ti]], [tmp_tk[tb]])
                                        op("act", lambda h: h.activation(out=PT[pb][:, cl:ch_], in_=tmp[tb][:, cl:ch_], func=AF.Exp), [tmp_tk[tb]], [PT_tk[pb]])

                                    def back(kc=kc, pb=pb, j=j, nch=nch, acc=acc, acc_tk=acc_tk, vt=vt, vt_tk=vt_tk, ub=ub, zb=zb, q0=q0, ab=ab, cl=cl, ch_=ch_):
                                        op("pe", lambda h: h.matmul(acc[:, cl:ch_], lhsT=vt[:, kc, :], rhs=PT[pb][:, cl:ch_], start=False, stop=True,
                                                                    skip_group_check=True), [vt_tk, PT_tk[pb]], [acc_tk])
                                        if j == nch - 1:
                                            def post(acc=acc, acc_tk=acc_tk, ub=ub, zb=zb, q0=q0, ab=ab):
                                                op("act", lambda h: h.activation(out=rz[ab][ub:ub + 64, :], in_=acc[zb:zb + 64, :], func=AF.Ln), [acc_tk], [rz_tk[ab]])
                                                op("act", lambda h: h.activation(out=rz[ab][ub:ub + 64, :], in_=rz[ab][ub:ub + 64, :], func=AF.Exp, scale=-1.0),
                                                   [rz_tk[ab]], [rz_tk[ab]])
                                                op("dve", lambda h: h.tensor_tensor(out=g2[ab][ub:ub + 64, :], in0=rz[ab][ub:ub + 64, :], in1=ZP[ub:ub + 64, q0:q0 + 512],
                                                                                    op=ALU.mult), [rz_tk[ab], ZP_tk], [g2_tk[ab]])
                                                op("dve", lambda h: h.tensor_tensor(out=YP[ub:ub + 64, q0:q0 + 512], in0=acc[ub:ub + 64, :], in1=g2[ab][ub:ub + 64, :],
                                                                                    op=ALU.mult), [acc_tk, g2_tk[ab]], [YP_tk])
                                            pipe.defer(post, 3)
                                    pipe.push(front, back)
                            pipe.flush()
                    dma(YG[i], YP[:], reads=[YP_tk])
                set_ring(range(7))
                del pbanks[7:], pb_tk[7:]
                S.barrier()

            with contextlib.ExitStack() as st:
                _maybe_skip(st, "B" not in stages)
                pb7 = st.enter_context(nc.psum_tensor(f"pb7_{l}b", [128, 512], F32))
                pbanks[7:] = [pb7]
                pb_tk[7:] = [Tk("pb7")]
                LA = 2
                pipe = Pipe(LA)
                LM = 4096
                Qa = [[sb(st, f"Qa{m}{i}", [69, LM], BF16) for m in range(2)] for i in range(2)]
                Kp = [[sb(st, f"Kp{m}{i}", [69, LM], BF16) for m in range(2)] for i in range(2)]
                Kn = [[sb(st, f"Kn{m}{i}", [69, LM], BF16) for m in range(2)] for i in range(2)]
                Qa_tk = [[Tk("Qa") for m in range(2)] for i in range(2)]
                Kp_tk = [[Tk("Kp") for m in range(2)] for i in range(2)]
                Kn_tk = [[Tk("Kn") for m in range(2)] for i in range(2)]
                Vb = [sb(st, f"Vb{i}", [128, 32, 128], BF16) for i in range(2)]; Vb_tk = [Tk("Vb") for i in range(2)]
                Zb = [sb(st, f"Zb{i}", [128, LM], BF16) for i in range(2)]; Zb_tk = [Tk("Zb") for i in range(2)]
                Yb = [sb(st, f"Yb{i}", [128, LM], BF16) for i in range(2)]; Yb_tk = [Tk("Yb") for i in range(2)]
                cdg = sb(st, "cdg", [128, 4, 128], F32); cdg_tk = Tk("cdg")
                NPT = LA + 2
                PT = [sb(st, f"PTb{i}", [128, 512], BF16) for i in range(NPT)]; PT_tk = [Tk(f"PTb{i}") for i in range(NPT)]
                r0 = sb(st, "r0", [128, 512], F32); r0_tk = Tk("r0")
                r1 = sb(st, "r1", [128, 512], F32); r1_tk = Tk("r1")
                a0 = sb(st, "a0", [128, 512], F32); a0_tk = Tk("a0")
                a1 = sb(st, "a1", [128, 512], F32); a1_tk = Tk("a1")
                o_t = sb(st, "o_t", [128, 512], F32); o_tk = Tk("o_t")
                sq = sb(st, "sq", [128, 512], BF16); sq_tk = Tk("sq")
                rs_t = sb(st, "rs_t", [128, 512], F32); rs_tk = Tk("rs_t")
                t2 = sb(st, "t2", [128, 512], F32); t2_tk = Tk("t2")
                Pa = [[sb(st, f"Pa{i}{e}", [128, 512], F32) for e in range(2)] for i in range(2)]
                Pa_tk = [[Tk(f"Pa{i}{e}") for e in range(2)] for i in range(2)]
                dma(cdg[:], cdiag_in, writes=[cdg_tk])
                pcnt = 0
                qmcnt = 0
                set_ring([5, 6, 7])
                jobs = [(s0, L, hd) for (s0, L) in SEQS for hd in range(4)]

                def b_load(jn):
                    s0, L, hd = jobs[jn]
                    sb_ = jn % 2
                    for m in range(2):
                        bq = B_QB + 2 * hd + m
                        bk = B_KB + 2 * hd + m
                        dma(Qa[sb_][m][0:64, 0:L], FM[bq, 0:64, s0:s0 + L], writes=[Qa_tk[sb_][m]])
                        dma(Qa[sb_][m][64:69, 0:L], qaug_in[:, 0:L], writes=[Qa_tk[sb_][m]])
                        dma(Kp[sb_][m][0:64, 0:L], FM[bk, 0:64, s0:s0 + L], writes=[Kp_tk[sb_][m]])
                        dma(Kp[sb_][m][64:69, 0:L], kaug_in[hd, 0, :, 0:L], writes=[Kp_tk[sb_][m]])
                        dma(Kn[sb_][m][0:64, 0:L], FM[bk, 0:64, s0:s0 + L], writes=[Kn_tk[sb_][m]])
                        dma(Kn[sb_][m][64:69, 0:L], kaug_in[hd, 1, :, 0:L], writes=[Kn_tk[sb_][m]])
                    dma(Vb[sb_][:, 0:L // 128, :], VT[1][s0:s0 + L, hd * 128:(hd + 1) * 128].rearrange("(c p) e -> p c e", p=128), writes=[Vb_tk[sb_]])
                    dma(Zb[sb_][:, 0:L], FM[B_ZB + hd, :, s0:s0 + L], writes=[Zb_tk[sb_]])

                b_load(0)
                for jn, (s0, L, hd) in enumerate(jobs):
                    sb_ = jn % 2
                    if jn + 1 < len(jobs):
                        b_load(jn + 1)
                    nkc = L // 128
                    nqt = L // 512
                    slope = 2.0 ** (-8.0 * (hd + 1) / 4)
                    for qt in range(nqt):
                        q0 = qt * 512
                        for m in range(2):
                            U, U_tk = pbanks[2 * m], pb_tk[2 * m]
                            Z, Z_tk = pbanks[2 * m + 1], pb_tk[2 * m + 1]
                            kcs = []
                            for kc in range(nkc):
                                k0 = kc * 128
                                if k0 + 128 <= q0:
                                    mind = q0 - (k0 + 127)
                                elif k0 >= q0 + 512:
                                    mind = k0 - (q0 + 511)
                                else:
                                    mind = 0
                                if slope * mind > 60.0:
                                    continue
                                kcs.append(kc)
                            nk_ = len(kcs)
                            assert nk_ > LA + 2
                            pset = qmcnt % 2
                            qmcnt += 1
                            for ji, kc in enumerate(kcs):
                                pb = pcnt % NPT
                                pcnt += 1

                                def front(kc=kc, pb=pb, q0=q0, m=m, hd=hd, sb_=sb_, ji=ji, pset=pset):
                                    k0 = kc * 128
                                    ps, ps_tk = next_ps()
                                    qa_, kp_, kn_ = Qa[sb_][m], Kp[sb_][m], Kn[sb_][m]
                                    qa_tk, kp_tk, kn_tk = Qa_tk[sb_][m], Kp_tk[sb_][m], Kn_tk[sb_][m]
                                    if k0 + 128 <= q0:
                                        op("pe", lambda h: h.matmul(ps[:], lhsT=kp_[:, k0:k0 + 128], rhs=qa_[:, q0:q0 + 512], start=True, stop=True),
                                           [kp_tk, qa_tk], [ps_tk])
                                    elif k0 >= q0 + 512:
                                        op("pe", lambda h: h.matmul(ps[:], lhsT=kn_[:, k0:k0 + 128], rhs=qa_[:, q0:q0 + 512], start=True, stop=True),
                                           [kn_tk, qa_tk], [ps_tk])
                                    else:
                                        jd = (k0 - q0) // 128
                                        if jd > 0:
                                            op("pe", lambda h: h.matmul(ps[:, 0:128 * jd], lhsT=kn_[:, k0:k0 + 128], rhs=qa_[:, q0:q0 + 128 * jd],
                                                                        start=True, stop=True), [kn_tk, qa_tk], [ps_tk])
                                        op("pe", lambda h: h.matmul(ps[:, 128 * jd:512], lhsT=kp_[:, k0:k0 + 128], rhs=qa_[:, q0 + 128 * jd:q0 + 512],
                                                                    start=True, stop=True), [kp_tk, qa_tk], [ps_tk])
                                        op("dve", lambda h: h.tensor_tensor(out=ps[:, 128 * jd:128 * (jd + 1)], in0=ps[:, 128 * jd:128 * (jd + 1)],
                                                                            in1=cdg[:, hd, :], op=ALU.add), [ps_tk, cdg_tk], [ps_tk])
                                    op("act", lambda h: h.activation(out=PT[pb][:], in_=ps[:], func=AF.Exp, scale=0.125), [ps_tk], [PT_tk[pb]])

                                def back(kc=kc, pb=pb, ji=ji, nk_=nk_, U=U, U_tk=U_tk, Z=Z, Z_tk=Z_tk, m=m, q0=q0, sb_=sb_, pset=pset):
                                    op("pe", lambda h: h.matmul(U[:], lhsT=Vb[sb_][:, kc, :], rhs=PT[pb][:], start=(ji == 0), stop=(ji == nk_ - 1)),
                                       [Vb_tk[sb_], PT_tk[pb]], [U_tk], signal=(ji == nk_ - 1))
                                    op("pe", lambda h: h.matmul(Z[:], lhsT=ones_b[:], rhs=PT[pb][:], start=(ji == 0), stop=(ji == nk_ - 1)),
                                       [ones_tk, PT_tk[pb]], [Z_tk])
                                    if ji == nk_ - 1 and m == 0:
                                        op("act", lambda h: h.activation(out=r0[:], in_=Z[:], func=AF.Ln), [Z_tk], [r0_tk])
                                        op("act", lambda h: h.activation(out=r0[:], in_=r0[:], func=AF.Exp, scale=-1.0), [r0_tk], [r0_tk])
                                        op("dve", lambda h: h.tensor_tensor(out=a0[:], in0=U[:], in1=r0[:], op=ALU.mult), [U_tk, r0_tk], [a0_tk])
                                    if ji == nk_ - 1 and m == 1:
                                        op("act", lambda h: h.activation(out=r1[:], in_=Z[:], func=AF.Ln), [Z_tk], [r1_tk])
                                        op("act", lambda h: h.activation(out=r1[:], in_=r1[:], func=AF.Exp, scale=-1.0), [r1_tk], [r1_tk])
                                        op("dve", lambda h: h.tensor_tensor(out=a1[:], in0=U[:], in1=r1[:], op=ALU.mult), [U_tk, r1_tk], [a1_tk])
                                        op("dve", lambda h: h.scalar_tensor_tensor(out=o_t[:], in0=a1[:], scalar=nlam_t[:, l:l + 1], in1=a0[:],
                                                                                   op0=ALU.mult, op1=ALU.add), [a1_tk, a0_tk, nlam_tk], [o_tk])
                                        op("pool", lambda h: h.tensor_tensor(out=sq[:], in0=o_t[:], in1=o_t[:], op=ALU.mult), [o_tk], [sq_tk])

                                        def post2(q0=q0, sb_=sb_):
                                            ssp, ssp_tk = pbanks[4], pb_tk[4]
                                            op("pe", lambda h: h.matmul(ssp[:], lhsT=ones_b[:], rhs=sq[:], start=True, stop=True), [ones_tk, sq_tk], [ssp_tk])
                                            op("act", lambda h: h.activation(out=rs_t[:], in_=ssp[:], func=AF.Ln, bias=eps_t[:], scale=1.0 / 128), [ssp_tk, eps_tk], [rs_tk])
                                            op("act", lambda h: h.activation(out=rs_t[:], in_=rs_t[:], func=AF.Exp, scale=-0.5), [rs_tk], [rs_tk])
                                            op("pool", lambda h: h.tensor_tensor(out=t2[:], in0=o_t[:], in1=rs_t[:], op=ALU.mult), [o_tk, rs_tk], [t2_tk])
                                            op("pool", lambda h: h.tensor_tensor(out=t2[:], in0=t2[:], in1=Zb[sb_][:, q0:q0 + 512], op=ALU.mult), [t2_tk, Zb_tk[sb_]], [t2_tk])
                                            op("pool", lambda h: h.tensor_scalar(out=Yb[sb_][:, q0:q0 + 512], in0=t2[:], scalar1=gdl_t[:, l:l + 1], scalar2=None,
                                                                                 op0=ALU.mult), [t2_tk, gdl_tk], [Yb_tk[sb_]])
                                        pipe.defer(post2, 3)
                                pipe.push(front, back)
                    pipe.flush()
                    dma(YG[4 + hd, :, s0:s0 + L], Yb[sb_][:, 0:L], reads=[Yb_tk[sb_]])
                set_ring(range(7))
                del pbanks[7:], pb_tk[7:]
                S.barrier()

            with contextlib.ExitStack() as st:
                _maybe_skip(st, "C" not in stages)
                pb7 = st.enter_context(nc.psum_tensor(f"pb7_{l}c", [128, 512], F32))
                pbanks[7:] = [pb7]
                pb_tk[7:] = [Tk("pb7")]
                LA = 3
                pipe = Pipe(LA)
                LM = 4096
                PAD = 1024
                QD = [sb(st, f"QD{i}", [128, LM], BF16) for i in range(2)]; QD_tk = [Tk(f"QD{i}") for i in range(2)]
                KD = [sb(st, f"KD{i}", [128, LM + 2 * PAD], BF16) for i in range(2)]; KD_tk = [Tk(f"KD{i}") for i in range(2)]
                ZD = [sb(st, f"ZD{i}", [128, LM], BF16) for i in range(2)]; ZD_tk = [Tk(f"ZD{i}") for i in range(2)]
                YD = [sb(st, f"YD{i}", [128, LM], BF16) for i in range(2)]; YD_tk = [Tk(f"YD{i}") for i in range(2)]
                VD = [sb(st, f"VD{i}", [128, 48, 128], BF16) for i in range(2)]; VD_tk = [Tk(f"VD{i}") for i in range(2)]
                AT = [sb(st, f"AT{i}", [128, 2, LM], F32) for i in range(2)]; AT_tk = [Tk(f"AT{i}") for i in range(2)]
                dlb = sb(st, "dlb", [128, 12, 256], F32); dlb_tk = Tk("dlb")
                NTMP = 4
                tmp = [sb(st, f"tmpc{i}", [128, 256], F32) for i in range(NTMP)]; tmp_tk = [Tk(f"tmpc{i}") for i in range(NTMP)]
                NPT = LA + 2
                PT = [sb(st, f"PTc{i}", [128, 256], BF16) for i in range(NPT)]; PT_tk = [Tk(f"PTc{i}") for i in range(NPT)]
                NRC = 3
                rzc = [sb(st, f"rzc{i}", [128, 512], F32) for i in range(NRC)]; rzc_tk = [Tk(f"rzc{i}") for i in range(NRC)]
                dma(dlb[:], dilb_in, writes=[dlb_tk])
                for i in range(2):
                    op("pool", lambda h: h.memset(KD[i][:], 0.0), writes=[KD_tk[i]])
                acc_tks = [pb_tk[3 + i] for i in range(4)]
                set_ring([0, 1, 2, 7])
                pcnt = 0
                rcnt = 0
                cjobs = [(s0, L, p, g) for (s0, L) in SEQS for p in range(2) for g in range(3)]
                hjobs = [(jn, hh) for jn in range(len(cjobs)) for hh in range(2)]

                def c_load(jn):
                    s0, L, p, g = cjobs[jn]
                    lb = jn % 2
                    blk = 2 * g + p
                    dma(QD[lb][:, 0:L], FM[B_QC + blk, :, s0:s0 + L], writes=[QD_tk[lb]])
                    dma(KD[lb][:, PAD:PAD + L], FM[B_KC + blk, :, s0:s0 + L], writes=[KD_tk[lb]])
                    if L < LM:
                        op("pool", lambda h: h.memset(KD[lb][:, PAD + L:PAD + L + PAD], 0.0), writes=[KD_tk[lb]])
                    if g == 0:
                        zi = (jn // 3) % 2
                        dma(ZD[zi][:, 0:L], FM[B_ZC + p, :, s0:s0 + L], writes=[ZD_tk[zi]])

                def v_prep(hn):
                    jn, hh = hjobs[hn]
                    s0, L, p, g = cjobs[jn]
                    d = DIL[g][1]
                    M = L // d
                    nch = M // 128 + 1
                    hc = 4 * g + 2 * p + hh
                    vc0 = 0 if hh == 0 else 64
                    oc0 = 64 - vc0
                    vt, vt_tk = VD[hn % 2], VD_tk[hn % 2]
                    vt4 = vt[:, 0:d * nch, :].rearrange("p (r j) e -> p r j e", r=d)
                    op("pool", lambda h: h.memset(vt[:, 0:d * nch, :], 0.0), writes=[vt_tk])
                    if nch > 2:
                        op("pool", lambda h: h.memset(vt4[:, :, 1:nch - 1, oc0:oc0 + 64], 1.0), writes=[vt_tk])
                    op("pool", lambda h: h.memset(vt4[64:128, :, 0, oc0:oc0 + 64], 1.0), writes=[vt_tk])
                    op("pool", lambda h: h.memset(vt4[0:64, :, nch - 1, oc0:oc0 + 64], 1.0), writes=[vt_tk])
                    vsrc = VT[2][s0:s0 + L, hc * 64:(hc + 1) * 64]
                    for r in range(d):
                        if nch > 2:
                            src = vsrc[64 * d:64 * d + 128 * (nch - 2) * d, :].rearrange("(j k r) e -> r k j e", k=128, r=d)[r]
                            dma(vt4[:, r, 1:nch - 1, vc0:vc0 + 64], src, writes=[vt_tk])
                        src = vsrc[0:64 * d, :].rearrange("(k r) e -> r k e", r=d)[r]
                        dma(vt4[64:128, r, 0, vc0:vc0 + 64], src, writes=[vt_tk])
                        src = vsrc[(M - 64) * d:M * d, :].rearrange("(k r) e -> r k e", r=d)[r]
                        dma(vt4[0:64, r, nch - 1, vc0:vc0 + 64], src, writes=[vt_tk])

                c_load(0)
                v_prep(0)
                for hn, (jn, hh) in enumerate(hjobs):
                    s0, L, p, g = cjobs[jn]
                    d = DIL[g][1]
                    M = L // d
                    nsub = M // 128
                    nch = nsub + 1
                    lb = jn % 2
                    ai = (jn // 3) % 2
                    if hh == 0 and jn + 1 < len(cjobs):
                        c_load(jn + 1)
                    hc = 4 * g + 2 * p + hh
                    ub = 64 * hh
                    vt, vt_tk = VD[hn % 2], VD_tk[hn % 2]
                    npush = 0
                    for r in range(d):
                        for j in range(nch):
                            if npush == LA + 1 and hn + 1 < len(hjobs):
                                v_prep(hn + 1)
                            npush += 1
                            tb = pcnt % NTMP
                            pb = pcnt % NPT
                            pcnt += 1

                            def front(r=r, j=j, d=d, nsub=nsub, lb=lb, ub=ub, hc=hc, tb=tb, pb=pb):
                                kstart = PAD + (128 * j - 64) * d + r
                                qlo = max(j - 1, 0)
                                c_lo = 0 if j - 1 >= 0 else 128
                                c_hi = 256 if j <= nsub - 1 else 128
                                ps, ps_tk = next_ps()
                                qa0 = (128 * qlo) * d + r
                                nq = c_hi - c_lo
                                op("pe", lambda h: h.matmul(ps[:, c_lo:c_hi], lhsT=KD[lb][ub:ub + 64, kstart:kstart + 127 * d + 1:d],
                                                            rhs=QD[lb][ub:ub + 64, qa0:qa0 + (nq - 1) * d + 1:d], start=True, stop=True),
                                   [KD_tk[lb], QD_tk[lb]], [ps_tk])
                                op("dve", lambda h: h.scalar_tensor_tensor(out=tmp[tb][:, c_lo:c_hi], in0=ps[:, c_lo:c_hi], scalar=0.125,
                                                                           in1=dlb[:, hc, c_lo:c_hi], op0=ALU.mult, op1=ALU.add),
                                   [ps_tk, dlb_tk], [tmp_tk[tb]])
                                op("act", lambda h: h.activation(out=PT[pb][:, c_lo:c_hi], in_=tmp[tb][:, c_lo:c_hi], func=AF.Exp),
                                   [tmp_tk[tb]], [PT_tk[pb]])

                            def back(r=r, j=j, d=d, nsub=nsub, nch=nch, pb=pb, vt=vt, vt_tk=vt_tk, hh=hh, g=g, ai=ai):
                                if j - 1 >= 0:
                                    sl = (j - 1) % 4
                                    op("pe", lambda h: h.matmul(pbanks[3 + sl][:, 0:128], lhsT=vt[:, r * nch + j, :], rhs=PT[pb][:, 0:128],
                                                                start=False, stop=True), [vt_tk, PT_tk[pb]], [acc_tks[sl]])
                                    a0_ = (128 * (j - 1)) * d + r
                                    dst = AT[ai][:, hh, a0_:a0_ + 127 * d + 1:d]
                                    if g == 0:
                                        op("dve", lambda h: h.tensor_copy(out=dst, in_=pbanks[3 + sl][:, 0:128]), [acc_tks[sl]], [AT_tk[ai]])
                                    else:
                                        op("dve", lambda h: h.tensor_tensor(out=dst, in0=pbanks[3 + sl][:, 0:128], in1=dst, op=ALU.add),
                                           [acc_tks[sl], AT_tk[ai]], [AT_tk[ai]])
                                if j <= nsub - 1:
                                    sl = j % 4
                                    op("pe", lambda h: h.matmul(pbanks[3 + sl][:, 0:128], lhsT=vt[:, r * nch + j, :], rhs=PT[pb][:, 128:256],
                                                                start=True, stop=False), [vt_tk, PT_tk[pb]], [acc_tks[sl]])
                            pipe.push(front, back)
                    if g == 2 and hh == 1:
                        pipe.flush()
                        last_job = (hn == len(hjobs) - 1)
                        kk = 0
                        for hh2 in range(2):
                            for c0 in range(0, L, 512):
                                ri = rcnt % NRC
                                rcnt += 1

                                def piece(hh2=hh2, c0=c0, ri=ri, ai=ai):
                                    ub2 = 64 * hh2
                                    zb2 = 64 - ub2
                                    op("act", lambda h: h.activation(out=rzc[ri][ub2:ub2 + 64, :], in_=AT[ai][zb2:zb2 + 64, hh2, c0:c0 + 512], func=AF.Ln),
                                       [AT_tk[ai]], [rzc_tk[ri]])
                                    op("act", lambda h: h.activation(out=rzc[ri][ub2:ub2 + 64, :], in_=rzc[ri][ub2:ub2 + 64, :], func=AF.Exp, scale=-1.0),
                                       [rzc_tk[ri]], [rzc_tk[ri]])
                                    op("pool", lambda h: h.tensor_tensor(out=rzc[ri][ub2:ub2 + 64, :], in0=rzc[ri][ub2:ub2 + 64, :], in1=ZD[ai][ub2:ub2 + 64, c0:c0 + 512],
                                                                         op=ALU.mult), [rzc_tk[ri], ZD_tk[ai]], [rzc_tk[ri]])
                                    op("pool", lambda h: h.tensor_tensor(out=YD[ai][ub2:ub2 + 64, c0:c0 + 512], in0=AT[ai][ub2:ub2 + 64, hh2, c0:c0 + 512],
                                                                         in1=rzc[ri][ub2:ub2 + 64, :], op=ALU.mult), [AT_tk[ai], rzc_tk[ri]], [YD_tk[ai]])
                                if last_job:
                                    piece()
                                else:
                                    pipe.defer(piece, 2 + 2 * kk)
                                kk += 1

                        def store(p=p, s0=s0, L=L, ai=ai):
                            dma(YG[8 + p, :, s0:s0 + L], YD[ai][:, 0:L], reads=[YD_tk[ai]])
                        if last_job:
                            store()
                        else:
                            pipe.defer(store, 2 + 2 * kk + 6)
                pipe.flush()
                set_ring(range(7))
                del pbanks[7:], pb_tk[7:]
                S.barrier()

            with contextlib.ExitStack() as st:
                _maybe_skip(st, "6" not in stages)
                WG = sb(st, "WG", [128, 8, 3072], BF16); WG_tk = Tk("WG")
                WBR = sb(st, "WBR", [128, 10, D], BF16); WBR_tk = Tk("WBR")
                WO = sb(st, "WO", [128, 8, D], BF16); WO_tk = Tk("WO")
                with contextlib.ExitStack() as st2:
                    stgs = make_stg(st2)
                    load_cast(stgs, WG, WG_tk, w_in[l, :, GL:INW].rearrange("(c p) n -> p c n", p=128), 8, 3072, l)
                    load_cast(stgs, WBR[:, 0:4], WBR_tk, w_bra[l].rearrange("(c p) n -> p c n", p=128), 4, D, None)
                    load_cast(stgs, WBR[:, 4:8], WBR_tk, w_brb[l].rearrange("(c p) n -> p c n", p=128), 4, D, None)
                    load_cast(stgs, WBR[:, 8:10], WBR_tk, w_brc[l].rearrange("(c p) n -> p c n", p=128), 2, D, None)
                    load_cast(stgs, WO, WO_tk, w_out[l].rearrange("(c p) n -> p c n", p=128), 8, D, None)
                    S.barrier()
                hT = [sb(st, f"hT6{i}", [128, 8, 512], BF16) for i in range(2)]; hT_tk = [Tk(f"hT6{i}") for i in range(2)]
                yg = [sb(st, f"yg6{i}", [128, 10, 512], BF16) for i in range(2)]; yg_tk = [Tk(f"yg6{i}") for i in range(2)]
                xt = [sb(st, f"x6{i}", [128, D], F32) for i in range(8)]; xt_tk = [Tk(f"x6{i}") for i in range(8)]
                xo = [sb(st, f"xo6{i}", [128, D], F32) for i in range(2)]; xo_tk = [Tk(f"xo6{i}") for i in range(2)]
                gt = [sb(st, f"gt6{i}", [128, 512], F32) for i in range(3)]; gt_tk = [Tk(f"gt6{i}") for i in range(3)]
                mt = [sb(st, f"mt6{i}", [128, 512], F32) for i in range(3)]; mt_tk = [Tk(f"mt6{i}") for i in range(3)]
                mg = [sb(st, f"mg6{i}", [128, 8, 512], BF16) for i in range(2)]; mg_tk = [Tk(f"mg6{i}") for i in range(2)]
                junk = sb(st, "junk6", [128, D], BF16); junk_tk = Tk("junk6")
                ss = [sb(st, f"ss6{i}", [128, 1], F32) for i in range(2)]; ss_tk = [Tk(f"ss6{i}") for i in range(2)]
                rstd = [sb(st, f"rstd6{i}", [128, 1], F32) for i in range(2)]; rstd_tk = [Tk(f"rstd6{i}") for i in range(2)]
                NT = NTOK // 512
                KBR = [(0, 4), (4, 4), (8, 2)]
                xcnt = 0
                ocnt = 0

                def s6_load(T):
                    b = T % 2
                    dma(hT[b][:], HT[:, :, T * 512:(T + 1) * 512].rearrange("c p t -> p c t"), writes=[hT_tk[b]])
                    dma(yg[b][:], YG[:, :, T * 512:(T + 1) * 512].rearrange("c p t -> p c t"), writes=[yg_tk[b]])

                s6_load(0)
                for T in range(NT):
                    b = T % 2
                    if T + 1 < NT:
                        s6_load(T + 1)
                    for s_ in range(4):
                        t0 = T * 512 + s_ * 128
                        dma(xt[(4 * T + s_) % 8][:], x_src[t0:t0 + 128, :], writes=[xt_tk[(4 * T + s_) % 8]])
                    for c in range(8):
                        for br in range(3):
                            gps, gps_tk = next_ps()
                            for k in range(8):
                                op("pe", lambda h: h.matmul(gps[:], lhsT=WG[:, k, br * D + c * 128:br * D + (c + 1) * 128], rhs=hT[b][:, k, :],
                                                            start=(k == 0), stop=(k == 7)), [WG_tk, hT_tk[b]], [gps_tk], signal=(k == 7))
                            op("act", lambda h: h.activation(out=gt[br][:], in_=gps[:], func=AF.Sigmoid, bias=bg_t[:, l, br * 8 + c:br * 8 + c + 1]),
                               [gps_tk, bg_tk], [gt_tk[br]])
                            pps, pps_tk = next_ps()
                            k0, nk = KBR[br]
                            for k in range(nk):
                                op("pe", lambda h: h.matmul(pps[:], lhsT=WBR[:, k0 + k, c * 128:(c + 1) * 128], rhs=yg[b][:, k0 + k, :],
                                                            start=(k == 0), stop=(k == nk - 1)), [WBR_tk, yg_tk[b]], [pps_tk], signal=(k == nk - 1))
                            op("dve", lambda h: h.tensor_tensor(out=mt[br][:], in0=pps[:], in1=gt[br][:], op=ALU.mult), [pps_tk, gt_tk[br]], [mt_tk[br]])
                        op("pool", lambda h: h.tensor_tensor(out=mt[0][:], in0=mt[0][:], in1=mt[1][:], op=ALU.add), [mt_tk[0], mt_tk[1]], [mt_tk[0]])
                        op("pool", lambda h: h.tensor_tensor(out=mg[b][:, c, :], in0=mt[0][:], in1=mt[2][:], op=ALU.add), [mt_tk[0], mt_tk[2]], [mg_tk[b]])
                    for s_ in range(4):
                        t0 = T * 512 + s_ * 128
                        xb = (4 * T + s_) % 8
                        ob = ocnt % 2
                        ocnt += 1
                        for half in range(2):
                            ops_, ops_tk = next_ps()
                            for k in range(8):
                                op("pe", lambda h: h.matmul(ops_[:], lhsT=mg[b][:, k, s_ * 128:(s_ + 1) * 128], rhs=WO[:, k, half * 512:(half + 1) * 512],
                                                            start=(k == 0), stop=(k == 7)), [mg_tk[b], WO_tk], [ops_tk], signal=(k == 7))
                            op("dve", lambda h: h.tensor_tensor(out=xo[ob][:, half * 512:(half + 1) * 512], in0=ops_[:], in1=xt[xb][:, half * 512:(half + 1) * 512],
                                                                op=ALU.add), [ops_tk, xt_tk[xb]], [xo_tk[ob]])
                        if l < depth - 1:
                            dma(XS[t0:t0 + 128, :], xo[ob][:], reads=[xo_tk[ob]])
                        else:
                            b2 = ocnt % 2
                            rms_rstd(st, xo[ob][:], xo_tk[ob], junk[:], junk_tk, ss[b2][:], ss_tk[b2], rstd[b2][:], rstd_tk[b2])
                            op("dve", lambda h: h.scalar_tensor_tensor(out=xo[ob][:], in0=xo[ob][:], scalar=rstd[b2][:], in1=gf_t[:],
                                                                        op0=ALU.mult, op1=ALU.mult), [xo_tk[ob], rstd_tk[b2], gf_tk], [xo_tk[ob]])
                            dma(y_out[t0:t0 + 128, :], xo[ob][:], reads=[xo_tk[ob]])
                S.barrier()
        S.barrier()
        import sys as _sys
        print("[build] nsem", S.nsem, {n: e.count for n, e in S.E.items()}, file=_sys.stderr)
    return nc


def _consts():
    bf = ml_dtypes.bfloat16
    ident = np.eye(128, dtype=np.float32).astype(bf)
    pos = np.arange(4096)
    qaug = np.zeros((5, 4096), np.float32)
    qaug[0] = (pos % 512) // 16
    qaug[1] = pos % 16
    qaug[2] = (pos // 512) * 512
    qaug[3] = 1.0
    qaug[4] = 1.0
    kaug = np.zeros((4, 2, 5, 4096), np.float32)
    for h in range(4):
        s = 2.0 ** (-8.0 * (h + 1) / 4)
        for si, sg in enumerate((1.0, -1.0)):
            kaug[h, si, 0] = -16.0 * s * 8.0 * sg
            kaug[h, si, 1] = -s * 8.0 * sg
            kaug[h, si, 2] = -s * 8.0 * sg
            kaug[h, si, 3] = sg * s * 8.0 * (pos % 128)
            kaug[h, si, 4] = sg * s * 8.0 * ((pos // 128) * 128)
    cdiag = np.zeros((128, 4, 128), np.float32)
    kk = np.arange(128)[:, None]
    qq = np.arange(128)[None, :]
    for h in range(4):
        s = 2.0 ** (-8.0 * (h + 1) / 4)
        cdiag[:, h, :] = np.where(qq < kk, 16.0 * s * (qq - kk), 0.0)
    dilb = np.zeros((128, 12, 256), np.float32)
    kk = np.arange(128)[:, None]
    qq = np.arange(256)[None, :]
    rel = qq - kk - 64
    for hc in range(12):
        s = np.float32(2.0 ** (-8.0 * (hc + 1) / 12))
        d = DIL[hc // 4][1]
        dilb[:, hc, :] = np.where(np.abs(rel) <= 64, -(s * np.float32(d) * np.abs(rel).astype(np.float32)), NEG)
    return dict(ident=ident, qaug=qaug.astype(bf), kaug=kaug.astype(bf), cdiag=cdiag, dilb=dilb)


def _trev(rpb):
    kc = np.arange(64)[:, None]
    qc = np.arange(64)[None, :]
    qstart = np.clip(qc - 8, 0, 48)
    ok = (kc >= qstart) & (kc < qstart + 16)
    cidx = np.clip(kc - qc + 15, 0, 30)
    t = rpb[:, :, :, cidx]
    t = np.where(ok[None, None, None], t, np.float32(NEG))
    t = t[:, :, ::-1]
    return np.ascontiguousarray(np.transpose(t, (0, 1, 3, 2, 4))).astype(np.float32)


_NC_CACHE = {}


def kernel(x_prompt, x_sample, g_norm, w_in, b_gate, rpb, lam_qk, g_diff, w_br_a, w_br_b, w_br_c, w_out, g_final, _depth=DEPTH, _debug=False, _stages="2ABC6", _ncores=8, _trace=False):
    f32 = lambda a: np.ascontiguousarray(np.asarray(a, dtype=np.float32))
    x_prompt, x_sample = f32(x_prompt), f32(x_sample)
    shared = dict(
        w_in=f32(w_in), w_br_a=f32(w_br_a), w_br_b=f32(w_br_b), w_br_c=f32(w_br_c), w_out=f32(w_out),
        gk=f32(np.transpose(np.asarray(g_norm).reshape(DEPTH, 8, 128), (0, 2, 1))),
        bg=f32(np.transpose(np.asarray(b_gate).reshape(DEPTH, 24, 128), (0, 2, 1))),
        gd=f32(np.asarray(g_diff).reshape(DEPTH, 128).T),
        lq=f32(np.asarray(lam_qk).reshape(DEPTH, 256)),
        gf=f32(g_final),
        trev=_trev(f32(rpb)),
    )
    shared.update(_consts())
    key = (_depth, _debug, _stages)
    if key not in _NC_CACHE:
        _NC_CACHE[key] = build_nc(_depth, _debug, _stages)
    nc = _NC_CACHE[key]
    in_maps = []
    for c in range(_ncores):
        xc = np.concatenate([x_prompt[2 * c], x_prompt[2 * c + 1], x_sample[c]], axis=0)
        m = dict(shared)
        m["x"] = np.ascontiguousarray(xc)
        in_maps.append(m)
    res = run_bass_kernel_spmd(nc, in_maps, core_ids=list(range(_ncores)), **({'trace': True} if _trace else {}))
    if _debug or _trace:
        return res
    y_prompt = np.empty((16, 2048, D), np.float32)
    y_sample = np.empty((8, 4096, D), np.float32)
    for c in range(8):
        y = res.results[c]["y"]
        y_prompt[2 * c] = y[0:2048]
        y_prompt[2 * c + 1] = y[2048:4096]
        y_sample[c] = y[4096:8192]
    return (y_prompt, y_sample)
```

```python
import contextlib
import math
import numpy as np
import ml_dtypes
import concourse.bass as bass
import concourse.mybir as mybir
from concourse.bass_utils import run_bass_kernel_spmd

F32 = mybir.dt.float32
BF16 = mybir.dt.bfloat16
AF = mybir.ActivationFunctionType
ALU = mybir.AluOpType

D = 1024
DEPTH = 4
NTOK = 8192
SEQS = [(0, 2048), (2048, 2048), (4096, 4096)]
INW = 9728
QA, KA, VA, ZA, QB, KB, VB, ZB, QC, KC, VC, ZC, GL = 0, 512, 1024, 1536, 2048, 2560, 3072, 3584, 4096, 4864, 5632, 6400, 6656
NEG = -30000.0
EPS = 1e-6
DIL = ((128, 1), (512, 4), (2048, 16))
EPOCH = 30000

FM_BLOCKS = []
for i in range(4): FM_BLOCKS.append((QA + 128 * i, 128, False))
for i in range(4): FM_BLOCKS.append((KA + 128 * i, 128, False))
for i in range(4): FM_BLOCKS.append((ZA + 128 * i, 128, True))
for i in range(8): FM_BLOCKS.append((QB + 64 * i, 64, False))
for i in range(8): FM_BLOCKS.append((KB + 64 * i, 64, False))
for i in range(4): FM_BLOCKS.append((ZB + 128 * i, 128, True))
for i in range(6): FM_BLOCKS.append((QC + 128 * i, 128, False))
for i in range(6): FM_BLOCKS.append((KC + 128 * i, 128, False))
for i in range(2): FM_BLOCKS.append((ZC + 128 * i, 128, True))
B_QA, B_KA, B_ZA, B_QB, B_KB, B_ZB, B_QC, B_KC, B_ZC = 0, 4, 8, 12, 20, 28, 32, 38, 44
NB = len(FM_BLOCKS)
TM_GROUPS = [(VA, 512, 0, 0), (VB, 512, 1, 0), (VC, 512, 2, 0), (VC + 512, 256, 2, 512)]


class _Skip(Exception):
    pass


def _maybe_skip(st, cond):
    if cond:
        st.push(lambda et, ev, tb: et is _Skip)
        raise _Skip()


class Tk:
    __slots__ = ("name", "w", "r", "dsem")

    def __init__(self, name):
        self.name = name
        self.w = None
        self.r = {}
        self.dsem = None


class Eng:
    def __init__(self, name, h):
        self.name = name
        self.h = h
        self.count = 0
        self.sems = []
        self.waited = {}


class Sched:
    def __init__(self, nc, es):
        self.nc = nc
        self.es = es
        self.E = {n: Eng(n, h) for n, h in (("pe", nc.tensor), ("act", nc.scalar), ("dve", nc.vector),
                                            ("pool", nc.gpsimd), ("sp", nc.sync))}
        self.nsem = 0
        self.dfree = []
        self.dcount = {}
        self.live_dsem = []
        self.rr = 0

    def new_sem(self, name):
        self.nsem += 1
        assert self.nsem < 140, "too many semaphores"
        return self.es.enter_context(self.nc.semaphore(f"{name}{self.nsem}"))

    def eng_sem(self, ename, idx):
        e = self.E[ename]
        k = (idx - 1) // EPOCH
        while len(e.sems) <= k:
            e.sems.append(self.new_sem("e" + ename))
        return e.sems[k], (idx - 1) % EPOCH + 1

    def wait_tok(self, e, tok):
        if tok is None:
            return
        if tok[0] == "e":
            if tok[1] == "pe" and e.name == "pe":
                return
            sem, val = self.eng_sem(tok[1], tok[2])
        else:
            sem, val = tok[1], tok[2]
        key = id(sem)
        if e.waited.get(key, 0) >= val:
            return
        e.h.wait_ge(sem, val)
        e.waited[key] = val

    def _deps(self, e, reads, writes):
        for t in reads:
            self.wait_tok(e, t.w)
        for t in writes:
            self.wait_tok(e, t.w)
            for tok in t.r.values():
                self.wait_tok(e, tok)

    def _mark(self, tok, reads, writes, rkey):
        for t in reads:
            t.r[rkey] = tok
        for t in writes:
            t.w = tok
            t.r = {}

    def op(self, ename, fn, reads=(), writes=()):
        e = self.E[ename]
        self._deps(e, reads, writes)
        inst = fn(e.h)
        e.count += 1
        sem, val = self.eng_sem(ename, e.count)
        inst.then_inc(sem, 1)
        self._mark(("e", ename, e.count), reads, writes, ename)

    def dma(self, out, in_, reads=(), writes=(), q="sp"):
        e = self.E[q]
        for t in reads:
            self.wait_tok(e, t.w)
        for t in writes:
            if not (t.w is not None and t.w[0] == "d" and t.dsem is not None and t.w[1] is t.dsem[0]):
                self.wait_tok(e, t.w)
            for tok in t.r.values():
                self.wait_tok(e, tok)
        t = writes[0] if writes else reads[0]
        if t.dsem is None:
            if self.dfree:
                t.dsem = self.dfree.pop()
            else:
                t.dsem = [self.new_sem("d"), 0]
            self.live_dsem.append(t.dsem)
        ds = t.dsem
        inst = e.h.dma_start(out=out, in_=in_)
        ds[1] += 16
        inst.then_inc(ds[0], 16)
        tok = ("d", ds[0], ds[1])
        self._mark(tok, reads, writes, id(ds[0]))

    def barrier(self):
        sp = self.E["sp"]
        for n, e in self.E.items():
            if n != "sp" and e.count > 0:
                self.wait_tok(sp, ("e", n, e.count))
        for ds in self.live_dsem:
            self.wait_tok(sp, ("d", ds[0], ds[1]))
        inst = sp.h.nop()
        sp.count += 1
        sem, val = self.eng_sem("sp", sp.count)
        inst.then_inc(sem, 1)
        for n, e in self.E.items():
            if n != "sp":
                self.wait_tok(e, ("e", "sp", sp.count))
        for ds in self.live_dsem:
            if ds[1] < 20000:
                self.dfree.append(ds)
        self.live_dsem = []


class Pipe:
    def __init__(self, la):
        self.la = la
        self.q = []
        self.d = []

    def push(self, front, back):
        front()
        self.q.append(back)
        if len(self.q) > self.la:
            self.q.pop(0)()
        for e in self.d:
            e[0] -= 1
        while self.d and self.d[0][0] <= 0:
            self.d.pop(0)[1]()

    def defer(self, fn, delay):
        self.d.append([delay, fn])

    def flush(self):
        while self.q:
            self.q.pop(0)()
        while self.d:
            self.d.pop(0)[1]()


def build_nc(depth=DEPTH, debug=False, stages="2ABC6"):
    nc = bass.Bass("TRN2", target_bir_lowering=False)
    dt_in = lambda name, shape, dt=F32: nc.dram_tensor(name, list(shape), dt, kind="ExternalInput").ap()
    x_in = dt_in("x", [NTOK, D])
    w_in = dt_in("w_in", [DEPTH, D, INW])
    w_bra = dt_in("w_br_a", [DEPTH, 512, D])
    w_brb = dt_in("w_br_b", [DEPTH, 512, D])
    w_brc = dt_in("w_br_c", [DEPTH, 256, D])
    w_out = dt_in("w_out", [DEPTH, D, D])
    gk_in = dt_in("gk", [DEPTH, 128, 8])
    bg_in = dt_in("bg", [DEPTH, 128, 24])
    gd_in = dt_in("gd", [128, DEPTH])
    lq_in = dt_in("lq", [DEPTH, 256])
    gf_in = dt_in("gf", [D])
    trev_in = dt_in("trev", [DEPTH, 8, 64, 15, 64])
    ident_in = dt_in("ident", [128, 128], BF16)
    qaug_in = dt_in("qaug", [5, 4096], BF16)
    kaug_in = dt_in("kaug", [4, 2, 5, 4096], BF16)
    cdiag_in = dt_in("cdiag", [128, 4, 128])
    dilb_in = dt_in("dilb", [128, 12, 256])
    y_out = nc.dram_tensor("y", [NTOK, D], F32, kind="ExternalOutput").ap()
    skind = "ExternalOutput" if debug else "Internal"
    XS = nc.dram_tensor("xs", [NTOK, D], F32, kind=skind).ap()
    HT = nc.dram_tensor("ht", [8, 128, NTOK], BF16, kind=skind).ap()
    FM = nc.dram_tensor("fm", [NB, 128, NTOK], BF16, kind=skind).ap()
    VT = [nc.dram_tensor("vta", [NTOK, 512], BF16, kind=skind).ap(),
          nc.dram_tensor("vtb", [NTOK, 512], BF16, kind=skind).ap(),
          nc.dram_tensor("vtc", [NTOK, 768], BF16, kind=skind).ap()]
    YG = nc.dram_tensor("yg", [10, 128, NTOK], BF16, kind=skind).ap()

    with contextlib.ExitStack() as es:
        S = Sched(nc, es)
        op, dma = S.op, S.dma

        uid = {"n": 0}

        def sb(st, name, shape, dt):
            uid["n"] += 1
            return st.enter_context(nc.sbuf_tensor(f"s{uid['n']}_{name}", list(shape), dt))

        pbanks = [es.enter_context(nc.psum_tensor(f"pb{i}", [128, 512], F32)) for i in range(7)]
        ptr_tk = Tk("ptr")
        pb_tk = [Tk(f"pb{i}") for i in range(7)]
        pstate = {"i": 0, "ring": list(range(7))}

        def set_ring(r):
            pstate["ring"] = list(r)
            pstate["i"] = 0

        def next_ps():
            ring = pstate["ring"]
            i = ring[pstate["i"] % len(ring)]
            pstate["i"] += 1
            return pbanks[i], pb_tk[i]

        rr = {"i": 0}

        def evac_eng():
            rr["i"] ^= 1
            return "act" if rr["i"] else "dve"

        def copy_op(ename, out, in_, reads, writes):
            if ename == "act":
                op("act", lambda h: h.copy(out=out, in_=in_), reads, writes)
            else:
                op(ename, lambda h: h.tensor_copy(out=out, in_=in_), reads, writes)

        ident = sb(es, "ident", [128, 128], BF16); ident_tk = Tk("ident")
        ones_b = sb(es, "ones_b", [128, 128], BF16); ones_tk = Tk("ones")
        eps_t = sb(es, "eps_t", [128, 1], F32); eps_tk = Tk("eps")
        gk_t = sb(es, "gk_t", [128, DEPTH, 8], F32); gk_tk = Tk("gk")
        bg_t = sb(es, "bg_t", [128, DEPTH, 24], F32); bg_tk = Tk("bg")
        gd_t = sb(es, "gd_t", [128, DEPTH], F32); gd_tk = Tk("gd")
        gdl_t = sb(es, "gdl_t", [128, DEPTH], F32); gdl_tk = Tk("gdl")
        nlam_t = sb(es, "nlam_t", [128, DEPTH], F32); nlam_tk = Tk("nlam")
        lq_t = sb(es, "lq_t", [128, DEPTH, 256], F32); lq_tk = Tk("lq")
        lqp_t = sb(es, "lqp_t", [128, 256], F32); lqp_tk = Tk("lqp")
        lqs_t = sb(es, "lqs_t", [128, 4], F32); lqs_tk = Tk("lqs")
        gf_t = sb(es, "gf_t", [128, D], F32); gf_tk = Tk("gf")
        dma(ident[:], ident_in, writes=[ident_tk])
        dma(gk_t[:], gk_in.rearrange("l p c -> p l c"), writes=[gk_tk])
        dma(bg_t[:], bg_in.rearrange("l p c -> p l c"), writes=[bg_tk])
        dma(gd_t[:], gd_in, writes=[gd_tk])
        dma(gf_t[:], gf_in.partition_broadcast(128), writes=[gf_tk])
        for l in range(DEPTH):
            dma(lq_t[:, l, :], lq_in[l].partition_broadcast(128), writes=[lq_tk])
        op("pool", lambda h: h.memset(ones_b[:], 1.0), writes=[ones_tk])
        op("pool", lambda h: h.memset(eps_t[:], EPS), writes=[eps_tk])
        for l in range(depth):
            lam_init = 0.8 - 0.6 * math.exp(-0.3 * l)
            op("dve", lambda h: h.tensor_tensor(out=lqp_t[:, 0:64], in0=lq_t[:, l, 0:64], in1=lq_t[:, l, 64:128], op=ALU.mult),
               [lq_tk], [lqp_tk])
            op("dve", lambda h: h.tensor_tensor(out=lqp_t[:, 64:128], in0=lq_t[:, l, 128:192], in1=lq_t[:, l, 192:256], op=ALU.mult),
               [lq_tk], [lqp_tk])
            op("dve", lambda h: h.reduce_sum(out=lqs_t[:, 0:1], in_=lqp_t[:, 0:64], axis=mybir.AxisListType.X), [lqp_tk], [lqs_tk])
            op("dve", lambda h: h.reduce_sum(out=lqs_t[:, 1:2], in_=lqp_t[:, 64:128], axis=mybir.AxisListType.X), [lqp_tk], [lqs_tk])
            op("act", lambda h: h.activation(out=lqs_t[:, 2:4], in_=lqs_t[:, 0:2], func=AF.Exp), [lqs_tk], [lqs_tk])
            op("dve", lambda h: h.scalar_tensor_tensor(out=nlam_t[:, l:l + 1], in0=lqs_t[:, 3:4], scalar=-lam_init, in1=lqs_t[:, 2:3],
                                                       op0=ALU.add, op1=ALU.subtract), [lqs_tk], [nlam_tk])
            op("dve", lambda h: h.tensor_scalar(out=gdl_t[:, l:l + 1], in0=gd_t[:, l:l + 1], scalar1=(1.0 - lam_init), scalar2=None,
                                                op0=ALU.mult), [gd_tk], [gdl_tk])
        S.barrier()

        lc_state = {"k": 0}

        def make_stg(st_stack, piece=512):
            return ([sb(st_stack, f"wstg{i}", [128, 8, piece], F32) for i in range(4)], [Tk(f"wstg{i}") for i in range(4)])

        def load_cast(stgs, dst, dst_tk, src3, nk, ncols, gl, piece=512):
            stg, stg_tk = stgs
            k = lc_state["k"]
            for c0 in range(0, ncols, piece):
                w = min(piece, ncols - c0)
                b = k % 4
                dma(stg[b][:, 0:nk, 0:w], src3[:, :, c0:c0 + w], writes=[stg_tk[b]])
                for c in range(nk):
                    en = "act" if (k * nk + c) % 2 else "dve"
                    if gl is not None:
                        if en == "act":
                            op("act", lambda h: h.activation(out=dst[:, c, c0:c0 + w], in_=stg[b][:, c, 0:w], func=AF.Copy, scale=gk_t[:, gl, c:c + 1]),
                               [stg_tk[b], gk_tk], [])
                        else:
                            op("dve", lambda h: h.tensor_scalar(out=dst[:, c, c0:c0 + w], in0=stg[b][:, c, 0:w], scalar1=gk_t[:, gl, c:c + 1],
                                                                scalar2=None, op0=ALU.mult), [stg_tk[b], gk_tk], [])
                    else:
                        if en == "act":
                            op("act", lambda h: h.copy(out=dst[:, c, c0:c0 + w], in_=stg[b][:, c, 0:w]), [stg_tk[b]], [])
                        else:
                            op("dve", lambda h: h.tensor_copy(out=dst[:, c, c0:c0 + w], in_=stg[b][:, c, 0:w]), [stg_tk[b]], [])
                k += 1
            lc_state["k"] = k

        def rms_rstd(st, xap, x_tk, junk, junk_tk, ss, ss_tk, rstd, rstd_tk):
            op("act", lambda h: h.activation(out=junk, in_=xap, func=AF.Square, accum_out=ss), [x_tk], [junk_tk, ss_tk])
            op("act", lambda h: h.activation(out=ss, in_=ss, func=AF.Sqrt, bias=eps_t[:], scale=1.0 / D), [ss_tk, eps_tk], [ss_tk])
            op("dve", lambda h: h.reciprocal(out=rstd, in_=ss), [ss_tk], [rstd_tk])

        for l in range(depth):
            x_src = x_in if l == 0 else XS
            with contextlib.ExitStack() as st:
                _maybe_skip(st, "2" not in stages)
                ptr = st.enter_context(nc.psum_tensor(f"ptr{l}", [128, 1024], BF16))
                W2 = sb(st, "W2", [128, 8, GL], BF16); W2_tk = Tk("W2")
                with contextlib.ExitStack() as st2:
                    stgs = make_stg(st2)
                    load_cast(stgs, W2, W2_tk, w_in[l, :, 0:GL].rearrange("(c p) n -> p c n", p=128), 8, GL, l)
                    S.barrier()
                NX = 8
                xt = [sb(st, f"xt{i}", [128, D], F32) for i in range(NX)]; xt_tk = [Tk(f"xt{i}") for i in range(NX)]
                hb = [sb(st, f"hb{i}", [128, D], BF16) for i in range(2)]; hb_tk = [Tk(f"hb{i}") for i in range(2)]
                junk = sb(st, "junk", [128, D], BF16); junk_tk = Tk("junk")
                ss = [sb(st, f"ss{i}", [128, 1], F32) for i in range(2)]; ss_tk = [Tk(f"ss{i}") for i in range(2)]
                rstd = [sb(st, f"rstd{i}", [128, 1], F32) for i in range(2)]; rstd_tk = [Tk(f"rstd{i}") for i in range(2)]
                hT = [sb(st, f"hT{i}", [128, 8, 512], BF16) for i in range(2)]; hT_tk = [Tk(f"hT{i}") for i in range(2)]
                GS = 8
                fst = [sb(st, f"fst{i}", [128, GS, 512], BF16) for i in range(2)]; fst_tk = [Tk(f"fst{i}") for i in range(2)]
                vst = [sb(st, f"vst{i}", [128, 1792], BF16) for i in range(2)]; vst_tk = [Tk(f"vst{i}") for i in range(2)]
                NT = NTOK // 512
                sub = 0
                fgrp = 0
                vgrp = 0

                def s2_load(T):
                    for s_ in range(4):
                        t0 = T * 512 + s_ * 128
                        xb = (T * 4 + s_) % NX
                        dma(xt[xb][:], x_src[t0:t0 + 128, :], writes=[xt_tk[xb]])

                def s2_front(T):
                    nonlocal sub
                    hb_ = T % 2
                    for s_ in range(4):
                        xb = (T * 4 + s_) % NX
                        b2 = sub % 2
                        sub += 1
                        rms_rstd(st, xt[xb][:], xt_tk[xb], junk[:], junk_tk, ss[b2][:], ss_tk[b2], rstd[b2][:], rstd_tk[b2])
                        op("dve", lambda h: h.tensor_scalar(out=hb[b2][:], in0=xt[xb][:], scalar1=rstd[b2][:], scalar2=None, op0=ALU.mult),
                           [xt_tk[xb], rstd_tk[b2]], [hb_tk[b2]])
                        for c in range(8):
                            op("pe", lambda h: h.transpose(ptr[:, c * 128:(c + 1) * 128], hb[b2][:, c * 128:(c + 1) * 128], ident[:]),
                               [hb_tk[b2], ident_tk], [ptr_tk])
                        copy_op(evac_eng(), hT[hb_][:, :, s_ * 128:(s_ + 1) * 128], ptr[:].rearrange("p (c t) -> p c t", c=8),
                                [ptr_tk], [hT_tk[hb_]])
                    dma(HT[:, :, T * 512:(T + 1) * 512].rearrange("c p t -> p c t"), hT[hb_][:], reads=[hT_tk[hb_]])

                s2_load(0)
                s2_load(1)
                s2_front(0)
                for T in range(NT):
                    hb_ = T % 2
                    for g0 in range(0, NB, GS):
                        if g0 == 2 * GS and T + 1 < NT:
                            s2_front(T + 1)
                            if T + 2 < NT:
                                s2_load(T + 2)
                        nb = min(GS, NB - g0)
                        fb = fgrp % 2
                        fgrp += 1
                        for bi in range(nb):
                            col, M, silu = FM_BLOCKS[g0 + bi]
                            if M == 64 and (g0 + bi) % 2 == 1:
                                continue
                            MM = 128
                            ps, ps_tk = next_ps()
                            for c in range(8):
                                op("pe", lambda h: h.matmul(ps[0:MM, :], lhsT=W2[:, c, col:col + MM], rhs=hT[hb_][:, c, :],
                                                            start=(c == 0), stop=(c == 7)), [W2_tk, hT_tk[hb_]], [ps_tk])
                            if silu:
                                op("act", lambda h: h.activation(out=fst[fb][0:M, bi, :], in_=ps[0:M, :], func=AF.Silu), [ps_tk], [fst_tk[fb]])
                            elif M == 64:
                                copy_op("act", fst[fb][0:64, bi, :], ps[0:64, :], [ps_tk], [fst_tk[fb]])
                                copy_op("dve", fst[fb][0:64, bi + 1, :], ps[64:128, :], [ps_tk], [fst_tk[fb]])
                            else:
                                copy_op(evac_eng(), fst[fb][0:M, bi, :], ps[0:M, :], [ps_tk], [fst_tk[fb]])
                        i0 = 0
                        while i0 < nb:
                            M0 = FM_BLOCKS[g0 + i0][1]
                            i1 = i0
                            while i1 < nb and FM_BLOCKS[g0 + i1][1] == M0:
                                i1 += 1
                            dma(FM[g0 + i0:g0 + i1, 0:M0, T * 512:(T + 1) * 512].rearrange("b p t -> p b t"), fst[fb][0:M0, i0:i1, :], reads=[fst_tk[fb]])
                            i0 = i1
                    for s_ in range(4):
                        vb_ = vgrp % 2
                        vgrp += 1
                        t0 = T * 512 + s_ * 128
                        off = 0
                        for (col, w, ti, dcol) in TM_GROUPS:
                            ps, ps_tk = next_ps()
                            for c in range(8):
                                op("pe", lambda h: h.matmul(ps[:, 0:w], lhsT=hT[hb_][:, c, s_ * 128:(s_ + 1) * 128], rhs=W2[:, c, col:col + w],
                                                            start=(c == 0), stop=(c == 7)), [W2_tk, hT_tk[hb_]], [ps_tk])
                            copy_op(evac_eng(), vst[vb_][:, off:off + w], ps[:, 0:w], [ps_tk], [vst_tk[vb_]])
                            off += w
                        dma(VT[0][t0:t0 + 128, :], vst[vb_][:, 0:512], reads=[vst_tk[vb_]])
                        dma(VT[1][t0:t0 + 128, :], vst[vb_][:, 512:1024], reads=[vst_tk[vb_]])
                        dma(VT[2][t0:t0 + 128, :], vst[vb_][:, 1024:1792], reads=[vst_tk[vb_]])
                S.barrier()

            with contextlib.ExitStack() as st:
                _maybe_skip(st, "A" not in stages)
                pb7 = st.enter_context(nc.psum_tensor(f"pb7_{l}a", [128, 512], F32))
                pbanks[7:] = [pb7]
                pb_tk[7:] = [Tk("pb7")]
                LA = 4
                pipe = Pipe(LA)
                QP = sb(st, "QP", [128, NTOK], BF16); QP_tk = Tk("QP")
                KP = sb(st, "KP", [128, NTOK], BF16); KP_tk = Tk("KP")
                ZP = sb(st, "ZP", [128, NTOK], BF16); ZP_tk = Tk("ZP")
                YP = sb(st, "YP", [128, NTOK], BF16); YP_tk = Tk("YP")
                tdup = [sb(st, f"tdup{i}", [128, 15, 64], F32) for i in range(2)]; tdup_tk = [Tk(f"tdup{i}") for i in range(2)]
                BT = [sb(st, f"BT{i}", [128, 20, 512], F32) for i in range(2)]
                BT_tks = [[Tk(f"BT{i}_{j}") for j in range(20)] for i in range(2)]
                VG = [sb(st, f"VG{i}", [128, 32, 128], BF16) for i in range(2)]; VG_tk = [Tk(f"VG{i}") for i in range(2)]
                NTMP = 4
                tmp = [sb(st, f"tmp{i}", [128, 512], F32) for i in range(NTMP)]; tmp_tk = [Tk(f"tmp{i}") for i in range(NTMP)]
                NPT = LA + 2
                PT = [sb(st, f"PT{i}", [128, 512], BF16) for i in range(NPT)]; PT_tk = [Tk(f"PT{i}") for i in range(NPT)]
                rz = [sb(st, f"rz{i}", [128, 512], F32) for i in range(2)]; rz_tk = [Tk(f"rz{i}") for i in range(2)]
                g2 = [sb(st, f"g2{i}", [128, 512], F32) for i in range(2)]; g2_tk = [Tk(f"g2{i}") for i in range(2)]
                set_ring([3, 4, 5, 6, 7])
                op("pool", lambda h: h.memset(VG[0][:, :, 64:128], 1.0), writes=[VG_tk[0]])
                op("pool", lambda h: h.memset(VG[1][:, :, 0:64], 1.0), writes=[VG_tk[1]])
                tbase = {0: 0, 1: 6, 2: 14}

                def build_bias(head):
                    bb = head % 2
                    td, td_tk = tdup[bb], tdup_tk[bb]
                    dma(td[0:64], trev_in[l, head], writes=[td_tk])
                    dma(td[64:128], trev_in[l, head], writes=[td_tk])
                    for ti_ in range(20):
                        op("pool", lambda h: h.memset(BT[bb][:, ti_, :], NEG), writes=[BT_tks[bb][ti_]])
                    rows_total = 32
                    for v, (R, ch0, nch) in enumerate(((0, 0, 6), (8, 2, 8), (rows_total - 8, (rows_total - 12) // 2, 6))):
                        for j in range(nch):
                            ti = tbase[v] + j
                            for kr in range(2):
                                krow = 2 * (ch0 + j) + kr
                                ent = []
                                for qr in range(8):
                                    qrow = R + qr
                                    rs = min(max(qrow - 4, 0), rows_total - 8)
                                    ent.append(14 - (krow - qrow + 7) if rs <= krow < rs + 8 else None)
                                qr = 0
                                while qr < 8:
                                    if ent[qr] is None:
                                        qr += 1
                                        continue
                                    q1 = qr
                                    while q1 + 1 < 8 and ent[q1 + 1] is not None and ent[q1 + 1] == ent[q1] + 1:
                                        q1 += 1
                                    n = q1 - qr + 1
                                    e0 = ent[qr]
                                    pl = 64 * kr
                                    op("pool", lambda h: h.tensor_copy(out=BT[bb][pl:pl + 64, ti, qr * 64:(qr + n) * 64].rearrange("p (a b) -> p a b", b=64),
                                                                       in_=td[pl:pl + 64, e0:e0 + n, :]), [td_tk], [BT_tks[bb][ti]])
                                    qr = q1 + 1

                acnt = 0
                pcnt = 0
                build_bias(0)
                for i in range(4):
                    dma(QP[:], FM[B_QA + i], writes=[QP_tk])
                    dma(KP[:], FM[B_KA + i], writes=[KP_tk])
                    dma(ZP[:], FM[B_ZA + i], writes=[ZP_tk])
                    for hh in range(2):
                        head = 2 * i + hh
                        bb = head % 2
                        ub = 64 * hh
                        zb = 64 - ub
                        if head + 1 < 8:
                            build_bias(head + 1)
                        vt = VG[hh]
                        vt_tk = VG_tk[hh]
                        vc0 = 0 if hh == 0 else 64
                        for (s0, L) in SEQS:
                            rows = L // 64
                            nqt = rows // 8
                            dma(vt[:, 0:L // 128, vc0:vc0 + 64],
                                VT[0][s0:s0 + L, head * 64:(head + 1) * 64].rearrange("(c p) e -> p c e", p=128), writes=[vt_tk])
                            for qt in range(nqt):
                                if qt == 0:
                                    v, ch0, nch = 0, 0, 6
                                elif qt == nqt - 1:
                                    v, ch0, nch = 2, (rows - 12) // 2, 6
                                else:
                                    v, ch0, nch = 1, 4 * qt - 2, 8
                                ab = acnt % 2
                                a3 = acnt % 3
                                acnt += 1
                                acc, acc_tk = pbanks[a3], pb_tk[a3]
                                q0 = s0 + qt * 512
                                for j in range(nch):
                                    kc = ch0 + j
                                    ti = tbase[v] + j
                                    tb = pcnt % NTMP
                                    pb = pcnt % NPT
                                    pcnt += 1
                                    vq = []
                                    for qr in range(8):
                                        qrow = 8 * qt + qr
                                        rs_ = min(max(qrow - 4, 0), rows - 8)
                                        if any(rs_ <= 2 * kc + kr < rs_ + 8 for kr in range(2)):
                                            vq.append(qr)
                                    assert vq and vq == list(range(vq[0], vq[-1] + 1))
                                    cl, ch_ = 64 * vq[0], 64 * (vq[-1] + 1)

                                    def front(kc=kc, ti=ti, tb=tb, pb=pb, q0=q0, s0=s0, ub=ub, bb=bb, cl=cl, ch_=ch_, j=j, acc=acc, acc_tk=acc_tk):
                                        if j == 0:
                                            op("act", lambda h: h.memzero(acc[:]), [], [acc_tk])
                                        ps, ps_tk = next_ps()
                                        op("pe", lambda h: h.matmul(ps[:, cl:ch_], lhsT=KP[ub:ub + 64, s0 + kc * 128:s0 + (kc + 1) * 128],
                                                                    rhs=QP[ub:ub + 64, q0 + cl:q0 + ch_], start=True, stop=True), [KP_tk, QP_tk], [ps_tk])
                                        op("dve", lambda h: h.scalar_tensor_tensor(out=tmp[tb][:, cl:ch_], in0=ps[:, cl:ch_], scalar=0.125, in1=BT[bb][:, ti, cl:ch_],
                                                                                   op0=ALU.mult, op1=ALU.add), [ps_tk, BT_tks[bb][ti]], [tmp_tk[tb]])
                                        op("act", lambda h: h.activation(out=PT[pb][:, cl:ch_], in_=tmp[tb][:, cl:ch_], func=AF.Exp), [tmp_tk[tb]], [PT_tk[pb]])

                                    def back(kc=kc, pb=pb, j=j, nch=nch, acc=acc, acc_tk=acc_tk, vt=vt, vt_tk=vt_tk, ub=ub, zb=zb, q0=q0, ab=ab, cl=cl, ch_=ch_):
                                        op("pe", lambda h: h.matmul(acc[:, cl:ch_], lhsT=vt[:, kc, :], rhs=PT[pb][:, cl:ch_], start=False, stop=True,
                                                                    skip_group_check=True), [vt_tk, PT_tk[pb]], [acc_tk])
                                        if j == nch - 1:
                                            def post(acc=acc, acc_tk=acc_tk, ub=ub, zb=zb, q0=q0, ab=ab):
                                                op("act", lambda h: h.activation(out=rz[ab][ub:ub + 64, :], in_=acc[zb:zb + 64, :], func=AF.Ln), [acc_tk], [rz_tk[ab]])
                                                op("act", lambda h: h.activation(out=rz[ab][ub:ub + 64, :], in_=rz[ab][ub:ub + 64, :], func=AF.Exp, scale=-1.0),
                                                   [rz_tk[ab]], [rz_tk[ab]])
                                                op("dve", lambda h: h.tensor_tensor(out=g2[ab][ub:ub + 64, :], in0=rz[ab][ub:ub + 64, :], in1=ZP[ub:ub + 64, q0:q0 + 512],
                                                                                    op=ALU.mult), [rz_tk[ab], ZP_tk], [g2_tk[ab]])
                                                op("dve", lambda h: h.tensor_tensor(out=YP[ub:ub + 64, q0:q0 + 512], in0=acc[ub:ub + 64, :], in1=g2[ab][ub:ub + 64, :],
                                                                                    op=ALU.mult), [acc_tk, g2_tk[ab]], [YP_tk])
                                            pipe.defer(post, 3)
                                    pipe.push(front, back)
                            pipe.flush()
                    dma(YG[i], YP[:], reads=[YP_tk])
                set_ring(range(7))
                del pbanks[7:], pb_tk[7:]
                S.barrier()

            with contextlib.ExitStack() as st:
                _maybe_skip(st, "B" not in stages)
                pb7 = st.enter_context(nc.psum_tensor(f"pb7_{l}b", [128, 512], F32))
                pbanks[7:] = [pb7]
                pb_tk[7:] = [Tk("pb7")]
                LA = 3
                pipe = Pipe(LA)
                LM = 4096
                Qa = [[sb(st, f"Qa{m}{i}", [69, LM], BF16) for m in range(2)] for i in range(2)]
                Kp = [[sb(st, f"Kp{m}{i}", [69, LM], BF16) for m in range(2)] for i in range(2)]
                Kn = [[sb(st, f"Kn{m}{i}", [69, LM], BF16) for m in range(2)] for i in range(2)]
                Qa_tk = [[Tk("Qa") for m in range(2)] for i in range(2)]
                Kp_tk = [[Tk("Kp") for m in range(2)] for i in range(2)]
                Kn_tk = [[Tk("Kn") for m in range(2)] for i in range(2)]
                Vb = [sb(st, f"Vb{i}", [128, 32, 128], BF16) for i in range(2)]; Vb_tk = [Tk("Vb") for i in range(2)]
                Zb = [sb(st, f"Zb{i}", [128, LM], BF16) for i in range(2)]; Zb_tk = [Tk("Zb") for i in range(2)]
                Yb = [sb(st, f"Yb{i}", [128, LM], BF16) for i in range(2)]; Yb_tk = [Tk("Yb") for i in range(2)]
                cdg = sb(st, "cdg", [128, 4, 128], F32); cdg_tk = Tk("cdg")
                NPT = LA + 2
                PT = [sb(st, f"PTb{i}", [128, 512], BF16) for i in range(NPT)]; PT_tk = [Tk(f"PTb{i}") for i in range(NPT)]
                r0 = sb(st, "r0", [128, 512], F32); r0_tk = Tk("r0")
                r1 = sb(st, "r1", [128, 512], F32); r1_tk = Tk("r1")
                a0 = sb(st, "a0", [128, 512], F32); a0_tk = Tk("a0")
                a1 = sb(st, "a1", [128, 512], F32); a1_tk = Tk("a1")
                o_t = sb(st, "o_t", [128, 512], F32); o_tk = Tk("o_t")
                sq = sb(st, "sq", [128, 512], BF16); sq_tk = Tk("sq")
                rs_t = sb(st, "rs_t", [128, 512], F32); rs_tk = Tk("rs_t")
                t2 = sb(st, "t2", [128, 512], F32); t2_tk = Tk("t2")
                Pa = [[sb(st, f"Pa{i}{e}", [128, 512], F32) for e in range(2)] for i in range(2)]
                Pa_tk = [[Tk(f"Pa{i}{e}") for e in range(2)] for i in range(2)]
                dma(cdg[:], cdiag_in, writes=[cdg_tk])
                pcnt = 0
                qmcnt = 0
                set_ring([5, 6, 7])
                jobs = [(s0, L, hd) for (s0, L) in SEQS for hd in range(4)]

                def b_load(jn):
                    s0, L, hd = jobs[jn]
                    sb_ = jn % 2
                    for m in range(2):
                        bq = B_QB + 2 * hd + m
                        bk = B_KB + 2 * hd + m
                        dma(Qa[sb_][m][0:64, 0:L], FM[bq, 0:64, s0:s0 + L], writes=[Qa_tk[sb_][m]])
                        dma(Qa[sb_][m][64:69, 0:L], qaug_in[:, 0:L], writes=[Qa_tk[sb_][m]])
                        dma(Kp[sb_][m][0:64, 0:L], FM[bk, 0:64, s0:s0 + L], writes=[Kp_tk[sb_][m]])
                        dma(Kp[sb_][m][64:69, 0:L], kaug_in[hd, 0, :, 0:L], writes=[Kp_tk[sb_][m]])
                        dma(Kn[sb_][m][0:64, 0:L], FM[bk, 0:64, s0:s0 + L], writes=[Kn_tk[sb_][m]])
                        dma(Kn[sb_][m][64:69, 0:L], kaug_in[hd, 1, :, 0:L], writes=[Kn_tk[sb_][m]])
                    dma(Vb[sb_][:, 0:L // 128, :], VT[1][s0:s0 + L, hd * 128:(hd + 1) * 128].rearrange("(c p) e -> p c e", p=128), writes=[Vb_tk[sb_]])
                    dma(Zb[sb_][:, 0:L], FM[B_ZB + hd, :, s0:s0 + L], writes=[Zb_tk[sb_]])

                b_load(0)
                for jn, (s0, L, hd) in enumerate(jobs):
                    sb_ = jn % 2
                    if jn + 1 < len(jobs):
                        b_load(jn + 1)
                    nkc = L // 128
                    nqt = L // 512
                    slope = 2.0 ** (-8.0 * (hd + 1) / 4)
                    for qt in range(nqt):
                        q0 = qt * 512
                        for m in range(2):
                            U, U_tk = pbanks[2 * m], pb_tk[2 * m]
                            Z, Z_tk = pbanks[2 * m + 1], pb_tk[2 * m + 1]
                            kcs = []
                            for kc in range(nkc):
                                k0 = kc * 128
                                if k0 + 128 <= q0:
                                    mind = q0 - (k0 + 127)
                                elif k0 >= q0 + 512:
                                    mind = k0 - (q0 + 511)
                                else:
                                    mind = 0
                                if slope * mind > 60.0:
                                    continue
                                kcs.append(kc)
                            nk_ = len(kcs)
                            assert nk_ > LA + 2
                            pset = qmcnt % 2
                            qmcnt += 1
                            for ji, kc in enumerate(kcs):
                                pb = pcnt % NPT
                                pcnt += 1

                                def front(kc=kc, pb=pb, q0=q0, m=m, hd=hd, sb_=sb_, ji=ji, pset=pset):
                                    k0 = kc * 128
                                    ps, ps_tk = next_ps()
                                    qa_, kp_, kn_ = Qa[sb_][m], Kp[sb_][m], Kn[sb_][m]
                                    qa_tk, kp_tk, kn_tk = Qa_tk[sb_][m], Kp_tk[sb_][m], Kn_tk[sb_][m]
                                    if k0 + 128 <= q0:
                                        op("pe", lambda h: h.matmul(ps[:], lhsT=kp_[:, k0:k0 + 128], rhs=qa_[:, q0:q0 + 512], start=True, stop=True),
                                           [kp_tk, qa_tk], [ps_tk])
                                    elif k0 >= q0 + 512:
                                        op("pe", lambda h: h.matmul(ps[:], lhsT=kn_[:, k0:k0 + 128], rhs=qa_[:, q0:q0 + 512], start=True, stop=True),
                                           [kn_tk, qa_tk], [ps_tk])
                                    else:
                                        jd = (k0 - q0) // 128
                                        if jd > 0:
                                            op("pe", lambda h: h.matmul(ps[:, 0:128 * jd], lhsT=kn_[:, k0:k0 + 128], rhs=qa_[:, q0:q0 + 128 * jd],
                                                                        start=True, stop=True), [kn_tk, qa_tk], [ps_tk])
                                        op("pe", lambda h: h.matmul(ps[:, 128 * jd:512], lhsT=kp_[:, k0:k0 + 128], rhs=qa_[:, q0 + 128 * jd:q0 + 512],
                                                                    start=True, stop=True), [kp_tk, qa_tk], [ps_tk])
                                        op("dve", lambda h: h.tensor_tensor(out=ps[:, 128 * jd:128 * (jd + 1)], in0=ps[:, 128 * jd:128 * (jd + 1)],
                                                                            in1=cdg[:, hd, :], op=ALU.add), [ps_tk, cdg_tk], [ps_tk])
                                    op("act", lambda h: h.activation(out=PT[pb][:], in_=ps[:], func=AF.Exp, scale=0.125), [ps_tk], [PT_tk[pb]])

                                def back(kc=kc, pb=pb, ji=ji, nk_=nk_, U=U, U_tk=U_tk, Z=Z, Z_tk=Z_tk, m=m, q0=q0, sb_=sb_, pset=pset):
                                    op("pe", lambda h: h.matmul(U[:], lhsT=Vb[sb_][:, kc, :], rhs=PT[pb][:], start=(ji == 0), stop=(ji == nk_ - 1)),
                                       [Vb_tk[sb_], PT_tk[pb]], [U_tk])
                                    op("pe", lambda h: h.matmul(Z[:], lhsT=ones_b[:], rhs=PT[pb][:], start=(ji == 0), stop=(ji == nk_ - 1)),
                                       [ones_tk, PT_tk[pb]], [Z_tk])
                                    if ji == nk_ - 1 and m == 0:
                                        op("act", lambda h: h.activation(out=r0[:], in_=Z[:], func=AF.Ln), [Z_tk], [r0_tk])
                                        op("act", lambda h: h.activation(out=r0[:], in_=r0[:], func=AF.Exp, scale=-1.0), [r0_tk], [r0_tk])
                                        op("dve", lambda h: h.tensor_tensor(out=a0[:], in0=U[:], in1=r0[:], op=ALU.mult), [U_tk, r0_tk], [a0_tk])
                                    if ji == nk_ - 1 and m == 1:
                                        op("act", lambda h: h.activation(out=r1[:], in_=Z[:], func=AF.Ln), [Z_tk], [r1_tk])
                                        op("act", lambda h: h.activation(out=r1[:], in_=r1[:], func=AF.Exp, scale=-1.0), [r1_tk], [r1_tk])
                                        op("dve", lambda h: h.tensor_tensor(out=a1[:], in0=U[:], in1=r1[:], op=ALU.mult), [U_tk, r1_tk], [a1_tk])
                                        op("dve", lambda h: h.scalar_tensor_tensor(out=o_t[:], in0=a1[:], scalar=nlam_t[:, l:l + 1], in1=a0[:],
                                                                                   op0=ALU.mult, op1=ALU.add), [a1_tk, a0_tk, nlam_tk], [o_tk])
                                        op("pool", lambda h: h.tensor_tensor(out=sq[:], in0=o_t[:], in1=o_t[:], op=ALU.mult), [o_tk], [sq_tk])

                                        def post2(q0=q0, sb_=sb_):
                                            ssp, ssp_tk = pbanks[4], pb_tk[4]
                                            op("pe", lambda h: h.matmul(ssp[:], lhsT=ones_b[:], rhs=sq[:], start=True, stop=True), [ones_tk, sq_tk], [ssp_tk])
                                            op("act", lambda h: h.activation(out=rs_t[:], in_=ssp[:], func=AF.Ln, bias=eps_t[:], scale=1.0 / 128), [ssp_tk, eps_tk], [rs_tk])
                                            op("act", lambda h: h.activation(out=rs_t[:], in_=rs_t[:], func=AF.Exp, scale=-0.5), [rs_tk], [rs_tk])
                                            op("pool", lambda h: h.tensor_tensor(out=t2[:], in0=o_t[:], in1=rs_t[:], op=ALU.mult), [o_tk, rs_tk], [t2_tk])
                                            op("pool", lambda h: h.tensor_tensor(out=t2[:], in0=t2[:], in1=Zb[sb_][:, q0:q0 + 512], op=ALU.mult), [t2_tk, Zb_tk[sb_]], [t2_tk])
                                            op("pool", lambda h: h.tensor_scalar(out=Yb[sb_][:, q0:q0 + 512], in0=t2[:], scalar1=gdl_t[:, l:l + 1], scalar2=None,
                                                                                 op0=ALU.mult), [t2_tk, gdl_tk], [Yb_tk[sb_]])
                                        pipe.defer(post2, 3)
                                pipe.push(front, back)
                    pipe.flush()
                    dma(YG[4 + hd, :, s0:s0 + L], Yb[sb_][:, 0:L], reads=[Yb_tk[sb_]])
                set_ring(range(7))
                del pbanks[7:], pb_tk[7:]
                S.barrier()

            with contextlib.ExitStack() as st:
                _maybe_skip(st, "C" not in stages)
                pb7 = st.enter_context(nc.psum_tensor(f"pb7_{l}c", [128, 512], F32))
                pbanks[7:] = [pb7]
                pb_tk[7:] = [Tk("pb7")]
                LA = 4
                pipe = Pipe(LA)
                LM = 4096
                PAD = 1024
                QD = [sb(st, f"QD{i}", [128, LM], BF16) for i in range(2)]; QD_tk = [Tk(f"QD{i}") for i in range(2)]
                KD = [sb(st, f"KD{i}", [128, LM + 2 * PAD], BF16) for i in range(2)]; KD_tk = [Tk(f"KD{i}") for i in range(2)]
                ZD = [sb(st, f"ZD{i}", [128, LM], BF16) for i in range(2)]; ZD_tk = [Tk(f"ZD{i}") for i in range(2)]
                YD = [sb(st, f"YD{i}", [128, LM], BF16) for i in range(2)]; YD_tk = [Tk(f"YD{i}") for i in range(2)]
                VD = [sb(st, f"VD{i}", [128, 48, 128], BF16) for i in range(2)]; VD_tk = [Tk(f"VD{i}") for i in range(2)]
                AT = [sb(st, f"AT{i}", [128, 2, LM], F32) for i in range(2)]; AT_tk = [Tk(f"AT{i}") for i in range(2)]
                dlb = sb(st, "dlb", [128, 12, 256], F32); dlb_tk = Tk("dlb")
                NTMP = 4
                tmp = [sb(st, f"tmpc{i}", [128, 256], F32) for i in range(NTMP)]; tmp_tk = [Tk(f"tmpc{i}") for i in range(NTMP)]
                NPT = LA + 2
                PT = [sb(st, f"PTc{i}", [128, 256], BF16) for i in range(NPT)]; PT_tk = [Tk(f"PTc{i}") for i in range(NPT)]
                NRC = 3
                rzc = [sb(st, f"rzc{i}", [128, 512], F32) for i in range(NRC)]; rzc_tk = [Tk(f"rzc{i}") for i in range(NRC)]
                dma(dlb[:], dilb_in, writes=[dlb_tk])
                for i in range(2):
                    op("pool", lambda h: h.memset(KD[i][:], 0.0), writes=[KD_tk[i]])
                acc_tks = [pb_tk[3 + i] for i in range(4)]
                set_ring([0, 1, 2, 7])
                pcnt = 0
                rcnt = 0
                cjobs = [(s0, L, p, g) for (s0, L) in SEQS for p in range(2) for g in range(3)]
                hjobs = [(jn, hh) for jn in range(len(cjobs)) for hh in range(2)]

                def c_load(jn):
                    s0, L, p, g = cjobs[jn]
                    lb = jn % 2
                    blk = 2 * g + p
                    dma(QD[lb][:, 0:L], FM[B_QC + blk, :, s0:s0 + L], writes=[QD_tk[lb]])
                    dma(KD[lb][:, PAD:PAD + L], FM[B_KC + blk, :, s0:s0 + L], writes=[KD_tk[lb]])
                    if L < LM:
                        op("pool", lambda h: h.memset(KD[lb][:, PAD + L:PAD + L + PAD], 0.0), writes=[KD_tk[lb]])
                    if g == 0:
                        zi = (jn // 3) % 2
                        dma(ZD[zi][:, 0:L], FM[B_ZC + p, :, s0:s0 + L], writes=[ZD_tk[zi]])

                def v_prep(hn):
                    jn, hh = hjobs[hn]
                    s0, L, p, g = cjobs[jn]
                    d = DIL[g][1]
                    M = L // d
                    nch = M // 128 + 1
                    hc = 4 * g + 2 * p + hh
                    vc0 = 0 if hh == 0 else 64
                    oc0 = 64 - vc0
                    vt, vt_tk = VD[hn % 2], VD_tk[hn % 2]
                    vt4 = vt[:, 0:d * nch, :].rearrange("p (r j) e -> p r j e", r=d)
                    op("pool", lambda h: h.memset(vt[:, 0:d * nch, :], 0.0), writes=[vt_tk])
                    if nch > 2:
                        op("pool", lambda h: h.memset(vt4[:, :, 1:nch - 1, oc0:oc0 + 64], 1.0), writes=[vt_tk])
                    op("pool", lambda h: h.memset(vt4[64:128, :, 0, oc0:oc0 + 64], 1.0), writes=[vt_tk])
                    op("pool", lambda h: h.memset(vt4[0:64, :, nch - 1, oc0:oc0 + 64], 1.0), writes=[vt_tk])
                    vsrc = VT[2][s0:s0 + L, hc * 64:(hc + 1) * 64]
                    for r in range(d):
                        if nch > 2:
                            src = vsrc[64 * d:64 * d + 128 * (nch - 2) * d, :].rearrange("(j k r) e -> r k j e", k=128, r=d)[r]
                            dma(vt4[:, r, 1:nch - 1, vc0:vc0 + 64], src, writes=[vt_tk])
                        src = vsrc[0:64 * d, :].rearrange("(k r) e -> r k e", r=d)[r]
                        dma(vt4[64:128, r, 0, vc0:vc0 + 64], src, writes=[vt_tk])
                        src = vsrc[(M - 64) * d:M * d, :].rearrange("(k r) e -> r k e", r=d)[r]
                        dma(vt4[0:64, r, nch - 1, vc0:vc0 + 64], src, writes=[vt_tk])

                c_load(0)
                v_prep(0)
                for hn, (jn, hh) in enumerate(hjobs):
                    s0, L, p, g = cjobs[jn]
                    d = DIL[g][1]
                    M = L // d
                    nsub = M // 128
                    nch = nsub + 1
                    lb = jn % 2
                    ai = (jn // 3) % 2
                    if hh == 0 and jn + 1 < len(cjobs):
                        c_load(jn + 1)
                    hc = 4 * g + 2 * p + hh
                    ub = 64 * hh
                    vt, vt_tk = VD[hn % 2], VD_tk[hn % 2]
                    npush = 0
                    for r in range(d):
                        for j in range(nch):
                            if npush == LA + 1 and hn + 1 < len(hjobs):
                                v_prep(hn + 1)
                            npush += 1
                            tb = pcnt % NTMP
                            pb = pcnt % NPT
                            pcnt += 1

                            def front(r=r, j=j, d=d, nsub=nsub, lb=lb, ub=ub, hc=hc, tb=tb, pb=pb):
                                kstart = PAD + (128 * j - 64) * d + r
                                qlo = max(j - 1, 0)
                                c_lo = 0 if j - 1 >= 0 else 128
                                c_hi = 256 if j <= nsub - 1 else 128
                                ps, ps_tk = next_ps()
                                qa0 = (128 * qlo) * d + r
                                nq = c_hi - c_lo
                                op("pe", lambda h: h.matmul(ps[:, c_lo:c_hi], lhsT=KD[lb][ub:ub + 64, kstart:kstart + 127 * d + 1:d],
                                                            rhs=QD[lb][ub:ub + 64, qa0:qa0 + (nq - 1) * d + 1:d], start=True, stop=True),
                                   [KD_tk[lb], QD_tk[lb]], [ps_tk])
                                op("dve", lambda h: h.scalar_tensor_tensor(out=tmp[tb][:, c_lo:c_hi], in0=ps[:, c_lo:c_hi], scalar=0.125,
                                                                           in1=dlb[:, hc, c_lo:c_hi], op0=ALU.mult, op1=ALU.add),
                                   [ps_tk, dlb_tk], [tmp_tk[tb]])
                                op("act", lambda h: h.activation(out=PT[pb][:, c_lo:c_hi], in_=tmp[tb][:, c_lo:c_hi], func=AF.Exp),
                                   [tmp_tk[tb]], [PT_tk[pb]])

                            def back(r=r, j=j, d=d, nsub=nsub, nch=nch, pb=pb, vt=vt, vt_tk=vt_tk, hh=hh, g=g, ai=ai):
                                if j - 1 >= 0:
                                    sl = (j - 1) % 4
                                    op("pe", lambda h: h.matmul(pbanks[3 + sl][:, 0:128], lhsT=vt[:, r * nch + j, :], rhs=PT[pb][:, 0:128],
                                                                start=False, stop=True), [vt_tk, PT_tk[pb]], [acc_tks[sl]])
                                    a0_ = (128 * (j - 1)) * d + r
                                    dst = AT[ai][:, hh, a0_:a0_ + 127 * d + 1:d]
                                    if g == 0:
                                        op("act", lambda h: h.copy(out=dst, in_=pbanks[3 + sl][:, 0:128]), [acc_tks[sl]], [AT_tk[ai]])
                                    else:
                                        op("dve", lambda h: h.tensor_tensor(out=dst, in0=pbanks[3 + sl][:, 0:128], in1=dst, op=ALU.add),
                                           [acc_tks[sl], AT_tk[ai]], [AT_tk[ai]])
                                if j <= nsub - 1:
                                    sl = j % 4
                                    op("pe", lambda h: h.matmul(pbanks[3 + sl][:, 0:128], lhsT=vt[:, r * nch + j, :], rhs=PT[pb][:, 128:256],
                                                                start=True, stop=False), [vt_tk, PT_tk[pb]], [acc_tks[sl]])
                            pipe.push(front, back)
                    if g == 2 and hh == 1:
                        pipe.flush()
                        last_job = (hn == len(hjobs) - 1)
                        kk = 0
                        for hh2 in range(2):
                            for c0 in range(0, L, 512):
                                ri = rcnt % NRC
                                rcnt += 1

                                def piece(hh2=hh2, c0=c0, ri=ri, ai=ai):
                                    ub2 = 64 * hh2
                                    zb2 = 64 - ub2
                                    op("act", lambda h: h.activation(out=rzc[ri][ub2:ub2 + 64, :], in_=AT[ai][zb2:zb2 + 64, hh2, c0:c0 + 512], func=AF.Ln),
                                       [AT_tk[ai]], [rzc_tk[ri]])
                                    op("act", lambda h: h.activation(out=rzc[ri][ub2:ub2 + 64, :], in_=rzc[ri][ub2:ub2 + 64, :], func=AF.Exp, scale=-1.0),
                                       [rzc_tk[ri]], [rzc_tk[ri]])
                                    op("pool", lambda h: h.tensor_tensor(out=rzc[ri][ub2:ub2 + 64, :], in0=rzc[ri][ub2:ub2 + 64, :], in1=ZD[ai][ub2:ub2 + 64, c0:c0 + 512],
                                                                         op=ALU.mult), [rzc_tk[ri], ZD_tk[ai]], [rzc_tk[ri]])
                                    op("pool", lambda h: h.tensor_tensor(out=YD[ai][ub2:ub2 + 64, c0:c0 + 512], in0=AT[ai][ub2:ub2 + 64, hh2, c0:c0 + 512],
                                                                         in1=rzc[ri][ub2:ub2 + 64, :], op=ALU.mult), [AT_tk[ai], rzc_tk[ri]], [YD_tk[ai]])
                                if last_job:
                                    piece()
                                else:
                                    pipe.defer(piece, 2 + 2 * kk)
                                kk += 1

                        def store(p=p, s0=s0, L=L, ai=ai):
                            dma(YG[8 + p, :, s0:s0 + L], YD[ai][:, 0:L], reads=[YD_tk[ai]])
                        if last_job:
                            store()
                        else:
                            pipe.defer(store, 2 + 2 * kk + 6)
                pipe.flush()
                set_ring(range(7))
                del pbanks[7:], pb_tk[7:]
                S.barrier()

            with contextlib.ExitStack() as st:
                _maybe_skip(st, "6" not in stages)
                WG = sb(st, "WG", [128, 8, 3072], BF16); WG_tk = Tk("WG")
                WBR = sb(st, "WBR", [128, 10, D], BF16); WBR_tk = Tk("WBR")
                WO = sb(st, "WO", [128, 8, D], BF16); WO_tk = Tk("WO")
                with contextlib.ExitStack() as st2:
                    stgs = make_stg(st2)
                    load_cast(stgs, WG, WG_tk, w_in[l, :, GL:INW].rearrange("(c p) n -> p c n", p=128), 8, 3072, l)
                    load_cast(stgs, WBR[:, 0:4], WBR_tk, w_bra[l].rearrange("(c p) n -> p c n", p=128), 4, D, None)
                    load_cast(stgs, WBR[:, 4:8], WBR_tk, w_brb[l].rearrange("(c p) n -> p c n", p=128), 4, D, None)
                    load_cast(stgs, WBR[:, 8:10], WBR_tk, w_brc[l].rearrange("(c p) n -> p c n", p=128), 2, D, None)
                    load_cast(stgs, WO, WO_tk, w_out[l].rearrange("(c p) n -> p c n", p=128), 8, D, None)
                    S.barrier()
                hT = [sb(st, f"hT6{i}", [128, 8, 512], BF16) for i in range(2)]; hT_tk = [Tk(f"hT6{i}") for i in range(2)]
                yg = [sb(st, f"yg6{i}", [128, 10, 512], BF16) for i in range(2)]; yg_tk = [Tk(f"yg6{i}") for i in range(2)]
                xt = [sb(st, f"x6{i}", [128, D], F32) for i in range(8)]; xt_tk = [Tk(f"x6{i}") for i in range(8)]
                xo = [sb(st, f"xo6{i}", [128, D], F32) for i in range(2)]; xo_tk = [Tk(f"xo6{i}") for i in range(2)]
                gt = [sb(st, f"gt6{i}", [128, 512], F32) for i in range(3)]; gt_tk = [Tk(f"gt6{i}") for i in range(3)]
                mt = [sb(st, f"mt6{i}", [128, 512], F32) for i in range(3)]; mt_tk = [Tk(f"mt6{i}") for i in range(3)]
                mg = [sb(st, f"mg6{i}", [128, 8, 512], BF16) for i in range(2)]; mg_tk = [Tk(f"mg6{i}") for i in range(2)]
                junk = sb(st, "junk6", [128, D], BF16); junk_tk = Tk("junk6")
                ss = [sb(st, f"ss6{i}", [128, 1], F32) for i in range(2)]; ss_tk = [Tk(f"ss6{i}") for i in range(2)]
                rstd = [sb(st, f"rstd6{i}", [128, 1], F32) for i in range(2)]; rstd_tk = [Tk(f"rstd6{i}") for i in range(2)]
                NT = NTOK // 512
                KBR = [(0, 4), (4, 4), (8, 2)]
                xcnt = 0
                ocnt = 0

                def s6_load(T):
                    b = T % 2
                    dma(hT[b][:], HT[:, :, T * 512:(T + 1) * 512].rearrange("c p t -> p c t"), writes=[hT_tk[b]])
                    dma(yg[b][:], YG[:, :, T * 512:(T + 1) * 512].rearrange("c p t -> p c t"), writes=[yg_tk[b]])

                s6_load(0)
                for T in range(NT):
                    b = T % 2
                    if T + 1 < NT:
                        s6_load(T + 1)
                    for s_ in range(4):
                        t0 = T * 512 + s_ * 128
                        dma(xt[(4 * T + s_) % 8][:], x_src[t0:t0 + 128, :], writes=[xt_tk[(4 * T + s_) % 8]])
                    for c in range(8):
                        for br in range(3):
                            gps, gps_tk = next_ps()
                            for k in range(8):
                                op("pe", lambda h: h.matmul(gps[:], lhsT=WG[:, k, br * D + c * 128:br * D + (c + 1) * 128], rhs=hT[b][:, k, :],
                                                            start=(k == 0), stop=(k == 7)), [WG_tk, hT_tk[b]], [gps_tk])
                            op("act", lambda h: h.activation(out=gt[br][:], in_=gps[:], func=AF.Sigmoid, bias=bg_t[:, l, br * 8 + c:br * 8 + c + 1]),
                               [gps_tk, bg_tk], [gt_tk[br]])
                            pps, pps_tk = next_ps()
                            k0, nk = KBR[br]
                            for k in range(nk):
                                op("pe", lambda h: h.matmul(pps[:], lhsT=WBR[:, k0 + k, c * 128:(c + 1) * 128], rhs=yg[b][:, k0 + k, :],
                                                            start=(k == 0), stop=(k == nk - 1)), [WBR_tk, yg_tk[b]], [pps_tk])
                            op("dve", lambda h: h.tensor_tensor(out=mt[br][:], in0=pps[:], in1=gt[br][:], op=ALU.mult), [pps_tk, gt_tk[br]], [mt_tk[br]])
                        op("pool", lambda h: h.tensor_tensor(out=mt[0][:], in0=mt[0][:], in1=mt[1][:], op=ALU.add), [mt_tk[0], mt_tk[1]], [mt_tk[0]])
                        op("pool", lambda h: h.tensor_tensor(out=mg[b][:, c, :], in0=mt[0][:], in1=mt[2][:], op=ALU.add), [mt_tk[0], mt_tk[2]], [mg_tk[b]])
                    for s_ in range(4):
                        t0 = T * 512 + s_ * 128
                        xb = (4 * T + s_) % 8
                        ob = ocnt % 2
                        ocnt += 1
                        for half in range(2):
                            ops_, ops_tk = next_ps()
                            for k in range(8):
                                op("pe", lambda h: h.matmul(ops_[:], lhsT=mg[b][:, k, s_ * 128:(s_ + 1) * 128], rhs=WO[:, k, half * 512:(half + 1) * 512],
                                                            start=(k == 0), stop=(k == 7)), [mg_tk[b], WO_tk], [ops_tk])
                            op("dve", lambda h: h.tensor_tensor(out=xo[ob][:, half * 512:(half + 1) * 512], in0=ops_[:], in1=xt[xb][:, half * 512:(half + 1) * 512],
                                                                op=ALU.add), [ops_tk, xt_tk[xb]], [xo_tk[ob]])
                        if l < depth - 1:
                            dma(XS[t0:t0 + 128, :], xo[ob][:], reads=[xo_tk[ob]])
                        else:
                            b2 = ocnt % 2
                            rms_rstd(st, xo[ob][:], xo_tk[ob], junk[:], junk_tk, ss[b2][:], ss_tk[b2], rstd[b2][:], rstd_tk[b2])
                            op("dve", lambda h: h.scalar_tensor_tensor(out=xo[ob][:], in0=xo[ob][:], scalar=rstd[b2][:], in1=gf_t[:],
                                                                        op0=ALU.mult, op1=ALU.mult), [xo_tk[ob], rstd_tk[b2], gf_tk], [xo_tk[ob]])
                            dma(y_out[t0:t0 + 128, :], xo[ob][:], reads=[xo_tk[ob]])
                S.barrier()
        S.barrier()
        import sys as _sys
        print("[build] nsem", S.nsem, {n: e.count for n, e in S.E.items()}, file=_sys.stderr)
    return nc


def _consts():
    bf = ml_dtypes.bfloat16
    ident = np.eye(128, dtype=np.float32).astype(bf)
    pos = np.arange(4096)
    qaug = np.zeros((5, 4096), np.float32)
    qaug[0] = (pos % 512) // 16
    qaug[1] = pos % 16
    qaug[2] = (pos // 512) * 512
    qaug[3] = 1.0
    qaug[4] = 1.0
    kaug = np.zeros((4, 2, 5, 4096), np.float32)
    for h in range(4):
        s = 2.0 ** (-8.0 * (h + 1) / 4)
        for si, sg in enumerate((1.0, -1.0)):
            kaug[h, si, 0] = -16.0 * s * 8.0 * sg
            kaug[h, si, 1] = -s * 8.0 * sg
            kaug[h, si, 2] = -s * 8.0 * sg
            kaug[h, si, 3] = sg * s * 8.0 * (pos % 128)
            kaug[h, si, 4] = sg * s * 8.0 * ((pos // 128) * 128)
    cdiag = np.zeros((128, 4, 128), np.float32)
    kk = np.arange(128)[:, None]
    qq = np.arange(128)[None, :]
    for h in range(4):
        s = 2.0 ** (-8.0 * (h + 1) / 4)
        cdiag[:, h, :] = np.where(qq < kk, 16.0 * s * (qq - kk), 0.0)
    dilb = np.zeros((128, 12, 256), np.float32)
    kk = np.arange(128)[:, None]
    qq = np.arange(256)[None, :]
    rel = qq - kk - 64
    for hc in range(12):
        s = np.float32(2.0 ** (-8.0 * (hc + 1) / 12))
        d = DIL[hc // 4][1]
        dilb[:, hc, :] = np.where(np.abs(rel) <= 64, -(s * np.float32(d) * np.abs(rel).astype(np.float32)), NEG)
    return dict(ident=ident, qaug=qaug.astype(bf), kaug=kaug.astype(bf), cdiag=cdiag, dilb=dilb)


def _trev(rpb):
    kc = np.arange(64)[:, None]
    qc = np.arange(64)[None, :]
    qstart = np.clip(qc - 8, 0, 48)
    ok = (kc >= qstart) & (kc < qstart + 16)
    cidx = np.clip(kc - qc + 15, 0, 30)
    t = rpb[:, :, :, cidx]
    t = np.where(ok[None, None, None], t, np.float32(NEG))
    t = t[:, :, ::-1]
    return np.ascontiguousarray(np.transpose(t, (0, 1, 3, 2, 4))).astype(np.float32)


_NC_CACHE = {}


def kernel(x_prompt, x_sample, g_norm, w_in, b_gate, rpb, lam_qk, g_diff, w_br_a, w_br_b, w_br_c, w_out, g_final, _depth=DEPTH, _debug=False, _stages="2ABC6", _ncores=8, _trace=False):
    f32 = lambda a: np.ascontiguousarray(np.asarray(a, dtype=np.float32))
    x_prompt, x_sample = f32(x_prompt), f32(x_sample)
    shared = dict(
        w_in=f32(w_in), w_br_a=f32(w_br_a), w_br_b=f32(w_br_b), w_br_c=f32(w_br_c), w_out=f32(w_out),
        gk=f32(np.transpose(np.asarray(g_norm).reshape(DEPTH, 8, 128), (0, 2, 1))),
        bg=f32(np.transpose(np.asarray(b_gate).reshape(DEPTH, 24, 128), (0, 2, 1))),
        gd=f32(np.asarray(g_diff).reshape(DEPTH, 128).T),
        lq=f32(np.asarray(lam_qk).reshape(DEPTH, 256)),
        gf=f32(g_final),
        trev=_trev(f32(rpb)),
    )
    shared.update(_consts())
    key = (_depth, _debug, _stages)
    if key not in _NC_CACHE:
        _NC_CACHE[key] = build_nc(_depth, _debug, _stages)
    nc = _NC_CACHE[key]
    in_maps = []
    for c in range(_ncores):
        xc = np.concatenate([x_prompt[2 * c], x_prompt[2 * c + 1], x_sample[c]], axis=0)
        m = dict(shared)
        m["x"] = np.ascontiguousarray(xc)
        in_maps.append(m)
    res = run_bass_kernel_spmd(nc, in_maps, core_ids=list(range(_ncores)), **({'trace': True} if _trace else {}))
    if _debug or _trace:
        return res
    y_prompt = np.empty((16, 2048, D), np.float32)
    y_sample = np.empty((8, 4096, D), np.float32)
    for c in range(8):
        y = res.results[c]["y"]
        y_prompt[2 * c] = y[0:2048]
        y_prompt[2 * c + 1] = y[2048:4096]
        y_sample[c] = y[4096:8192]
    return (y_prompt, y_sample)
```

```python
import contextlib
import math
import numpy as np
import ml_dtypes
import concourse.bass as bass
import concourse.mybir as mybir
from concourse.bass_utils import run_bass_kernel_spmd

F32 = mybir.dt.float32
BF16 = mybir.dt.bfloat16
AF = mybir.ActivationFunctionType
ALU = mybir.AluOpType

D = 1024
DEPTH = 4
NTOK = 8192
SEQS = [(0, 2048), (2048, 2048), (4096, 4096)]
INW = 9728
QA, KA, VA, ZA, QB, KB, VB, ZB, QC, KC, VC, ZC, GL = 0, 512, 1024, 1536, 2048, 2560, 3072, 3584, 4096, 4864, 5632, 6400, 6656
NEG = -30000.0
EPS = 1e-6
DIL = ((128, 1), (512, 4), (2048, 16))
EPOCH = 30000

FM_BLOCKS = []
for i in range(4): FM_BLOCKS.append((QA + 128 * i, 128, False))
for i in range(4): FM_BLOCKS.append((KA + 128 * i, 128, False))
for i in range(4): FM_BLOCKS.append((ZA + 128 * i, 128, True))
for i in range(8): FM_BLOCKS.append((QB + 64 * i, 64, False))
for i in range(8): FM_BLOCKS.append((KB + 64 * i, 64, False))
for i in range(4): FM_BLOCKS.append((ZB + 128 * i, 128, True))
for i in range(6): FM_BLOCKS.append((QC + 128 * i, 128, False))
for i in range(6): FM_BLOCKS.append((KC + 128 * i, 128, False))
for i in range(2): FM_BLOCKS.append((ZC + 128 * i, 128, True))
B_QA, B_KA, B_ZA, B_QB, B_KB, B_ZB, B_QC, B_KC, B_ZC = 0, 4, 8, 12, 20, 28, 32, 38, 44
NB = len(FM_BLOCKS)
TM_GROUPS = [(VA, 512, 0, 0), (VB, 512, 1, 0), (VC, 512, 2, 0), (VC + 512, 256, 2, 512)]


class _Skip(Exception):
    pass


def _maybe_skip(st, cond):
    if cond:
        st.push(lambda et, ev, tb: et is _Skip)
        raise _Skip()


class Tk:
    __slots__ = ("name", "w", "r", "dsem")

    def __init__(self, name):
        self.name = name
        self.w = None
        self.r = {}
        self.dsem = None


class Eng:
    def __init__(self, name, h):
        self.name = name
        self.h = h
        self.count = 0
        self.sems = []
        self.waited = {}


class Sched:
    def __init__(self, nc, es):
        self.nc = nc
        self.es = es
        self.E = {n: Eng(n, h) for n, h in (("pe", nc.tensor), ("act", nc.scalar), ("dve", nc.vector),
                                            ("pool", nc.gpsimd), ("sp", nc.sync))}
        self.nsem = 0
        self.dfree = []
        self.dcount = {}
        self.live_dsem = []
        self.rr = 0

    def new_sem(self, name):
        self.nsem += 1
        assert self.nsem < 140, "too many semaphores"
        return self.es.enter_context(self.nc.semaphore(f"{name}{self.nsem}"))

    def eng_sem(self, ename, idx):
        e = self.E[ename]
        k = (idx - 1) // EPOCH
        while len(e.sems) <= k:
            e.sems.append(self.new_sem("e" + ename))
        return e.sems[k], (idx - 1) % EPOCH + 1

    def wait_tok(self, e, tok):
        if tok is None:
            return
        if tok[0] == "e":
            if tok[1] == "pe" and e.name == "pe":
                return
            sem, val = self.eng_sem(tok[1], tok[2])
        else:
            sem, val = tok[1], tok[2]
        key = id(sem)
        if e.waited.get(key, 0) >= val:
            return
        e.h.wait_ge(sem, val)
        e.waited[key] = val

    def _deps(self, e, reads, writes):
        for t in reads:
            self.wait_tok(e, t.w)
        for t in writes:
            self.wait_tok(e, t.w)
            for tok in t.r.values():
                self.wait_tok(e, tok)

    def _mark(self, tok, reads, writes, rkey):
        for t in reads:
            t.r[rkey] = tok
        for t in writes:
            t.w = tok
            t.r = {}

    def op(self, ename, fn, reads=(), writes=()):
        e = self.E[ename]
        self._deps(e, reads, writes)
        inst = fn(e.h)
        e.count += 1
        sem, val = self.eng_sem(ename, e.count)
        inst.then_inc(sem, 1)
        self._mark(("e", ename, e.count), reads, writes, ename)

    def dma(self, out, in_, reads=(), writes=(), q="sp"):
        e = self.E[q]
        for t in reads:
            self.wait_tok(e, t.w)
        for t in writes:
            if not (t.w is not None and t.w[0] == "d" and t.dsem is not None and t.w[1] is t.dsem[0]):
                self.wait_tok(e, t.w)
            for tok in t.r.values():
                self.wait_tok(e, tok)
        t = writes[0] if writes else reads[0]
        if t.dsem is None:
            if self.dfree:
                t.dsem = self.dfree.pop()
            else:
                t.dsem = [self.new_sem("d"), 0]
            self.live_dsem.append(t.dsem)
        ds = t.dsem
        inst = e.h.dma_start(out=out, in_=in_)
        ds[1] += 16
        inst.then_inc(ds[0], 16)
        tok = ("d", ds[0], ds[1])
        self._mark(tok, reads, writes, id(ds[0]))

    def barrier(self):
        sp = self.E["sp"]
        for n, e in self.E.items():
            if n != "sp" and e.count > 0:
                self.wait_tok(sp, ("e", n, e.count))
        for ds in self.live_dsem:
            self.wait_tok(sp, ("d", ds[0], ds[1]))
        inst = sp.h.nop()
        sp.count += 1
        sem, val = self.eng_sem("sp", sp.count)
        inst.then_inc(sem, 1)
        for n, e in self.E.items():
            if n != "sp":
                self.wait_tok(e, ("e", "sp", sp.count))
        for ds in self.live_dsem:
            if ds[1] < 20000:
                self.dfree.append(ds)
        self.live_dsem = []


class Pipe:
    def __init__(self, la):
        self.la = la
        self.q = []
        self.d = []

    def push(self, front, back):
        front()
        self.q.append(back)
        if len(self.q) > self.la:
            self.q.pop(0)()
        for e in self.d:
            e[0] -= 1
        while self.d and self.d[0][0] <= 0:
            self.d.pop(0)[1]()

    def defer(self, fn, delay):
        self.d.append([delay, fn])

    def flush(self):
        while self.q:
            self.q.pop(0)()
        while self.d:
            self.d.pop(0)[1]()


def build_nc(depth=DEPTH, debug=False, stages="2ABC6"):
    nc = bass.Bass("TRN2", target_bir_lowering=False)
    dt_in = lambda name, shape, dt=F32: nc.dram_tensor(name, list(shape), dt, kind="ExternalInput").ap()
    x_in = dt_in("x", [NTOK, D])
    w_in = dt_in("w_in", [DEPTH, D, INW])
    w_bra = dt_in("w_br_a", [DEPTH, 512, D])
    w_brb = dt_in("w_br_b", [DEPTH, 512, D])
    w_brc = dt_in("w_br_c", [DEPTH, 256, D])
    w_out = dt_in("w_out", [DEPTH, D, D])
    gk_in = dt_in("gk", [DEPTH, 128, 8])
    bg_in = dt_in("bg", [DEPTH, 128, 24])
    gd_in = dt_in("gd", [128, DEPTH])
    lq_in = dt_in("lq", [DEPTH, 256])
    gf_in = dt_in("gf", [D])
    trev_in = dt_in("trev", [DEPTH, 8, 64, 15, 64])
    ident_in = dt_in("ident", [128, 128], BF16)
    qaug_in = dt_in("qaug", [5, 4096], BF16)
    kaug_in = dt_in("kaug", [4, 2, 5, 4096], BF16)
    cdiag_in = dt_in("cdiag", [128, 4, 128])
    dilb_in = dt_in("dilb", [128, 12, 256])
    y_out = nc.dram_tensor("y", [NTOK, D], F32, kind="ExternalOutput").ap()
    skind = "ExternalOutput" if debug else "Internal"
    XS = nc.dram_tensor("xs", [NTOK, D], F32, kind=skind).ap()
    HT = nc.dram_tensor("ht", [8, 128, NTOK], BF16, kind=skind).ap()
    FM = nc.dram_tensor("fm", [NB, 128, NTOK], BF16, kind=skind).ap()
    VT = [nc.dram_tensor("vta", [NTOK, 512], BF16, kind=skind).ap(),
          nc.dram_tensor("vtb", [NTOK, 512], BF16, kind=skind).ap(),
          nc.dram_tensor("vtc", [NTOK, 768], BF16, kind=skind).ap()]
    YG = nc.dram_tensor("yg", [10, 128, NTOK], BF16, kind=skind).ap()

    with contextlib.ExitStack() as es:
        S = Sched(nc, es)
        op, dma = S.op, S.dma

        uid = {"n": 0}

        def sb(st, name, shape, dt):
            uid["n"] += 1
            return st.enter_context(nc.sbuf_tensor(f"s{uid['n']}_{name}", list(shape), dt))

        pbanks = [es.enter_context(nc.psum_tensor(f"pb{i}", [128, 512], F32)) for i in range(7)]
        ptr_tk = Tk("ptr")
        pb_tk = [Tk(f"pb{i}") for i in range(7)]
        pstate = {"i": 0, "ring": list(range(7))}

        def set_ring(r):
            pstate["ring"] = list(r)
            pstate["i"] = 0

        def next_ps():
            ring = pstate["ring"]
            i = ring[pstate["i"] % len(ring)]
            pstate["i"] += 1
            return pbanks[i], pb_tk[i]

        rr = {"i": 0}

        def evac_eng():
            rr["i"] ^= 1
            return "act" if rr["i"] else "dve"

        def copy_op(ename, out, in_, reads, writes):
            if ename == "act":
                op("act", lambda h: h.copy(out=out, in_=in_), reads, writes)
            else:
                op(ename, lambda h: h.tensor_copy(out=out, in_=in_), reads, writes)

        ident = sb(es, "ident", [128, 128], BF16); ident_tk = Tk("ident")
        ones_b = sb(es, "ones_b", [128, 128], BF16); ones_tk = Tk("ones")
        eps_t = sb(es, "eps_t", [128, 1], F32); eps_tk = Tk("eps")
        gk_t = sb(es, "gk_t", [128, DEPTH, 8], F32); gk_tk = Tk("gk")
        bg_t = sb(es, "bg_t", [128, DEPTH, 24], F32); bg_tk = Tk("bg")
        gd_t = sb(es, "gd_t", [128, DEPTH], F32); gd_tk = Tk("gd")
        gdl_t = sb(es, "gdl_t", [128, DEPTH], F32); gdl_tk = Tk("gdl")
        nlam_t = sb(es, "nlam_t", [128, DEPTH], F32); nlam_tk = Tk("nlam")
        lq_t = sb(es, "lq_t", [128, DEPTH, 256], F32); lq_tk = Tk("lq")
        lqp_t = sb(es, "lqp_t", [128, 256], F32); lqp_tk = Tk("lqp")
        lqs_t = sb(es, "lqs_t", [128, 4], F32); lqs_tk = Tk("lqs")
        gf_t = sb(es, "gf_t", [128, D], F32); gf_tk = Tk("gf")
        dma(ident[:], ident_in, writes=[ident_tk])
        dma(gk_t[:], gk_in.rearrange("l p c -> p l c"), writes=[gk_tk])
        dma(bg_t[:], bg_in.rearrange("l p c -> p l c"), writes=[bg_tk])
        dma(gd_t[:], gd_in, writes=[gd_tk])
        dma(gf_t[:], gf_in.partition_broadcast(128), writes=[gf_tk])
        for l in range(DEPTH):
            dma(lq_t[:, l, :], lq_in[l].partition_broadcast(128), writes=[lq_tk])
        op("pool", lambda h: h.memset(ones_b[:], 1.0), writes=[ones_tk])
        op("pool", lambda h: h.memset(eps_t[:], EPS), writes=[eps_tk])
        for l in range(depth):
            lam_init = 0.8 - 0.6 * math.exp(-0.3 * l)
            op("dve", lambda h: h.tensor_tensor(out=lqp_t[:, 0:64], in0=lq_t[:, l, 0:64], in1=lq_t[:, l, 64:128], op=ALU.mult),
               [lq_tk], [lqp_tk])
            op("dve", lambda h: h.tensor_tensor(out=lqp_t[:, 64:128], in0=lq_t[:, l, 128:192], in1=lq_t[:, l, 192:256], op=ALU.mult),
               [lq_tk], [lqp_tk])
            op("dve", lambda h: h.reduce_sum(out=lqs_t[:, 0:1], in_=lqp_t[:, 0:64], axis=mybir.AxisListType.X), [lqp_tk], [lqs_tk])
            op("dve", lambda h: h.reduce_sum(out=lqs_t[:, 1:2], in_=lqp_t[:, 64:128], axis=mybir.AxisListType.X), [lqp_tk], [lqs_tk])
            op("act", lambda h: h.activation(out=lqs_t[:, 2:4], in_=lqs_t[:, 0:2], func=AF.Exp), [lqs_tk], [lqs_tk])
            op("dve", lambda h: h.scalar_tensor_tensor(out=nlam_t[:, l:l + 1], in0=lqs_t[:, 3:4], scalar=-lam_init, in1=lqs_t[:, 2:3],
                                                       op0=ALU.add, op1=ALU.subtract), [lqs_tk], [nlam_tk])
            op("dve", lambda h: h.tensor_scalar(out=gdl_t[:, l:l + 1], in0=gd_t[:, l:l + 1], scalar1=(1.0 - lam_init), scalar2=None,
                                                op0=ALU.mult), [gd_tk], [gdl_tk])
        S.barrier()

        lc_state = {"k": 0}

        def make_stg(st_stack, piece=512):
            return ([sb(st_stack, f"wstg{i}", [128, 8, piece], F32) for i in range(4)], [Tk(f"wstg{i}") for i in range(4)])

        def load_cast(stgs, dst, dst_tk, src3, nk, ncols, gl, piece=512):
            stg, stg_tk = stgs
            k = lc_state["k"]
            for c0 in range(0, ncols, piece):
                w = min(piece, ncols - c0)
                b = k % 4
                dma(stg[b][:, 0:nk, 0:w], src3[:, :, c0:c0 + w], writes=[stg_tk[b]])
                for c in range(nk):
                    en = "act" if (k * nk + c) % 2 else "dve"
                    if gl is not None:
                        if en == "act":
                            op("act", lambda h: h.activation(out=dst[:, c, c0:c0 + w], in_=stg[b][:, c, 0:w], func=AF.Copy, scale=gk_t[:, gl, c:c + 1]),
                               [stg_tk[b], gk_tk], [])
                        else:
                            op("dve", lambda h: h.tensor_scalar(out=dst[:, c, c0:c0 + w], in0=stg[b][:, c, 0:w], scalar1=gk_t[:, gl, c:c + 1],
                                                                scalar2=None, op0=ALU.mult), [stg_tk[b], gk_tk], [])
                    else:
                        if en == "act":
                            op("act", lambda h: h.copy(out=dst[:, c, c0:c0 + w], in_=stg[b][:, c, 0:w]), [stg_tk[b]], [])
                        else:
                            op("dve", lambda h: h.tensor_copy(out=dst[:, c, c0:c0 + w], in_=stg[b][:, c, 0:w]), [stg_tk[b]], [])
                k += 1
            lc_state["k"] = k

        def rms_rstd(st, xap, x_tk, junk, junk_tk, ss, ss_tk, rstd, rstd_tk):
            op("act", lambda h: h.activation(out=junk, in_=xap, func=AF.Square, accum_out=ss), [x_tk], [junk_tk, ss_tk])
            op("act", lambda h: h.activation(out=ss, in_=ss, func=AF.Sqrt, bias=eps_t[:], scale=1.0 / D), [ss_tk, eps_tk], [ss_tk])
            op("dve", lambda h: h.reciprocal(out=rstd, in_=ss), [ss_tk], [rstd_tk])

        for l in range(depth):
            x_src = x_in if l == 0 else XS
            with contextlib.ExitStack() as st:
                _maybe_skip(st, "2" not in stages)
                ptr = st.enter_context(nc.psum_tensor(f"ptr{l}", [128, 1024], BF16))
                W2 = sb(st, "W2", [128, 8, GL], BF16); W2_tk = Tk("W2")
                with contextlib.ExitStack() as st2:
                    stgs = make_stg(st2)
                    load_cast(stgs, W2, W2_tk, w_in[l, :, 0:GL].rearrange("(c p) n -> p c n", p=128), 8, GL, l)
                    S.barrier()
                NX = 8
                xt = [sb(st, f"xt{i}", [128, D], F32) for i in range(NX)]; xt_tk = [Tk(f"xt{i}") for i in range(NX)]
                hb = [sb(st, f"hb{i}", [128, D], BF16) for i in range(2)]; hb_tk = [Tk(f"hb{i}") for i in range(2)]
                junk = sb(st, "junk", [128, D], BF16); junk_tk = Tk("junk")
                ss = [sb(st, f"ss{i}", [128, 1], F32) for i in range(2)]; ss_tk = [Tk(f"ss{i}") for i in range(2)]
                rstd = [sb(st, f"rstd{i}", [128, 1], F32) for i in range(2)]; rstd_tk = [Tk(f"rstd{i}") for i in range(2)]
                hT = [sb(st, f"hT{i}", [128, 8, 512], BF16) for i in range(2)]; hT_tk = [Tk(f"hT{i}") for i in range(2)]
                GS = 8
                fst = [sb(st, f"fst{i}", [128, GS, 512], BF16) for i in range(2)]; fst_tk = [Tk(f"fst{i}") for i in range(2)]
                vst = [sb(st, f"vst{i}", [128, 1792], BF16) for i in range(2)]; vst_tk = [Tk(f"vst{i}") for i in range(2)]
                NT = NTOK // 512
                sub = 0
                fgrp = 0
                vgrp = 0

                def s2_load(T):
                    for s_ in range(4):
                        t0 = T * 512 + s_ * 128
                        xb = (T * 4 + s_) % NX
                        dma(xt[xb][:], x_src[t0:t0 + 128, :], writes=[xt_tk[xb]])

                def s2_front(T):
                    nonlocal sub
                    hb_ = T % 2
                    for s_ in range(4):
                        xb = (T * 4 + s_) % NX
                        b2 = sub % 2
                        sub += 1
                        rms_rstd(st, xt[xb][:], xt_tk[xb], junk[:], junk_tk, ss[b2][:], ss_tk[b2], rstd[b2][:], rstd_tk[b2])
                        op("dve", lambda h: h.tensor_scalar(out=hb[b2][:], in0=xt[xb][:], scalar1=rstd[b2][:], scalar2=None, op0=ALU.mult),
                           [xt_tk[xb], rstd_tk[b2]], [hb_tk[b2]])
                        for c in range(8):
                            op("pe", lambda h: h.transpose(ptr[:, c * 128:(c + 1) * 128], hb[b2][:, c * 128:(c + 1) * 128], ident[:]),
                               [hb_tk[b2], ident_tk], [ptr_tk])
                        copy_op(evac_eng(), hT[hb_][:, :, s_ * 128:(s_ + 1) * 128], ptr[:].rearrange("p (c t) -> p c t", c=8),
                                [ptr_tk], [hT_tk[hb_]])
                    dma(HT[:, :, T * 512:(T + 1) * 512].rearrange("c p t -> p c t"), hT[hb_][:], reads=[hT_tk[hb_]])

                s2_load(0)
                s2_load(1)
                s2_front(0)
                for T in range(NT):
                    hb_ = T % 2
                    for g0 in range(0, NB, GS):
                        if g0 == 2 * GS and T + 1 < NT:
                            s2_front(T + 1)
                            if T + 2 < NT:
                                s2_load(T + 2)
                        nb = min(GS, NB - g0)
                        fb = fgrp % 2
                        fgrp += 1
                        for bi in range(nb):
                            col, M, silu = FM_BLOCKS[g0 + bi]
                            if M == 64 and (g0 + bi) % 2 == 1:
                                continue
                            MM = 128
                            ps, ps_tk = next_ps()
                            for c in range(8):
                                op("pe", lambda h: h.matmul(ps[0:MM, :], lhsT=W2[:, c, col:col + MM], rhs=hT[hb_][:, c, :],
                                                            start=(c == 0), stop=(c == 7)), [W2_tk, hT_tk[hb_]], [ps_tk])
                            if silu:
                                op("act", lambda h: h.activation(out=fst[fb][0:M, bi, :], in_=ps[0:M, :], func=AF.Silu), [ps_tk], [fst_tk[fb]])
                            elif M == 64:
                                copy_op("act", fst[fb][0:64, bi, :], ps[0:64, :], [ps_tk], [fst_tk[fb]])
                                copy_op("dve", fst[fb][0:64, bi + 1, :], ps[64:128, :], [ps_tk], [fst_tk[fb]])
                            else:
                                copy_op(evac_eng(), fst[fb][0:M, bi, :], ps[0:M, :], [ps_tk], [fst_tk[fb]])
                        i0 = 0
                        while i0 < nb:
                            M0 = FM_BLOCKS[g0 + i0][1]
                            i1 = i0
                            while i1 < nb and FM_BLOCKS[g0 + i1][1] == M0:
                                i1 += 1
                            dma(FM[g0 + i0:g0 + i1, 0:M0, T * 512:(T + 1) * 512].rearrange("b p t -> p b t"), fst[fb][0:M0, i0:i1, :], reads=[fst_tk[fb]])
                            i0 = i1
                    for s_ in range(4):
                        vb_ = vgrp % 2
                        vgrp += 1
                        t0 = T * 512 + s_ * 128
                        off = 0
                        for (col, w, ti, dcol) in TM_GROUPS:
                            ps, ps_tk = next_ps()
                            for c in range(8):
                                op("pe", lambda h: h.matmul(ps[:, 0:w], lhsT=hT[hb_][:, c, s_ * 128:(s_ + 1) * 128], rhs=W2[:, c, col:col + w],
                                                            start=(c == 0), stop=(c == 7)), [W2_tk, hT_tk[hb_]], [ps_tk])
                            copy_op(evac_eng(), vst[vb_][:, off:off + w], ps[:, 0:w], [ps_tk], [vst_tk[vb_]])
                            off += w
                        dma(VT[0][t0:t0 + 128, :], vst[vb_][:, 0:512], reads=[vst_tk[vb_]])
                        dma(VT[1][t0:t0 + 128, :], vst[vb_][:, 512:1024], reads=[vst_tk[vb_]])
                        dma(VT[2][t0:t0 + 128, :], vst[vb_][:, 1024:1792], reads=[vst_tk[vb_]])
                S.barrier()

            with contextlib.ExitStack() as st:
                _maybe_skip(st, "A" not in stages)
                pb7 = st.enter_context(nc.psum_tensor(f"pb7_{l}a", [128, 512], F32))
                pbanks[7:] = [pb7]
                pb_tk[7:] = [Tk("pb7")]
                LA = 4
                pipe = Pipe(LA)
                QP = sb(st, "QP", [128, NTOK], BF16); QP_tk = Tk("QP")
                KP = sb(st, "KP", [128, NTOK], BF16); KP_tk = Tk("KP")
                ZP = sb(st, "ZP", [128, NTOK], BF16); ZP_tk = Tk("ZP")
                YP = sb(st, "YP", [128, NTOK], BF16); YP_tk = Tk("YP")
                tdup = [sb(st, f"tdup{i}", [128, 15, 64], F32) for i in range(2)]; tdup_tk = [Tk(f"tdup{i}") for i in range(2)]
                BT = [sb(st, f"BT{i}", [128, 20, 512], F32) for i in range(2)]
                BT_tks = [[Tk(f"BT{i}_{j}") for j in range(20)] for i in range(2)]
                VG = [sb(st, f"VG{i}", [128, 32, 128], BF16) for i in range(2)]; VG_tk = [Tk(f"VG{i}") for i in range(2)]
                NTMP = 4
                tmp = [sb(st, f"tmp{i}", [128, 512], F32) for i in range(NTMP)]; tmp_tk = [Tk(f"tmp{i}") for i in range(NTMP)]
                NPT = LA + 2
                PT = [sb(st, f"PT{i}", [128, 512], BF16) for i in range(NPT)]; PT_tk = [Tk(f"PT{i}") for i in range(NPT)]
                rz = [sb(st, f"rz{i}", [128, 512], F32) for i in range(2)]; rz_tk = [Tk(f"rz{i}") for i in range(2)]
                g2 = [sb(st, f"g2{i}", [128, 512], F32) for i in range(2)]; g2_tk = [Tk(f"g2{i}") for i in range(2)]
                set_ring([3, 4, 5, 6, 7])
                op("pool", lambda h: h.memset(VG[0][:, :, 64:128], 1.0), writes=[VG_tk[0]])
                op("pool", lambda h: h.memset(VG[1][:, :, 0:64], 1.0), writes=[VG_tk[1]])
                tbase = {0: 0, 1: 6, 2: 14}

                def build_bias(head):
                    bb = head % 2
                    td, td_tk = tdup[bb], tdup_tk[bb]
                    dma(td[0:64], trev_in[l, head], writes=[td_tk])
                    dma(td[64:128], trev_in[l, head], writes=[td_tk])
                    for ti_ in range(20):
                        op("pool", lambda h: h.memset(BT[bb][:, ti_, :], NEG), writes=[BT_tks[bb][ti_]])
                    rows_total = 32
                    for v, (R, ch0, nch) in enumerate(((0, 0, 6), (8, 2, 8), (rows_total - 8, (rows_total - 12) // 2, 6))):
                        for j in range(nch):
                            ti = tbase[v] + j
                            for kr in range(2):
                                krow = 2 * (ch0 + j) + kr
                                ent = []
                                for qr in range(8):
                                    qrow = R + qr
                                    rs = min(max(qrow - 4, 0), rows_total - 8)
                                    ent.append(14 - (krow - qrow + 7) if rs <= krow < rs + 8 else None)
                                qr = 0
                                while qr < 8:
                                    if ent[qr] is None:
                                        qr += 1
                                        continue
                                    q1 = qr
                                    while q1 + 1 < 8 and ent[q1 + 1] is not None and ent[q1 + 1] == ent[q1] + 1:
                                        q1 += 1
                                    n = q1 - qr + 1
                                    e0 = ent[qr]
                                    pl = 64 * kr
                                    op("pool", lambda h: h.tensor_copy(out=BT[bb][pl:pl + 64, ti, qr * 64:(qr + n) * 64].rearrange("p (a b) -> p a b", b=64),
                                                                       in_=td[pl:pl + 64, e0:e0 + n, :]), [td_tk], [BT_tks[bb][ti]])
                                    qr = q1 + 1

                acnt = 0
                pcnt = 0
                build_bias(0)
                for i in range(4):
                    dma(QP[:], FM[B_QA + i], writes=[QP_tk])
                    dma(KP[:], FM[B_KA + i], writes=[KP_tk])
                    dma(ZP[:], FM[B_ZA + i], writes=[ZP_tk])
                    for hh in range(2):
                        head = 2 * i + hh
                        bb = head % 2
                        ub = 64 * hh
                        zb = 64 - ub
                        if head + 1 < 8:
                            build_bias(head + 1)
                        vt = VG[hh]
                        vt_tk = VG_tk[hh]
                        vc0 = 0 if hh == 0 else 64
                        for (s0, L) in SEQS:
                            rows = L // 64
                            nqt = rows // 8
                            dma(vt[:, 0:L // 128, vc0:vc0 + 64],
                                VT[0][s0:s0 + L, head * 64:(head + 1) * 64].rearrange("(c p) e -> p c e", p=128), writes=[vt_tk])
                            for qt in range(nqt):
                                if qt == 0:
                                    v, ch0, nch = 0, 0, 6
                                elif qt == nqt - 1:
                                    v, ch0, nch = 2, (rows - 12) // 2, 6
                                else:
                                    v, ch0, nch = 1, 4 * qt - 2, 8
                                ab = acnt % 2
                                a3 = acnt % 3
                                acnt += 1
                                acc, acc_tk = pbanks[a3], pb_tk[a3]
                                q0 = s0 + qt * 512
                                for j in range(nch):
                                    kc = ch0 + j
                                    ti = tbase[v] + j
                                    tb = pcnt % NTMP
                                    pb = pcnt % NPT
                                    pcnt += 1
                                    vq = []
                                    for qr in range(8):
                                        qrow = 8 * qt + qr
                                        rs_ = min(max(qrow - 4, 0), rows - 8)
                                        if any(rs_ <= 2 * kc + kr < rs_ + 8 for kr in range(2)):
                                            vq.append(qr)
                                    assert vq and vq == list(range(vq[0], vq[-1] + 1))
                                    cl, ch_ = 64 * vq[0], 64 * (vq[-1] + 1)

                                    def front(kc=kc, ti=ti, tb=tb, pb=pb, q0=q0, s0=s0, ub=ub, bb=bb, cl=cl, ch_=ch_, j=j, acc=acc, acc_tk=acc_tk):
                                        if j == 0:
                                            op("act", lambda h: h.memzero(acc[:]), [], [acc_tk])
                                        ps, ps_tk = next_ps()
                                        op("pe", lambda h: h.matmul(ps[:, cl:ch_], lhsT=KP[ub:ub + 64, s0 + kc * 128:s0 + (kc + 1) * 128],
                                                                    rhs=QP[ub:ub + 64, q0 + cl:q0 + ch_], start=True, stop=True), [KP_tk, QP_tk], [ps_tk])
                                        op("dve", lambda h: h.scalar_tensor_tensor(out=tmp[tb][:, cl:ch_], in0=ps[:, cl:ch_], scalar=0.125, in1=BT[bb][:, ti, cl:ch_],
                                                                                   op0=ALU.mult, op1=ALU.add), [ps_tk, BT_tks[bb][ti]], [tmp_tk[tb]])
                                        op("act", lambda h: h.activation(out=PT[pb][:, cl:ch_], in_=tmp[tb][:, cl:ch_], func=AF.Exp), [tmp_tk[tb]], [PT_tk[pb]])

                                    def back(kc=kc, pb=pb, j=j, nch=nch, acc=acc, acc_tk=acc_tk, vt=vt, vt_tk=vt_tk, ub=ub, zb=zb, q0=q0, ab=ab, cl=cl, ch_=ch_):
                                        op("pe", lambda h: h.matmul(acc[:, cl:ch_], lhsT=vt[:, kc, :], rhs=PT[pb][:, cl:ch_], start=False, stop=True,
                                                                    skip_group_check=True), [vt_tk, PT_tk[pb]], [acc_tk])
                                        if j == nch - 1:
                                            def post(acc=acc, acc_tk=acc_tk, ub=ub, zb=zb, q0=q0, ab=ab):
                                                op("act", lambda h: h.activation(out=rz[ab][ub:ub + 64, :], in_=acc[zb:zb + 64, :], func=AF.Ln), [acc_tk], [rz_tk[ab]])
                                                op("act", lambda h: h.activation(out=rz[ab][ub:ub + 64, :], in_=rz[ab][ub:ub + 64, :], func=AF.Exp, scale=-1.0),
                                                   [rz_tk[ab]], [rz_tk[ab]])
                                                op("dve", lambda h: h.tensor_tensor(out=g2[ab][ub:ub + 64, :], in0=rz[ab][ub:ub + 64, :], in1=ZP[ub:ub + 64, q0:q0 + 512],
                                                                                    op=ALU.mult), [rz_tk[ab], ZP_tk], [g2_tk[ab]])
                                                op("dve", lambda h: h.tensor_tensor(out=YP[ub:ub + 64, q0:q0 + 512], in0=acc[ub:ub + 64, :], in1=g2[ab][ub:ub + 64, :],
                                                                                    op=ALU.mult), [acc_tk, g2_tk[ab]], [YP_tk])
                                            pipe.defer(post, 3)
                                    pipe.push(front, back)
                            pipe.flush()
                    dma(YG[i], YP[:], reads=[YP_tk])
                set_ring(range(7))
                del pbanks[7:], pb_tk[7:]
                S.barrier()

            with contextlib.ExitStack() as st:
                _maybe_skip(st, "B" not in stages)
                pb7 = st.enter_context(nc.psum_tensor(f"pb7_{l}b", [128, 512], F32))
                pbanks[7:] = [pb7]
                pb_tk[7:] = [Tk("pb7")]
                LA = 3
                pipe = Pipe(LA)
                LM = 4096
                Qa = [[sb(st, f"Qa{m}{i}", [69, LM], BF16) for m in range(2)] for i in range(2)]
                Kp = [[sb(st, f"Kp{m}{i}", [69, LM], BF16) for m in range(2)] for i in range(2)]
                Kn = [[sb(st, f"Kn{m}{i}", [69, LM], BF16) for m in range(2)] for i in range(2)]
                Qa_tk = [[Tk("Qa") for m in range(2)] for i in range(2)]
                Kp_tk = [[Tk("Kp") for m in range(2)] for i in range(2)]
                Kn_tk = [[Tk("Kn") for m in range(2)] for i in range(2)]
                Vb = [sb(st, f"Vb{i}", [128, 32, 128], BF16) for i in range(2)]; Vb_tk = [Tk("Vb") for i in range(2)]
                Zb = [sb(st, f"Zb{i}", [128, LM], BF16) for i in range(2)]; Zb_tk = [Tk("Zb") for i in range(2)]
                Yb = [sb(st, f"Yb{i}", [128, LM], BF16) for i in range(2)]; Yb_tk = [Tk("Yb") for i in range(2)]
                cdg = sb(st, "cdg", [128, 4, 128], F32); cdg_tk = Tk("cdg")
                NPT = LA + 2
                PT = [sb(st, f"PTb{i}", [128, 512], BF16) for i in range(NPT)]; PT_tk = [Tk(f"PTb{i}") for i in range(NPT)]
                r0 = sb(st, "r0", [128, 512], F32); r0_tk = Tk("r0")
                r1 = sb(st, "r1", [128, 512], F32); r1_tk = Tk("r1")
                a0 = sb(st, "a0", [128, 512], F32); a0_tk = Tk("a0")
                a1 = sb(st, "a1", [128, 512], F32); a1_tk = Tk("a1")
                o_t = sb(st, "o_t", [128, 512], F32); o_tk = Tk("o_t")
                sq = sb(st, "sq", [128, 512], BF16); sq_tk = Tk("sq")
                rs_t = sb(st, "rs_t", [128, 512], F32); rs_tk = Tk("rs_t")
                t2 = sb(st, "t2", [128, 512], F32); t2_tk = Tk("t2")
                Pa = [[sb(st, f"Pa{i}{e}", [128, 512], F32) for e in range(2)] for i in range(2)]
                Pa_tk = [[Tk(f"Pa{i}{e}") for e in range(2)] for i in range(2)]
                dma(cdg[:], cdiag_in, writes=[cdg_tk])
                pcnt = 0
                qmcnt = 0
                set_ring([4, 5, 6, 7])
                jobs = [(s0, L, hd) for (s0, L) in SEQS for hd in range(4)]

                def b_load(jn):
                    s0, L, hd = jobs[jn]
                    sb_ = jn % 2
                    for m in range(2):
                        bq = B_QB + 2 * hd + m
                        bk = B_KB + 2 * hd + m
                        dma(Qa[sb_][m][0:64, 0:L], FM[bq, 0:64, s0:s0 + L], writes=[Qa_tk[sb_][m]])
                        dma(Qa[sb_][m][64:69, 0:L], qaug_in[:, 0:L], writes=[Qa_tk[sb_][m]])
                        dma(Kp[sb_][m][0:64, 0:L], FM[bk, 0:64, s0:s0 + L], writes=[Kp_tk[sb_][m]])
                        dma(Kp[sb_][m][64:69, 0:L], kaug_in[hd, 0, :, 0:L], writes=[Kp_tk[sb_][m]])
                        dma(Kn[sb_][m][0:64, 0:L], FM[bk, 0:64, s0:s0 + L], writes=[Kn_tk[sb_][m]])
                        dma(Kn[sb_][m][64:69, 0:L], kaug_in[hd, 1, :, 0:L], writes=[Kn_tk[sb_][m]])
                    dma(Vb[sb_][:, 0:L // 128, :], VT[1][s0:s0 + L, hd * 128:(hd + 1) * 128].rearrange("(c p) e -> p c e", p=128), writes=[Vb_tk[sb_]])
                    dma(Zb[sb_][:, 0:L], FM[B_ZB + hd, :, s0:s0 + L], writes=[Zb_tk[sb_]])

                b_load(0)
                for jn, (s0, L, hd) in enumerate(jobs):
                    sb_ = jn % 2
                    if jn + 1 < len(jobs):
                        b_load(jn + 1)
                    nkc = L // 128
                    nqt = L // 512
                    slope = 2.0 ** (-8.0 * (hd + 1) / 4)
                    for qt in range(nqt):
                        q0 = qt * 512
                        for m in range(2):
                            U, U_tk = pbanks[2 * m], pb_tk[2 * m]
                            Z, Z_tk = pbanks[2 * m + 1], pb_tk[2 * m + 1]
                            kcs = []
                            for kc in range(nkc):
                                k0 = kc * 128
                                if k0 + 128 <= q0:
                                    mind = q0 - (k0 + 127)
                                elif k0 >= q0 + 512:
                                    mind = k0 - (q0 + 511)
                                else:
                                    mind = 0
                                if slope * mind > 60.0:
                                    continue
                                kcs.append(kc)
                            nk_ = len(kcs)
                            assert nk_ > LA + 2
                            pset = qmcnt % 2
                            qmcnt += 1
                            for ji, kc in enumerate(kcs):
                                pb = pcnt % NPT
                                pcnt += 1

                                def front(kc=kc, pb=pb, q0=q0, m=m, hd=hd, sb_=sb_, ji=ji, pset=pset):
                                    k0 = kc * 128
                                    ps, ps_tk = next_ps()
                                    qa_, kp_, kn_ = Qa[sb_][m], Kp[sb_][m], Kn[sb_][m]
                                    qa_tk, kp_tk, kn_tk = Qa_tk[sb_][m], Kp_tk[sb_][m], Kn_tk[sb_][m]
                                    if k0 + 128 <= q0:
                                        op("pe", lambda h: h.matmul(ps[:], lhsT=kp_[:, k0:k0 + 128], rhs=qa_[:, q0:q0 + 512], start=True, stop=True),
                                           [kp_tk, qa_tk], [ps_tk])
                                    elif k0 >= q0 + 512:
                                        op("pe", lambda h: h.matmul(ps[:], lhsT=kn_[:, k0:k0 + 128], rhs=qa_[:, q0:q0 + 512], start=True, stop=True),
                                           [kn_tk, qa_tk], [ps_tk])
                                    else:
                                        jd = (k0 - q0) // 128
                                        if jd > 0:
                                            op("pe", lambda h: h.matmul(ps[:, 0:128 * jd], lhsT=kn_[:, k0:k0 + 128], rhs=qa_[:, q0:q0 + 128 * jd],
                                                                        start=True, stop=True), [kn_tk, qa_tk], [ps_tk])
                                        op("pe", lambda h: h.matmul(ps[:, 128 * jd:512], lhsT=kp_[:, k0:k0 + 128], rhs=qa_[:, q0 + 128 * jd:q0 + 512],
                                                                    start=True, stop=True), [kp_tk, qa_tk], [ps_tk])
                                        op("dve", lambda h: h.tensor_tensor(out=ps[:, 128 * jd:128 * (jd + 1)], in0=ps[:, 128 * jd:128 * (jd + 1)],
                                                                            in1=cdg[:, hd, :], op=ALU.add), [ps_tk, cdg_tk], [ps_tk])
                                    op("act", lambda h: h.activation(out=PT[pb][:], in_=ps[:], func=AF.Exp, scale=0.125), [ps_tk], [PT_tk[pb]])

                                def back(kc=kc, pb=pb, ji=ji, nk_=nk_, U=U, U_tk=U_tk, Z=Z, Z_tk=Z_tk, m=m, q0=q0, sb_=sb_, pset=pset):
                                    op("pe", lambda h: h.matmul(U[:], lhsT=Vb[sb_][:, kc, :], rhs=PT[pb][:], start=(ji == 0), stop=(ji == nk_ - 1)),
                                       [Vb_tk[sb_], PT_tk[pb]], [U_tk])
                                    op("pe", lambda h: h.matmul(Z[:], lhsT=ones_b[:], rhs=PT[pb][:], start=(ji == 0), stop=(ji == nk_ - 1)),
                                       [ones_tk, PT_tk[pb]], [Z_tk])
                                    if ji == nk_ - 1 and m == 0:
                                        op("act", lambda h: h.activation(out=r0[:], in_=Z[:], func=AF.Ln), [Z_tk], [r0_tk])
                                        op("act", lambda h: h.activation(out=r0[:], in_=r0[:], func=AF.Exp, scale=-1.0), [r0_tk], [r0_tk])
                                        op("dve", lambda h: h.tensor_tensor(out=a0[:], in0=U[:], in1=r0[:], op=ALU.mult), [U_tk, r0_tk], [a0_tk])
                                    if ji == nk_ - 1 and m == 1:
                                        op("act", lambda h: h.activation(out=r1[:], in_=Z[:], func=AF.Ln), [Z_tk], [r1_tk])
                                        op("act", lambda h: h.activation(out=r1[:], in_=r1[:], func=AF.Exp, scale=-1.0), [r1_tk], [r1_tk])
                                        op("dve", lambda h: h.tensor_tensor(out=a1[:], in0=U[:], in1=r1[:], op=ALU.mult), [U_tk, r1_tk], [a1_tk])
                                        op("dve", lambda h: h.scalar_tensor_tensor(out=o_t[:], in0=a1[:], scalar=nlam_t[:, l:l + 1], in1=a0[:],
                                                                                   op0=ALU.mult, op1=ALU.add), [a1_tk, a0_tk, nlam_tk], [o_tk])
                                        op("pool", lambda h: h.tensor_tensor(out=sq[:], in0=o_t[:], in1=o_t[:], op=ALU.mult), [o_tk], [sq_tk])

                                        def post2(q0=q0, sb_=sb_):
                                            ssp, ssp_tk = next_ps()
                                            op("pe", lambda h: h.matmul(ssp[:], lhsT=ones_b[:], rhs=sq[:], start=True, stop=True), [ones_tk, sq_tk], [ssp_tk])
                                            op("act", lambda h: h.activation(out=rs_t[:], in_=ssp[:], func=AF.Ln, bias=eps_t[:], scale=1.0 / 128), [ssp_tk, eps_tk], [rs_tk])
                                            op("act", lambda h: h.activation(out=rs_t[:], in_=rs_t[:], func=AF.Exp, scale=-0.5), [rs_tk], [rs_tk])
                                            op("pool", lambda h: h.tensor_tensor(out=t2[:], in0=o_t[:], in1=rs_t[:], op=ALU.mult), [o_tk, rs_tk], [t2_tk])
                                            op("pool", lambda h: h.tensor_tensor(out=t2[:], in0=t2[:], in1=Zb[sb_][:, q0:q0 + 512], op=ALU.mult), [t2_tk, Zb_tk[sb_]], [t2_tk])
                                            op("pool", lambda h: h.tensor_scalar(out=Yb[sb_][:, q0:q0 + 512], in0=t2[:], scalar1=gdl_t[:, l:l + 1], scalar2=None,
                                                                                 op0=ALU.mult), [t2_tk, gdl_tk], [Yb_tk[sb_]])
                                        pipe.defer(post2, 3)
                                pipe.push(front, back)
                    pipe.flush()
                    dma(YG[4 + hd, :, s0:s0 + L], Yb[sb_][:, 0:L], reads=[Yb_tk[sb_]])
                set_ring(range(7))
                del pbanks[7:], pb_tk[7:]
                S.barrier()

            with contextlib.ExitStack() as st:
                _maybe_skip(st, "C" not in stages)
                pb7 = st.enter_context(nc.psum_tensor(f"pb7_{l}c", [128, 512], F32))
                pbanks[7:] = [pb7]
                pb_tk[7:] = [Tk("pb7")]
                LA = 4
                pipe = Pipe(LA)
                LM = 4096
                PAD = 1024
                QD = [sb(st, f"QD{i}", [128, LM], BF16) for i in range(2)]; QD_tk = [Tk(f"QD{i}") for i in range(2)]
                KD = [sb(st, f"KD{i}", [128, LM + 2 * PAD], BF16) for i in range(2)]; KD_tk = [Tk(f"KD{i}") for i in range(2)]
                ZD = [sb(st, f"ZD{i}", [128, LM], BF16) for i in range(2)]; ZD_tk = [Tk(f"ZD{i}") for i in range(2)]
                YD = [sb(st, f"YD{i}", [128, LM], BF16) for i in range(2)]; YD_tk = [Tk(f"YD{i}") for i in range(2)]
                VD = [sb(st, f"VD{i}", [128, 48, 128], BF16) for i in range(2)]; VD_tk = [Tk(f"VD{i}") for i in range(2)]
                AT = [sb(st, f"AT{i}", [128, 2, LM], F32) for i in range(2)]; AT_tk = [Tk(f"AT{i}") for i in range(2)]
                dlb = sb(st, "dlb", [128, 12, 256], F32); dlb_tk = Tk("dlb")
                NTMP = 4
                tmp = [sb(st, f"tmpc{i}", [128, 256], F32) for i in range(NTMP)]; tmp_tk = [Tk(f"tmpc{i}") for i in range(NTMP)]
                NPT = LA + 2
                PT = [sb(st, f"PTc{i}", [128, 256], BF16) for i in range(NPT)]; PT_tk = [Tk(f"PTc{i}") for i in range(NPT)]
                NRC = 3
                rzc = [sb(st, f"rzc{i}", [128, 512], F32) for i in range(NRC)]; rzc_tk = [Tk(f"rzc{i}") for i in range(NRC)]
                dma(dlb[:], dilb_in, writes=[dlb_tk])
                for i in range(2):
                    op("pool", lambda h: h.memset(KD[i][:], 0.0), writes=[KD_tk[i]])
                acc_tks = [pb_tk[3 + i] for i in range(4)]
                set_ring([0, 1, 2, 7])
                pcnt = 0
                rcnt = 0
                cjobs = [(s0, L, p, g) for (s0, L) in SEQS for p in range(2) for g in range(3)]
                hjobs = [(jn, hh) for jn in range(len(cjobs)) for hh in range(2)]

                def c_load(jn):
                    s0, L, p, g = cjobs[jn]
                    lb = jn % 2
                    blk = 2 * g + p
                    dma(QD[lb][:, 0:L], FM[B_QC + blk, :, s0:s0 + L], writes=[QD_tk[lb]])
                    dma(KD[lb][:, PAD:PAD + L], FM[B_KC + blk, :, s0:s0 + L], writes=[KD_tk[lb]])
                    if L < LM:
                        op("pool", lambda h: h.memset(KD[lb][:, PAD + L:PAD + L + PAD], 0.0), writes=[KD_tk[lb]])
                    if g == 0:
                        zi = (jn // 3) % 2
                        dma(ZD[zi][:, 0:L], FM[B_ZC + p, :, s0:s0 + L], writes=[ZD_tk[zi]])

                def v_prep(hn):
                    jn, hh = hjobs[hn]
                    s0, L, p, g = cjobs[jn]
                    d = DIL[g][1]
                    M = L // d
                    nch = M // 128 + 1
                    hc = 4 * g + 2 * p + hh
                    vc0 = 0 if hh == 0 else 64
                    oc0 = 64 - vc0
                    vt, vt_tk = VD[hn % 2], VD_tk[hn % 2]
                    vt4 = vt[:, 0:d * nch, :].rearrange("p (r j) e -> p r j e", r=d)
                    op("pool", lambda h: h.memset(vt[:, 0:d * nch, :], 0.0), writes=[vt_tk])
                    if nch > 2:
                        op("pool", lambda h: h.memset(vt4[:, :, 1:nch - 1, oc0:oc0 + 64], 1.0), writes=[vt_tk])
                    op("pool", lambda h: h.memset(vt4[64:128, :, 0, oc0:oc0 + 64], 1.0), writes=[vt_tk])
                    op("pool", lambda h: h.memset(vt4[0:64, :, nch - 1, oc0:oc0 + 64], 1.0), writes=[vt_tk])
                    vsrc = VT[2][s0:s0 + L, hc * 64:(hc + 1) * 64]
                    for r in range(d):
                        if nch > 2:
                            src = vsrc[64 * d:64 * d + 128 * (nch - 2) * d, :].rearrange("(j k r) e -> r k j e", k=128, r=d)[r]
                            dma(vt4[:, r, 1:nch - 1, vc0:vc0 + 64], src, writes=[vt_tk])
                        src = vsrc[0:64 * d, :].rearrange("(k r) e -> r k e", r=d)[r]
                        dma(vt4[64:128, r, 0, vc0:vc0 + 64], src, writes=[vt_tk])
                        src = vsrc[(M - 64) * d:M * d, :].rearrange("(k r) e -> r k e", r=d)[r]
                        dma(vt4[0:64, r, nch - 1, vc0:vc0 + 64], src, writes=[vt_tk])

                c_load(0)
                v_prep(0)
                for hn, (jn, hh) in enumerate(hjobs):
                    s0, L, p, g = cjobs[jn]
                    d = DIL[g][1]
                    M = L // d
                    nsub = M // 128
                    nch = nsub + 1
                    lb = jn % 2
                    ai = (jn // 3) % 2
                    if hh == 0 and jn + 1 < len(cjobs):
                        c_load(jn + 1)
                    hc = 4 * g + 2 * p + hh
                    ub = 64 * hh
                    vt, vt_tk = VD[hn % 2], VD_tk[hn % 2]
                    npush = 0
                    for r in range(d):
                        for j in range(nch):
                            if npush == LA + 1 and hn + 1 < len(hjobs):
                                v_prep(hn + 1)
                            npush += 1
                            tb = pcnt % NTMP
                            pb = pcnt % NPT
                            pcnt += 1

                            def front(r=r, j=j, d=d, nsub=nsub, lb=lb, ub=ub, hc=hc, tb=tb, pb=pb):
                                kstart = PAD + (128 * j - 64) * d + r
                                qlo = max(j - 1, 0)
                                c_lo = 0 if j - 1 >= 0 else 128
                                c_hi = 256 if j <= nsub - 1 else 128
                                ps, ps_tk = next_ps()
                                qa0 = (128 * qlo) * d + r
                                nq = c_hi - c_lo
                                op("pe", lambda h: h.matmul(ps[:, c_lo:c_hi], lhsT=KD[lb][ub:ub + 64, kstart:kstart + 127 * d + 1:d],
                                                            rhs=QD[lb][ub:ub + 64, qa0:qa0 + (nq - 1) * d + 1:d], start=True, stop=True),
                                   [KD_tk[lb], QD_tk[lb]], [ps_tk])
                                op("dve", lambda h: h.scalar_tensor_tensor(out=tmp[tb][:, c_lo:c_hi], in0=ps[:, c_lo:c_hi], scalar=0.125,
                                                                           in1=dlb[:, hc, c_lo:c_hi], op0=ALU.mult, op1=ALU.add),
                                   [ps_tk, dlb_tk], [tmp_tk[tb]])
                                op("act", lambda h: h.activation(out=PT[pb][:, c_lo:c_hi], in_=tmp[tb][:, c_lo:c_hi], func=AF.Exp),
                                   [tmp_tk[tb]], [PT_tk[pb]])

                            def back(r=r, j=j, d=d, nsub=nsub, nch=nch, pb=pb, vt=vt, vt_tk=vt_tk, hh=hh, g=g, ai=ai):
                                if j - 1 >= 0:
                                    sl = (j - 1) % 4
                                    op("pe", lambda h: h.matmul(pbanks[3 + sl][:, 0:128], lhsT=vt[:, r * nch + j, :], rhs=PT[pb][:, 0:128],
                                                                start=False, stop=True), [vt_tk, PT_tk[pb]], [acc_tks[sl]])
                                    a0_ = (128 * (j - 1)) * d + r
                                    dst = AT[ai][:, hh, a0_:a0_ + 127 * d + 1:d]
                                    if g == 0:
                                        op("act", lambda h: h.copy(out=dst, in_=pbanks[3 + sl][:, 0:128]), [acc_tks[sl]], [AT_tk[ai]])
                                    else:
                                        op("dve", lambda h: h.tensor_tensor(out=dst, in0=pbanks[3 + sl][:, 0:128], in1=dst, op=ALU.add),
                                           [acc_tks[sl], AT_tk[ai]], [AT_tk[ai]])
                                if j <= nsub - 1:
                                    sl = j % 4
                                    op("pe", lambda h: h.matmul(pbanks[3 + sl][:, 0:128], lhsT=vt[:, r * nch + j, :], rhs=PT[pb][:, 128:256],
                                                                start=True, stop=False), [vt_tk, PT_tk[pb]], [acc_tks[sl]])
                            pipe.push(front, back)
                    if g == 2 and hh == 1:
                        pipe.flush()
                        last_job = (hn == len(hjobs) - 1)
                        kk = 0
                        for hh2 in range(2):
                            for c0 in range(0, L, 512):
                                ri = rcnt % NRC
                                rcnt += 1

                                def piece(hh2=hh2, c0=c0, ri=ri, ai=ai):
                                    ub2 = 64 * hh2
                                    zb2 = 64 - ub2
                                    op("act", lambda h: h.activation(out=rzc[ri][ub2:ub2 + 64, :], in_=AT[ai][zb2:zb2 + 64, hh2, c0:c0 + 512], func=AF.Ln),
                                       [AT_tk[ai]], [rzc_tk[ri]])
                                    op("act", lambda h: h.activation(out=rzc[ri][ub2:ub2 + 64, :], in_=rzc[ri][ub2:ub2 + 64, :], func=AF.Exp, scale=-1.0),
                                       [rzc_tk[ri]], [rzc_tk[ri]])
                                    op("pool", lambda h: h.tensor_tensor(out=rzc[ri][ub2:ub2 + 64, :], in0=rzc[ri][ub2:ub2 + 64, :], in1=ZD[ai][ub2:ub2 + 64, c0:c0 + 512],
                                                                         op=ALU.mult), [rzc_tk[ri], ZD_tk[ai]], [rzc_tk[ri]])
                                    op("pool", lambda h: h.tensor_tensor(out=YD[ai][ub2:ub2 + 64, c0:c0 + 512], in0=AT[ai][ub2:ub2 + 64, hh2, c0:c0 + 512],
                                                                         in1=rzc[ri][ub2:ub2 + 64, :], op=ALU.mult), [AT_tk[ai], rzc_tk[ri]], [YD_tk[ai]])
                                if last_job:
                                    piece()
                                else:
                                    pipe.defer(piece, 2 + 2 * kk)
                                kk += 1

                        def store(p=p, s0=s0, L=L, ai=ai):
                            dma(YG[8 + p, :, s0:s0 + L], YD[ai][:, 0:L], reads=[YD_tk[ai]])
                        if last_job:
                            store()
                        else:
                            pipe.defer(store, 2 + 2 * kk + 6)
                pipe.flush()
                set_ring(range(7))
                del pbanks[7:], pb_tk[7:]
                S.barrier()

            with contextlib.ExitStack() as st:
                _maybe_skip(st, "6" not in stages)
                WG = sb(st, "WG", [128, 8, 3072], BF16); WG_tk = Tk("WG")
                WBR = sb(st, "WBR", [128, 10, D], BF16); WBR_tk = Tk("WBR")
                WO = sb(st, "WO", [128, 8, D], BF16); WO_tk = Tk("WO")
                with contextlib.ExitStack() as st2:
                    stgs = make_stg(st2)
                    load_cast(stgs, WG, WG_tk, w_in[l, :, GL:INW].rearrange("(c p) n -> p c n", p=128), 8, 3072, l)
                    load_cast(stgs, WBR[:, 0:4], WBR_tk, w_bra[l].rearrange("(c p) n -> p c n", p=128), 4, D, None)
                    load_cast(stgs, WBR[:, 4:8], WBR_tk, w_brb[l].rearrange("(c p) n -> p c n", p=128), 4, D, None)
                    load_cast(stgs, WBR[:, 8:10], WBR_tk, w_brc[l].rearrange("(c p) n -> p c n", p=128), 2, D, None)
                    load_cast(stgs, WO, WO_tk, w_out[l].rearrange("(c p) n -> p c n", p=128), 8, D, None)
                    S.barrier()
                hT = [sb(st, f"hT6{i}", [128, 8, 512], BF16) for i in range(2)]; hT_tk = [Tk(f"hT6{i}") for i in range(2)]
                yg = [sb(st, f"yg6{i}", [128, 10, 512], BF16) for i in range(2)]; yg_tk = [Tk(f"yg6{i}") for i in range(2)]
                xt = [sb(st, f"x6{i}", [128, D], F32) for i in range(8)]; xt_tk = [Tk(f"x6{i}") for i in range(8)]
                xo = [sb(st, f"xo6{i}", [128, D], F32) for i in range(2)]; xo_tk = [Tk(f"xo6{i}") for i in range(2)]
                gt = [sb(st, f"gt6{i}", [128, 512], F32) for i in range(3)]; gt_tk = [Tk(f"gt6{i}") for i in range(3)]
                mt = [sb(st, f"mt6{i}", [128, 512], F32) for i in range(3)]; mt_tk = [Tk(f"mt6{i}") for i in range(3)]
                mg = [sb(st, f"mg6{i}", [128, 8, 512], BF16) for i in range(2)]; mg_tk = [Tk(f"mg6{i}") for i in range(2)]
                junk = sb(st, "junk6", [128, D], BF16); junk_tk = Tk("junk6")
                ss = [sb(st, f"ss6{i}", [128, 1], F32) for i in range(2)]; ss_tk = [Tk(f"ss6{i}") for i in range(2)]
                rstd = [sb(st, f"rstd6{i}", [128, 1], F32) for i in range(2)]; rstd_tk = [Tk(f"rstd6{i}") for i in range(2)]
                NT = NTOK // 512
                KBR = [(0, 4), (4, 4), (8, 2)]
                xcnt = 0
                ocnt = 0

                def s6_load(T):
                    b = T % 2
                    dma(hT[b][:], HT[:, :, T * 512:(T + 1) * 512].rearrange("c p t -> p c t"), writes=[hT_tk[b]])
                    dma(yg[b][:], YG[:, :, T * 512:(T + 1) * 512].rearrange("c p t -> p c t"), writes=[yg_tk[b]])

                s6_load(0)
                for T in range(NT):
                    b = T % 2
                    if T + 1 < NT:
                        s6_load(T + 1)
                    for s_ in range(4):
                        t0 = T * 512 + s_ * 128
                        dma(xt[(4 * T + s_) % 8][:], x_src[t0:t0 + 128, :], writes=[xt_tk[(4 * T + s_) % 8]])
                    for c in range(8):
                        for br in range(3):
                            gps, gps_tk = next_ps()
                            for k in range(8):
                                op("pe", lambda h: h.matmul(gps[:], lhsT=WG[:, k, br * D + c * 128:br * D + (c + 1) * 128], rhs=hT[b][:, k, :],
                                                            start=(k == 0), stop=(k == 7)), [WG_tk, hT_tk[b]], [gps_tk])
                            op("act", lambda h: h.activation(out=gt[br][:], in_=gps[:], func=AF.Sigmoid, bias=bg_t[:, l, br * 8 + c:br * 8 + c + 1]),
                               [gps_tk, bg_tk], [gt_tk[br]])
                            pps, pps_tk = next_ps()
                            k0, nk = KBR[br]
                            for k in range(nk):
                                op("pe", lambda h: h.matmul(pps[:], lhsT=WBR[:, k0 + k, c * 128:(c + 1) * 128], rhs=yg[b][:, k0 + k, :],
                                                            start=(k == 0), stop=(k == nk - 1)), [WBR_tk, yg_tk[b]], [pps_tk])
                            op("dve", lambda h: h.tensor_tensor(out=mt[br][:], in0=pps[:], in1=gt[br][:], op=ALU.mult), [pps_tk, gt_tk[br]], [mt_tk[br]])
                        op("pool", lambda h: h.tensor_tensor(out=mt[0][:], in0=mt[0][:], in1=mt[1][:], op=ALU.add), [mt_tk[0], mt_tk[1]], [mt_tk[0]])
                        op("pool", lambda h: h.tensor_tensor(out=mg[b][:, c, :], in0=mt[0][:], in1=mt[2][:], op=ALU.add), [mt_tk[0], mt_tk[2]], [mg_tk[b]])
                    for s_ in range(4):
                        t0 = T * 512 + s_ * 128
                        xb = (4 * T + s_) % 8
                        ob = ocnt % 2
                        ocnt += 1
                        for half in range(2):
                            ops_, ops_tk = next_ps()
                            for k in range(8):
                                op("pe", lambda h: h.matmul(ops_[:], lhsT=mg[b][:, k, s_ * 128:(s_ + 1) * 128], rhs=WO[:, k, half * 512:(half + 1) * 512],
                                                            start=(k == 0), stop=(k == 7)), [mg_tk[b], WO_tk], [ops_tk])
                            op("dve", lambda h: h.tensor_tensor(out=xo[ob][:, half * 512:(half + 1) * 512], in0=ops_[:], in1=xt[xb][:, half * 512:(half + 1) * 512],
                                                                op=ALU.add), [ops_tk, xt_tk[xb]], [xo_tk[ob]])
                        if l < depth - 1:
                            dma(XS[t0:t0 + 128, :], xo[ob][:], reads=[xo_tk[ob]])
                        else:
                            b2 = ocnt % 2
                            rms_rstd(st, xo[ob][:], xo_tk[ob], junk[:], junk_tk, ss[b2][:], ss_tk[b2], rstd[b2][:], rstd_tk[b2])
                            op("dve", lambda h: h.scalar_tensor_tensor(out=xo[ob][:], in0=xo[ob][:], scalar=rstd[b2][:], in1=gf_t[:],
                                                                        op0=ALU.mult, op1=ALU.mult), [xo_tk[ob], rstd_tk[b2], gf_tk], [xo_tk[ob]])
                            dma(y_out[t0:t0 + 128, :], xo[ob][:], reads=[xo_tk[ob]])
                S.barrier()
        S.barrier()
        import sys as _sys
        print("[build] nsem", S.nsem, {n: e.count for n, e in S.E.items()}, file=_sys.stderr)
    return nc


def _consts():
    bf = ml_dtypes.bfloat16
    ident = np.eye(128, dtype=np.float32).astype(bf)
    pos = np.arange(4096)
    qaug = np.zeros((5, 4096), np.float32)
    qaug[0] = (pos % 512) // 16
    qaug[1] = pos % 16
    qaug[2] = (pos // 512) * 512
    qaug[3] = 1.0
    qaug[4] = 1.0
    kaug = np.zeros((4, 2, 5, 4096), np.float32)
    for h in range(4):
        s = 2.0 ** (-8.0 * (h + 1) / 4)
        for si, sg in enumerate((1.0, -1.0)):
            kaug[h, si, 0] = -16.0 * s * 8.0 * sg
            kaug[h, si, 1] = -s * 8.0 * sg
            kaug[h, si, 2] = -s * 8.0 * sg
            kaug[h, si, 3] = sg * s * 8.0 * (pos % 128)
            kaug[h, si, 4] = sg * s * 8.0 * ((pos // 128) * 128)
    cdiag = np.zeros((128, 4, 128), np.float32)
    kk = np.arange(128)[:, None]
    qq = np.arange(128)[None, :]
    for h in range(4):
        s = 2.0 ** (-8.0 * (h + 1) / 4)
        cdiag[:, h, :] = np.where(qq < kk, 16.0 * s * (qq - kk), 0.0)
    dilb = np.zeros((128, 12, 256), np.float32)
    kk = np.arange(128)[:, None]
    qq = np.arange(256)[None, :]
    rel = qq - kk - 64
    for hc in range(12):
        s = np.float32(2.0 ** (-8.0 * (hc + 1) / 12))
        d = DIL[hc // 4][1]
        dilb[:, hc, :] = np.where(np.abs(rel) <= 64, -(s * np.float32(d) * np.abs(rel).astype(np.float32)), NEG)
    return dict(ident=ident, qaug=qaug.astype(bf), kaug=kaug.astype(bf), cdiag=cdiag, dilb=dilb)


def _trev(rpb):
    kc = np.arange(64)[:, None]
    qc = np.arange(64)[None, :]
    qstart = np.clip(qc - 8, 0, 48)
    ok = (kc >= qstart) & (kc < qstart + 16)
    cidx = np.clip(kc - qc + 15, 0, 30)
    t = rpb[:, :, :, cidx]
    t = np.where(ok[None, None, None], t, np.float32(NEG))
    t = t[:, :, ::-1]
    return np.ascontiguousarray(np.transpose(t, (0, 1, 3, 2, 4))).astype(np.float32)


_NC_CACHE = {}


def kernel(x_prompt, x_sample, g_norm, w_in, b_gate, rpb, lam_qk, g_diff, w_br_a, w_br_b, w_br_c, w_out, g_final, _depth=DEPTH, _debug=False, _stages="2ABC6", _ncores=8, _trace=False):
    f32 = lambda a: np.ascontiguousarray(np.asarray(a, dtype=np.float32))
    x_prompt, x_sample = f32(x_prompt), f32(x_sample)
    shared = dict(
        w_in=f32(w_in), w_br_a=f32(w_br_a), w_br_b=f32(w_br_b), w_br_c=f32(w_br_c), w_out=f32(w_out),
        gk=f32(np.transpose(np.asarray(g_norm).reshape(DEPTH, 8, 128), (0, 2, 1))),
        bg=f32(np.transpose(np.asarray(b_gate).reshape(DEPTH, 24, 128), (0, 2, 1))),
        gd=f32(np.asarray(g_diff).reshape(DEPTH, 128).T),
        lq=f32(np.asarray(lam_qk).reshape(DEPTH, 256)),
        gf=f32(g_final),
        trev=_trev(f32(rpb)),
    )
    shared.update(_consts())
    key = (_depth, _debug, _stages)
    if key not in _NC_CACHE:
        _NC_CACHE[key] = build_nc(_depth, _debug, _stages)
    nc = _NC_CACHE[key]
    in_maps = []
    for c in range(_ncores):
        xc = np.concatenate([x_prompt[2 * c], x_prompt[2 * c + 1], x_sample[c]], axis=0)
        m = dict(shared)
        m["x"] = np.ascontiguousarray(xc)
        in_maps.append(m)
    res = run_bass_kernel_spmd(nc, in_maps, core_ids=list(range(_ncores)), **({'trace': True} if _trace else {}))
    if _debug or _trace:
        return res
    y_prompt = np.empty((16, 2048, D), np.float32)
    y_sample = np.empty((8, 4096, D), np.float32)
    for c in range(8):
        y = res.results[c]["y"]
        y_prompt[2 * c] = y[0:2048]
        y_prompt[2 * c + 1] = y[2048:4096]
        y_sample[c] = y[4096:8192]
    return (y_prompt, y_sample)
```

```python
import contextlib
import math
import numpy as np
import ml_dtypes
import concourse.bass as bass
import concourse.mybir as mybir
from concourse.bass_utils import run_bass_kernel_spmd

F32 = mybir.dt.float32
BF16 = mybir.dt.bfloat16
AF = mybir.ActivationFunctionType
ALU = mybir.AluOpType

D = 1024
DEPTH = 4
NTOK = 8192
SEQS = [(0, 2048), (2048, 2048), (4096, 4096)]
INW = 9728
QA, KA, VA, ZA, QB, KB, VB, ZB, QC, KC, VC, ZC, GL = 0, 512, 1024, 1536, 2048, 2560, 3072, 3584, 4096, 4864, 5632, 6400, 6656
NEG = -30000.0
EPS = 1e-6
DIL = ((128, 1), (512, 4), (2048, 16))
EPOCH = 30000

FM_BLOCKS = []
for i in range(4): FM_BLOCKS.append((QA + 128 * i, 128, False))
for i in range(4): FM_BLOCKS.append((KA + 128 * i, 128, False))
for i in range(4): FM_BLOCKS.append((ZA + 128 * i, 128, True))
for i in range(8): FM_BLOCKS.append((QB + 64 * i, 64, False))
for i in range(8): FM_BLOCKS.append((KB + 64 * i, 64, False))
for i in range(4): FM_BLOCKS.append((ZB + 128 * i, 128, True))
for i in range(6): FM_BLOCKS.append((QC + 128 * i, 128, False))
for i in range(6): FM_BLOCKS.append((KC + 128 * i, 128, False))
for i in range(2): FM_BLOCKS.append((ZC + 128 * i, 128, True))
B_QA, B_KA, B_ZA, B_QB, B_KB, B_ZB, B_QC, B_KC, B_ZC = 0, 4, 8, 12, 20, 28, 32, 38, 44
NB = len(FM_BLOCKS)
TM_GROUPS = [(VA, 512, 0, 0), (VB, 512, 1, 0), (VC, 512, 2, 0), (VC + 512, 256, 2, 512)]


class _Skip(Exception):
    pass


def _maybe_skip(st, cond):
    if cond:
        st.push(lambda et, ev, tb: et is _Skip)
        raise _Skip()


class Tk:
    __slots__ = ("name", "w", "r", "dsem")

    def __init__(self, name):
        self.name = name
        self.w = None
        self.r = {}
        self.dsem = None


class Eng:
    def __init__(self, name, h):
        self.name = name
        self.h = h
        self.count = 0
        self.sems = []
        self.waited = {}


class Sched:
    def __init__(self, nc, es):
        self.nc = nc
        self.es = es
        self.E = {n: Eng(n, h) for n, h in (("pe", nc.tensor), ("act", nc.scalar), ("dve", nc.vector),
                                            ("pool", nc.gpsimd), ("sp", nc.sync))}
        self.nsem = 0
        self.dfree = []
        self.dcount = {}
        self.live_dsem = []
        self.rr = 0

    def new_sem(self, name):
        self.nsem += 1
        assert self.nsem < 140, "too many semaphores"
        return self.es.enter_context(self.nc.semaphore(f"{name}{self.nsem}"))

    def eng_sem(self, ename, idx):
        e = self.E[ename]
        k = (idx - 1) // EPOCH
        while len(e.sems) <= k:
            e.sems.append(self.new_sem("e" + ename))
        return e.sems[k], (idx - 1) % EPOCH + 1

    def wait_tok(self, e, tok):
        if tok is None:
            return
        if tok[0] == "e":
            if tok[1] == "pe" and e.name == "pe":
                return
            sem, val = self.eng_sem(tok[1], tok[2])
        else:
            sem, val = tok[1], tok[2]
        key = id(sem)
        if e.waited.get(key, 0) >= val:
            return
        e.h.wait_ge(sem, val)
        e.waited[key] = val

    def _deps(self, e, reads, writes):
        for t in reads:
            self.wait_tok(e, t.w)
        for t in writes:
            self.wait_tok(e, t.w)
            for tok in t.r.values():
                self.wait_tok(e, tok)

    def _mark(self, tok, reads, writes, rkey):
        for t in reads:
            t.r[rkey] = tok
        for t in writes:
            t.w = tok
            t.r = {}

    def op(self, ename, fn, reads=(), writes=()):
        e = self.E[ename]
        self._deps(e, reads, writes)
        inst = fn(e.h)
        e.count += 1
        sem, val = self.eng_sem(ename, e.count)
        inst.then_inc(sem, 1)
        self._mark(("e", ename, e.count), reads, writes, ename)

    def dma(self, out, in_, reads=(), writes=(), q="sp"):
        e = self.E[q]
        for t in reads:
            self.wait_tok(e, t.w)
        for t in writes:
            if not (t.w is not None and t.w[0] == "d" and t.dsem is not None and t.w[1] is t.dsem[0]):
                self.wait_tok(e, t.w)
            for tok in t.r.values():
                self.wait_tok(e, tok)
        t = writes[0] if writes else reads[0]
        if t.dsem is None:
            if self.dfree:
                t.dsem = self.dfree.pop()
            else:
                t.dsem = [self.new_sem("d"), 0]
            self.live_dsem.append(t.dsem)
        ds = t.dsem
        inst = e.h.dma_start(out=out, in_=in_)
        ds[1] += 16
        inst.then_inc(ds[0], 16)
        tok = ("d", ds[0], ds[1])
        self._mark(tok, reads, writes, id(ds[0]))

    def barrier(self):
        sp = self.E["sp"]
        for n, e in self.E.items():
            if n != "sp" and e.count > 0:
                self.wait_tok(sp, ("e", n, e.count))
        for ds in self.live_dsem:
            self.wait_tok(sp, ("d", ds[0], ds[1]))
        inst = sp.h.nop()
        sp.count += 1
        sem, val = self.eng_sem("sp", sp.count)
        inst.then_inc(sem, 1)
        for n, e in self.E.items():
            if n != "sp":
                self.wait_tok(e, ("e", "sp", sp.count))
        for ds in self.live_dsem:
            if ds[1] < 20000:
                self.dfree.append(ds)
        self.live_dsem = []


class Pipe:
    def __init__(self, la):
        self.la = la
        self.q = []
        self.d = []

    def push(self, front, back):
        front()
        self.q.append(back)
        if len(self.q) > self.la:
            self.q.pop(0)()
        for e in self.d:
            e[0] -= 1
        while self.d and self.d[0][0] <= 0:
            self.d.pop(0)[1]()

    def defer(self, fn, delay):
        self.d.append([delay, fn])

    def flush(self):
        while self.q:
            self.q.pop(0)()
        while self.d:
            self.d.pop(0)[1]()


def build_nc(depth=DEPTH, debug=False, stages="2ABC6"):
    nc = bass.Bass("TRN2", target_bir_lowering=False)
    dt_in = lambda name, shape, dt=F32: nc.dram_tensor(name, list(shape), dt, kind="ExternalInput").ap()
    x_in = dt_in("x", [NTOK, D])
    w_in = dt_in("w_in", [DEPTH, D, INW])
    w_bra = dt_in("w_br_a", [DEPTH, 512, D])
    w_brb = dt_in("w_br_b", [DEPTH, 512, D])
    w_brc = dt_in("w_br_c", [DEPTH, 256, D])
    w_out = dt_in("w_out", [DEPTH, D, D])
    gk_in = dt_in("gk", [DEPTH, 128, 8])
    bg_in = dt_in("bg", [DEPTH, 128, 24])
    gd_in = dt_in("gd", [128, DEPTH])
    lq_in = dt_in("lq", [DEPTH, 256])
    gf_in = dt_in("gf", [D])
    trev_in = dt_in("trev", [DEPTH, 8, 64, 15, 64])
    ident_in = dt_in("ident", [128, 128], BF16)
    qaug_in = dt_in("qaug", [5, 4096], BF16)
    kaug_in = dt_in("kaug", [4, 2, 5, 4096], BF16)
    cdiag_in = dt_in("cdiag", [128, 4, 128])
    dilb_in = dt_in("dilb", [128, 12, 256])
    y_out = nc.dram_tensor("y", [NTOK, D], F32, kind="ExternalOutput").ap()
    skind = "ExternalOutput" if debug else "Internal"
    XS = nc.dram_tensor("xs", [NTOK, D], F32, kind=skind).ap()
    HT = nc.dram_tensor("ht", [8, 128, NTOK], BF16, kind=skind).ap()
    FM = nc.dram_tensor("fm", [NB, 128, NTOK], BF16, kind=skind).ap()
    VT = [nc.dram_tensor("vta", [NTOK, 512], BF16, kind=skind).ap(),
          nc.dram_tensor("vtb", [NTOK, 512], BF16, kind=skind).ap(),
          nc.dram_tensor("vtc", [NTOK, 768], BF16, kind=skind).ap()]
    YG = nc.dram_tensor("yg", [10, 128, NTOK], BF16, kind=skind).ap()

    with contextlib.ExitStack() as es:
        S = Sched(nc, es)
        op, dma = S.op, S.dma

        uid = {"n": 0}

        def sb(st, name, shape, dt):
            uid["n"] += 1
            return st.enter_context(nc.sbuf_tensor(f"s{uid['n']}_{name}", list(shape), dt))

        pbanks = [es.enter_context(nc.psum_tensor(f"pb{i}", [128, 512], F32)) for i in range(7)]
        ptr_tk = Tk("ptr")
        pb_tk = [Tk(f"pb{i}") for i in range(7)]
        pstate = {"i": 0, "ring": list(range(7))}

        def set_ring(r):
            pstate["ring"] = list(r)
            pstate["i"] = 0

        def next_ps():
            ring = pstate["ring"]
            i = ring[pstate["i"] % len(ring)]
            pstate["i"] += 1
            return pbanks[i], pb_tk[i]

        rr = {"i": 0}

        def evac_eng():
            rr["i"] ^= 1
            return "act" if rr["i"] else "dve"

        def copy_op(ename, out, in_, reads, writes):
            if ename == "act":
                op("act", lambda h: h.copy(out=out, in_=in_), reads, writes)
            else:
                op(ename, lambda h: h.tensor_copy(out=out, in_=in_), reads, writes)

        ident = sb(es, "ident", [128, 128], BF16); ident_tk = Tk("ident")
        ones_b = sb(es, "ones_b", [128, 128], BF16); ones_tk = Tk("ones")
        eps_t = sb(es, "eps_t", [128, 1], F32); eps_tk = Tk("eps")
        gk_t = sb(es, "gk_t", [128, DEPTH, 8], F32); gk_tk = Tk("gk")
        bg_t = sb(es, "bg_t", [128, DEPTH, 24], F32); bg_tk = Tk("bg")
        gd_t = sb(es, "gd_t", [128, DEPTH], F32); gd_tk = Tk("gd")
        gdl_t = sb(es, "gdl_t", [128, DEPTH], F32); gdl_tk = Tk("gdl")
        nlam_t = sb(es, "nlam_t", [128, DEPTH], F32); nlam_tk = Tk("nlam")
        lq_t = sb(es, "lq_t", [128, DEPTH, 256], F32); lq_tk = Tk("lq")
        lqp_t = sb(es, "lqp_t", [128, 256], F32); lqp_tk = Tk("lqp")
        lqs_t = sb(es, "lqs_t", [128, 4], F32); lqs_tk = Tk("lqs")
        gf_t = sb(es, "gf_t", [128, D], F32); gf_tk = Tk("gf")
        dma(ident[:], ident_in, writes=[ident_tk])
        dma(gk_t[:], gk_in.rearrange("l p c -> p l c"), writes=[gk_tk])
        dma(bg_t[:], bg_in.rearrange("l p c -> p l c"), writes=[bg_tk])
        dma(gd_t[:], gd_in, writes=[gd_tk])
        dma(gf_t[:], gf_in.partition_broadcast(128), writes=[gf_tk])
        for l in range(DEPTH):
            dma(lq_t[:, l, :], lq_in[l].partition_broadcast(128), writes=[lq_tk])
        op("pool", lambda h: h.memset(ones_b[:], 1.0), writes=[ones_tk])
        op("pool", lambda h: h.memset(eps_t[:], EPS), writes=[eps_tk])
        for l in range(depth):
            lam_init = 0.8 - 0.6 * math.exp(-0.3 * l)
            op("dve", lambda h: h.tensor_tensor(out=lqp_t[:, 0:64], in0=lq_t[:, l, 0:64], in1=lq_t[:, l, 64:128], op=ALU.mult),
               [lq_tk], [lqp_tk])
            op("dve", lambda h: h.tensor_tensor(out=lqp_t[:, 64:128], in0=lq_t[:, l, 128:192], in1=lq_t[:, l, 192:256], op=ALU.mult),
               [lq_tk], [lqp_tk])
            op("dve", lambda h: h.reduce_sum(out=lqs_t[:, 0:1], in_=lqp_t[:, 0:64], axis=mybir.AxisListType.X), [lqp_tk], [lqs_tk])
            op("dve", lambda h: h.reduce_sum(out=lqs_t[:, 1:2], in_=lqp_t[:, 64:128], axis=mybir.AxisListType.X), [lqp_tk], [lqs_tk])
            op("act", lambda h: h.activation(out=lqs_t[:, 2:4], in_=lqs_t[:, 0:2], func=AF.Exp), [lqs_tk], [lqs_tk])
            op("dve", lambda h: h.scalar_tensor_tensor(out=nlam_t[:, l:l + 1], in0=lqs_t[:, 3:4], scalar=-lam_init, in1=lqs_t[:, 2:3],
                                                       op0=ALU.add, op1=ALU.subtract), [lqs_tk], [nlam_tk])
            op("dve", lambda h: h.tensor_scalar(out=gdl_t[:, l:l + 1], in0=gd_t[:, l:l + 1], scalar1=(1.0 - lam_init), scalar2=None,
                                                op0=ALU.mult), [gd_tk], [gdl_tk])
        S.barrier()

        lc_state = {"k": 0}

        def make_stg(st_stack, piece=512):
            return ([sb(st_stack, f"wstg{i}", [128, 8, piece], F32) for i in range(4)], [Tk(f"wstg{i}") for i in range(4)])

        def load_cast(stgs, dst, dst_tk, src3, nk, ncols, gl, piece=512):
            stg, stg_tk = stgs
            k = lc_state["k"]
            for c0 in range(0, ncols, piece):
                w = min(piece, ncols - c0)
                b = k % 4
                dma(stg[b][:, 0:nk, 0:w], src3[:, :, c0:c0 + w], writes=[stg_tk[b]])
                for c in range(nk):
                    en = "act" if (k * nk + c) % 2 else "dve"
                    if gl is not None:
                        if en == "act":
                            op("act", lambda h: h.activation(out=dst[:, c, c0:c0 + w], in_=stg[b][:, c, 0:w], func=AF.Copy, scale=gk_t[:, gl, c:c + 1]),
                               [stg_tk[b], gk_tk], [])
                        else:
                            op("dve", lambda h: h.tensor_scalar(out=dst[:, c, c0:c0 + w], in0=stg[b][:, c, 0:w], scalar1=gk_t[:, gl, c:c + 1],
                                                                scalar2=None, op0=ALU.mult), [stg_tk[b], gk_tk], [])
                    elif c == 0:
                        if k % 2:
                            op("act", lambda h: h.copy(out=dst[:, 0:nk, c0:c0 + w], in_=stg[b][:, 0:nk, 0:w]), [stg_tk[b]], [])
                        else:
                            op("dve", lambda h: h.tensor_copy(out=dst[:, 0:nk, c0:c0 + w], in_=stg[b][:, 0:nk, 0:w]), [stg_tk[b]], [])
                k += 1
            lc_state["k"] = k

        def rms_rstd(st, xap, x_tk, junk, junk_tk, ss, ss_tk, rstd, rstd_tk):
            op("act", lambda h: h.activation(out=junk, in_=xap, func=AF.Square, accum_out=ss), [x_tk], [junk_tk, ss_tk])
            op("act", lambda h: h.activation(out=ss, in_=ss, func=AF.Sqrt, bias=eps_t[:], scale=1.0 / D), [ss_tk, eps_tk], [ss_tk])
            op("dve", lambda h: h.reciprocal(out=rstd, in_=ss), [ss_tk], [rstd_tk])

        for l in range(depth):
            x_src = x_in if l == 0 else XS
            with contextlib.ExitStack() as st:
                _maybe_skip(st, "2" not in stages)
                ptr = st.enter_context(nc.psum_tensor(f"ptr{l}", [128, 1024], BF16))
                W2 = sb(st, "W2", [128, 8, GL], BF16); W2_tk = Tk("W2")
                with contextlib.ExitStack() as st2:
                    stgs = make_stg(st2)
                    load_cast(stgs, W2, W2_tk, w_in[l, :, 0:GL].rearrange("(c p) n -> p c n", p=128), 8, GL, l)
                    S.barrier()
                NX = 8
                xt = [sb(st, f"xt{i}", [128, D], F32) for i in range(NX)]; xt_tk = [Tk(f"xt{i}") for i in range(NX)]
                hb = [sb(st, f"hb{i}", [128, D], BF16) for i in range(2)]; hb_tk = [Tk(f"hb{i}") for i in range(2)]
                junk = sb(st, "junk", [128, D], BF16); junk_tk = Tk("junk")
                ss = [sb(st, f"ss{i}", [128, 1], F32) for i in range(2)]; ss_tk = [Tk(f"ss{i}") for i in range(2)]
                rstd = [sb(st, f"rstd{i}", [128, 1], F32) for i in range(2)]; rstd_tk = [Tk(f"rstd{i}") for i in range(2)]
                hT = [sb(st, f"hT{i}", [128, 8, 512], BF16) for i in range(2)]; hT_tk = [Tk(f"hT{i}") for i in range(2)]
                GS = 8
                fst = [sb(st, f"fst{i}", [128, GS, 512], BF16) for i in range(2)]; fst_tk = [Tk(f"fst{i}") for i in range(2)]
                vst = [sb(st, f"vst{i}", [128, 1792], BF16) for i in range(2)]; vst_tk = [Tk(f"vst{i}") for i in range(2)]
                NT = NTOK // 512
                sub = 0
                fgrp = 0
                vgrp = 0

                def s2_load(T):
                    for s_ in range(4):
                        t0 = T * 512 + s_ * 128
                        xb = (T * 4 + s_) % NX
                        dma(xt[xb][:], x_src[t0:t0 + 128, :], writes=[xt_tk[xb]])

                def s2_front(T):
                    nonlocal sub
                    hb_ = T % 2
                    for s_ in range(4):
                        xb = (T * 4 + s_) % NX
                        b2 = sub % 2
                        sub += 1
                        rms_rstd(st, xt[xb][:], xt_tk[xb], junk[:], junk_tk, ss[b2][:], ss_tk[b2], rstd[b2][:], rstd_tk[b2])
                        op("dve", lambda h: h.tensor_scalar(out=hb[b2][:], in0=xt[xb][:], scalar1=rstd[b2][:], scalar2=None, op0=ALU.mult),
                           [xt_tk[xb], rstd_tk[b2]], [hb_tk[b2]])
                        for c in range(8):
                            op("pe", lambda h: h.transpose(ptr[:, c * 128:(c + 1) * 128], hb[b2][:, c * 128:(c + 1) * 128], ident[:]),
                               [hb_tk[b2], ident_tk], [ptr_tk])
                        copy_op(evac_eng(), hT[hb_][:, :, s_ * 128:(s_ + 1) * 128], ptr[:].rearrange("p (c t) -> p c t", c=8),
                                [ptr_tk], [hT_tk[hb_]])
                    dma(HT[:, :, T * 512:(T + 1) * 512].rearrange("c p t -> p c t"), hT[hb_][:], reads=[hT_tk[hb_]])

                s2_load(0)
                s2_load(1)
                s2_front(0)
                for T in range(NT):
                    hb_ = T % 2
                    for g0 in range(0, NB, GS):
                        if g0 == 2 * GS and T + 1 < NT:
                            s2_front(T + 1)
                            if T + 2 < NT:
                                s2_load(T + 2)
                        nb = min(GS, NB - g0)
                        fb = fgrp % 2
                        fgrp += 1
                        for bi in range(nb):
                            col, M, silu = FM_BLOCKS[g0 + bi]
                            if M == 64 and (g0 + bi) % 2 == 1:
                                continue
                            MM = 128
                            ps, ps_tk = next_ps()
                            for c in range(8):
                                op("pe", lambda h: h.matmul(ps[0:MM, :], lhsT=W2[:, c, col:col + MM], rhs=hT[hb_][:, c, :],
                                                            start=(c == 0), stop=(c == 7)), [W2_tk, hT_tk[hb_]], [ps_tk])
                            if silu:
                                op("act", lambda h: h.activation(out=fst[fb][0:M, bi, :], in_=ps[0:M, :], func=AF.Silu), [ps_tk], [fst_tk[fb]])
                            elif M == 64:
                                copy_op("act", fst[fb][0:64, bi, :], ps[0:64, :], [ps_tk], [fst_tk[fb]])
                                copy_op("dve", fst[fb][0:64, bi + 1, :], ps[64:128, :], [ps_tk], [fst_tk[fb]])
                            else:
                                copy_op(evac_eng(), fst[fb][0:M, bi, :], ps[0:M, :], [ps_tk], [fst_tk[fb]])
                        i0 = 0
                        while i0 < nb:
                            M0 = FM_BLOCKS[g0 + i0][1]
                            i1 = i0
                            while i1 < nb and FM_BLOCKS[g0 + i1][1] == M0:
                                i1 += 1
                            dma(FM[g0 + i0:g0 + i1, 0:M0, T * 512:(T + 1) * 512].rearrange("b p t -> p b t"), fst[fb][0:M0, i0:i1, :], reads=[fst_tk[fb]])
                            i0 = i1
                    for s_ in range(4):
                        vb_ = vgrp % 2
                        vgrp += 1
                        t0 = T * 512 + s_ * 128
                        off = 0
                        for (col, w, ti, dcol) in TM_GROUPS:
                            ps, ps_tk = next_ps()
                            for c in range(8):
                                op("pe", lambda h: h.matmul(ps[:, 0:w], lhsT=hT[hb_][:, c, s_ * 128:(s_ + 1) * 128], rhs=W2[:, c, col:col + w],
                                                            start=(c == 0), stop=(c == 7)), [W2_tk, hT_tk[hb_]], [ps_tk])
                            copy_op(evac_eng(), vst[vb_][:, off:off + w], ps[:, 0:w], [ps_tk], [vst_tk[vb_]])
                            off += w
                        dma(VT[0][t0:t0 + 128, :], vst[vb_][:, 0:512], reads=[vst_tk[vb_]])
                        dma(VT[1][t0:t0 + 128, :], vst[vb_][:, 512:1024], reads=[vst_tk[vb_]])
                        dma(VT[2][t0:t0 + 128, :], vst[vb_][:, 1024:1792], reads=[vst_tk[vb_]])
                S.barrier()

            with contextlib.ExitStack() as st:
                _maybe_skip(st, "A" not in stages)
                pb7 = st.enter_context(nc.psum_tensor(f"pb7_{l}a", [128, 512], F32))
                pbanks[7:] = [pb7]
                pb_tk[7:] = [Tk("pb7")]
                LA = 4
                pipe = Pipe(LA)
                QP = sb(st, "QP", [128, NTOK], BF16); QP_tk = Tk("QP")
                KP = sb(st, "KP", [128, NTOK], BF16); KP_tk = Tk("KP")
                ZP = sb(st, "ZP", [128, NTOK], BF16); ZP_tk = Tk("ZP")
                YP = sb(st, "YP", [128, NTOK], BF16); YP_tk = Tk("YP")
                tdup = [sb(st, f"tdup{i}", [128, 15, 64], F32) for i in range(2)]; tdup_tk = [Tk(f"tdup{i}") for i in range(2)]
                BT = [sb(st, f"BT{i}", [128, 20, 512], F32) for i in range(2)]
                BT_tks = [[Tk(f"BT{i}_{j}") for j in range(20)] for i in range(2)]
                VG = [sb(st, f"VG{i}", [128, 32, 128], BF16) for i in range(2)]; VG_tk = [Tk(f"VG{i}") for i in range(2)]
                NTMP = 4
                tmp = [sb(st, f"tmp{i}", [128, 512], F32) for i in range(NTMP)]; tmp_tk = [Tk(f"tmp{i}") for i in range(NTMP)]
                NPT = LA + 2
                PT = [sb(st, f"PT{i}", [128, 512], BF16) for i in range(NPT)]; PT_tk = [Tk(f"PT{i}") for i in range(NPT)]
                rz = [sb(st, f"rz{i}", [128, 512], F32) for i in range(2)]; rz_tk = [Tk(f"rz{i}") for i in range(2)]
                g2 = [sb(st, f"g2{i}", [128, 512], F32) for i in range(2)]; g2_tk = [Tk(f"g2{i}") for i in range(2)]
                set_ring([3, 4, 5, 6, 7])
                op("pool", lambda h: h.memset(VG[0][:, :, 64:128], 1.0), writes=[VG_tk[0]])
                op("pool", lambda h: h.memset(VG[1][:, :, 0:64], 1.0), writes=[VG_tk[1]])
                tbase = {0: 0, 1: 6, 2: 14}

                def build_bias(head):
                    bb = head % 2
                    td, td_tk = tdup[bb], tdup_tk[bb]
                    dma(td[0:64], trev_in[l, head], writes=[td_tk])
                    dma(td[64:128], trev_in[l, head], writes=[td_tk])
                    for ti_ in range(20):
                        op("pool", lambda h: h.memset(BT[bb][:, ti_, :], NEG), writes=[BT_tks[bb][ti_]])
                    rows_total = 32
                    for v, (R, ch0, nch) in enumerate(((0, 0, 6), (8, 2, 8), (rows_total - 8, (rows_total - 12) // 2, 6))):
                        for j in range(nch):
                            ti = tbase[v] + j
                            for kr in range(2):
                                krow = 2 * (ch0 + j) + kr
                                ent = []
                                for qr in range(8):
                                    qrow = R + qr
                                    rs = min(max(qrow - 4, 0), rows_total - 8)
                                    ent.append(14 - (krow - qrow + 7) if rs <= krow < rs + 8 else None)
                                qr = 0
                                while qr < 8:
                                    if ent[qr] is None:
                                        qr += 1
                                        continue
                                    q1 = qr
                                    while q1 + 1 < 8 and ent[q1 + 1] is not None and ent[q1 + 1] == ent[q1] + 1:
                                        q1 += 1
                                    n = q1 - qr + 1
                                    e0 = ent[qr]
                                    pl = 64 * kr
                                    op("pool", lambda h: h.tensor_copy(out=BT[bb][pl:pl + 64, ti, qr * 64:(qr + n) * 64].rearrange("p (a b) -> p a b", b=64),
                                                                       in_=td[pl:pl + 64, e0:e0 + n, :]), [td_tk], [BT_tks[bb][ti]])
                                    qr = q1 + 1

                acnt = 0
                pcnt = 0
                build_bias(0)
                for i in range(4):
                    dma(QP[:], FM[B_QA + i], writes=[QP_tk])
                    dma(KP[:], FM[B_KA + i], writes=[KP_tk])
                    dma(ZP[:], FM[B_ZA + i], writes=[ZP_tk])
                    for hh in range(2):
                        head = 2 * i + hh
                        bb = head % 2
                        ub = 64 * hh
                        zb = 64 - ub
                        if head + 1 < 8:
                            build_bias(head + 1)
                        vt = VG[hh]
                        vt_tk = VG_tk[hh]
                        vc0 = 0 if hh == 0 else 64
                        for (s0, L) in SEQS:
                            rows = L // 64
                            nqt = rows // 8
                            dma(vt[:, 0:L // 128, vc0:vc0 + 64],
                                VT[0][s0:s0 + L, head * 64:(head + 1) * 64].rearrange("(c p) e -> p c e", p=128), writes=[vt_tk])
                            for qt in range(nqt):
                                if qt == 0:
                                    v, ch0, nch = 0, 0, 6
                                elif qt == nqt - 1:
                                    v, ch0, nch = 2, (rows - 12) // 2, 6
                                else:
                                    v, ch0, nch = 1, 4 * qt - 2, 8
                                ab = acnt % 2
                                a3 = acnt % 3
                                acnt += 1
                                acc, acc_tk = pbanks[a3], pb_tk[a3]
                                q0 = s0 + qt * 512
                                for j in range(nch):
                                    kc = ch0 + j
                                    ti = tbase[v] + j
                                    tb = pcnt % NTMP
                                    pb = pcnt % NPT
                                    pcnt += 1
                                    vq = []
                                    for qr in range(8):
                                        qrow = 8 * qt + qr
                                        rs_ = min(max(qrow - 4, 0), rows - 8)
                                        if any(rs_ <= 2 * kc + kr < rs_ + 8 for kr in range(2)):
                                            vq.append(qr)
                                    assert vq and vq == list(range(vq[0], vq[-1] + 1))
                                    cl, ch_ = 64 * vq[0], 64 * (vq[-1] + 1)

                                    def front(kc=kc, ti=ti, tb=tb, pb=pb, q0=q0, s0=s0, ub=ub, bb=bb, cl=cl, ch_=ch_, j=j, acc=acc, acc_tk=acc_tk):
                                        if j == 0:
                                            op("act", lambda h: h.memzero(acc[:]), [], [acc_tk])
                                        ps, ps_tk = next_ps()
                                        op("pe", lambda h: h.matmul(ps[:, cl:ch_], lhsT=KP[ub:ub + 64, s0 + kc * 128:s0 + (kc + 1) * 128],
                                                                    rhs=QP[ub:ub + 64, q0 + cl:q0 + ch_], start=True, stop=True), [KP_tk, QP_tk], [ps_tk])
                                        op("dve", lambda h: h.scalar_tensor_tensor(out=tmp[tb][:, cl:ch_], in0=ps[:, cl:ch_], scalar=0.125, in1=BT[bb][:, ti, cl:ch_],
                                                                                   op0=ALU.mult, op1=ALU.add), [ps_tk, BT_tks[bb][ti]], [tmp_tk[tb]])
                                        op("act", lambda h: h.activation(out=PT[pb][:, cl:ch_], in_=tmp[tb][:, cl:ch_], func=AF.Exp), [tmp_tk[tb]], [PT_tk[pb]])

                                    def back(kc=kc, pb=pb, j=j, nch=nch, acc=acc, acc_tk=acc_tk, vt=vt, vt_tk=vt_tk, ub=ub, zb=zb, q0=q0, ab=ab, cl=cl, ch_=ch_):
                                        op("pe", lambda h: h.matmul(acc[:, cl:ch_], lhsT=vt[:, kc, :], rhs=PT[pb][:, cl:ch_], start=False, stop=True,
                                                                    skip_group_check=True), [vt_tk, PT_tk[pb]], [acc_tk])
                                        if j == nch - 1:
                                            def post(acc=acc, acc_tk=acc_tk, ub=ub, zb=zb, q0=q0, ab=ab):
                                                op("act", lambda h: h.activation(out=rz[ab][ub:ub + 64, :], in_=acc[zb:zb + 64, :], func=AF.Ln), [acc_tk], [rz_tk[ab]])
                                                op("act", lambda h: h.activation(out=rz[ab][ub:ub + 64, :], in_=rz[ab][ub:ub + 64, :], func=AF.Exp, scale=-1.0),
                                                   [rz_tk[ab]], [rz_tk[ab]])
                                                op("dve", lambda h: h.tensor_tensor(out=g2[ab][ub:ub + 64, :], in0=rz[ab][ub:ub + 64, :], in1=ZP[ub:ub + 64, q0:q0 + 512],
                                                                                    op=ALU.mult), [rz_tk[ab], ZP_tk], [g2_tk[ab]])
                                                op("dve", lambda h: h.tensor_tensor(out=YP[ub:ub + 64, q0:q0 + 512], in0=acc[ub:ub + 64, :], in1=g2[ab][ub:ub + 64, :],
                                                                                    op=ALU.mult), [acc_tk, g2_tk[ab]], [YP_tk])
                                            pipe.defer(post, 3)
                                    pipe.push(front, back)
                            pipe.flush()
                    dma(YG[i], YP[:], reads=[YP_tk])
                set_ring(range(7))
                del pbanks[7:], pb_tk[7:]
                S.barrier()

            with contextlib.ExitStack() as st:
                _maybe_skip(st, "B" not in stages)
                pb7 = st.enter_context(nc.psum_tensor(f"pb7_{l}b", [128, 512], F32))
                pbanks[7:] = [pb7]
                pb_tk[7:] = [Tk("pb7")]
                LA = 3
                pipe = Pipe(LA)
                LM = 4096
                Qa = [[sb(st, f"Qa{m}{i}", [69, LM], BF16) for m in range(2)] for i in range(2)]
                Kp = [[sb(st, f"Kp{m}{i}", [69, LM], BF16) for m in range(2)] for i in range(2)]
                Kn = [[sb(st, f"Kn{m}{i}", [69, LM], BF16) for m in range(2)] for i in range(2)]
                Qa_tk = [[Tk("Qa") for m in range(2)] for i in range(2)]
                Kp_tk = [[Tk("Kp") for m in range(2)] for i in range(2)]
                Kn_tk = [[Tk("Kn") for m in range(2)] for i in range(2)]
                Vb = [sb(st, f"Vb{i}", [128, 32, 128], BF16) for i in range(2)]; Vb_tk = [Tk("Vb") for i in range(2)]
                Zb = [sb(st, f"Zb{i}", [128, LM], BF16) for i in range(2)]; Zb_tk = [Tk("Zb") for i in range(2)]
                Yb = [sb(st, f"Yb{i}", [128, LM], BF16) for i in range(2)]; Yb_tk = [Tk("Yb") for i in range(2)]
                cdg = sb(st, "cdg", [128, 4, 128], F32); cdg_tk = Tk("cdg")
                NPT = LA + 2
                PT = [sb(st, f"PTb{i}", [128, 512], BF16) for i in range(NPT)]; PT_tk = [Tk(f"PTb{i}") for i in range(NPT)]
                r0 = sb(st, "r0", [128, 512], F32); r0_tk = Tk("r0")
                r1 = sb(st, "r1", [128, 512], F32); r1_tk = Tk("r1")
                a0 = sb(st, "a0", [128, 512], F32); a0_tk = Tk("a0")
                a1 = sb(st, "a1", [128, 512], F32); a1_tk = Tk("a1")
                o_t = sb(st, "o_t", [128, 512], F32); o_tk = Tk("o_t")
                sq = sb(st, "sq", [128, 512], BF16); sq_tk = Tk("sq")
                rs_t = sb(st, "rs_t", [128, 512], F32); rs_tk = Tk("rs_t")
                t2 = sb(st, "t2", [128, 512], F32); t2_tk = Tk("t2")
                Pa = [[sb(st, f"Pa{i}{e}", [128, 512], F32) for e in range(2)] for i in range(2)]
                Pa_tk = [[Tk(f"Pa{i}{e}") for e in range(2)] for i in range(2)]
                dma(cdg[:], cdiag_in, writes=[cdg_tk])
                pcnt = 0
                qmcnt = 0
                set_ring([5, 6, 7])
                jobs = [(s0, L, hd) for (s0, L) in SEQS for hd in range(4)]

                def b_load(jn):
                    s0, L, hd = jobs[jn]
                    sb_ = jn % 2
                    for m in range(2):
                        bq = B_QB + 2 * hd + m
                        bk = B_KB + 2 * hd + m
                        dma(Qa[sb_][m][0:64, 0:L], FM[bq, 0:64, s0:s0 + L], writes=[Qa_tk[sb_][m]])
                        dma(Qa[sb_][m][64:69, 0:L], qaug_in[:, 0:L], writes=[Qa_tk[sb_][m]])
                        dma(Kp[sb_][m][0:64, 0:L], FM[bk, 0:64, s0:s0 + L], writes=[Kp_tk[sb_][m]])
                        dma(Kp[sb_][m][64:69, 0:L], kaug_in[hd, 0, :, 0:L], writes=[Kp_tk[sb_][m]])
                        dma(Kn[sb_][m][0:64, 0:L], FM[bk, 0:64, s0:s0 + L], writes=[Kn_tk[sb_][m]])
                        dma(Kn[sb_][m][64:69, 0:L], kaug_in[hd, 1, :, 0:L], writes=[Kn_tk[sb_][m]])
                    dma(Vb[sb_][:, 0:L // 128, :], VT[1][s0:s0 + L, hd * 128:(hd + 1) * 128].rearrange("(c p) e -> p c e", p=128), writes=[Vb_tk[sb_]])
                    dma(Zb[sb_][:, 0:L], FM[B_ZB + hd, :, s0:s0 + L], writes=[Zb_tk[sb_]])

                b_load(0)
                for jn, (s0, L, hd) in enumerate(jobs):
                    sb_ = jn % 2
                    if jn + 1 < len(jobs):
                        b_load(jn + 1)
                    nkc = L // 128
                    nqt = L // 512
                    slope = 2.0 ** (-8.0 * (hd + 1) / 4)
                    for qt in range(nqt):
                        q0 = qt * 512
                        for m in range(2):
                            U, U_tk = pbanks[2 * m], pb_tk[2 * m]
                            Z, Z_tk = pbanks[2 * m + 1], pb_tk[2 * m + 1]
                            kcs = []
                            for kc in range(nkc):
                                k0 = kc * 128
                                if k0 + 128 <= q0:
                                    mind = q0 - (k0 + 127)
                                elif k0 >= q0 + 512:
                                    mind = k0 - (q0 + 511)
                                else:
                                    mind = 0
                                if slope * mind > 60.0:
                                    continue
                                kcs.append(kc)
                            nk_ = len(kcs)
                            assert nk_ > LA + 2
                            pset = qmcnt % 2
                            qmcnt += 1
                            for ji, kc in enumerate(kcs):
                                pb = pcnt % NPT
                                pcnt += 1

                                def front(kc=kc, pb=pb, q0=q0, m=m, hd=hd, sb_=sb_, ji=ji, pset=pset):
                                    k0 = kc * 128
                                    ps, ps_tk = next_ps()
                                    qa_, kp_, kn_ = Qa[sb_][m], Kp[sb_][m], Kn[sb_][m]
                                    qa_tk, kp_tk, kn_tk = Qa_tk[sb_][m], Kp_tk[sb_][m], Kn_tk[sb_][m]
                                    if k0 + 128 <= q0:
                                        op("pe", lambda h: h.matmul(ps[:], lhsT=kp_[:, k0:k0 + 128], rhs=qa_[:, q0:q0 + 512], start=True, stop=True),
                                           [kp_tk, qa_tk], [ps_tk])
                                    elif k0 >= q0 + 512:
                                        op("pe", lambda h: h.matmul(ps[:], lhsT=kn_[:, k0:k0 + 128], rhs=qa_[:, q0:q0 + 512], start=True, stop=True),
                                           [kn_tk, qa_tk], [ps_tk])
                                    else:
                                        jd = (k0 - q0) // 128
                                        if jd > 0:
                                            op("pe", lambda h: h.matmul(ps[:, 0:128 * jd], lhsT=kn_[:, k0:k0 + 128], rhs=qa_[:, q0:q0 + 128 * jd],
                                                                        start=True, stop=True), [kn_tk, qa_tk], [ps_tk])
                                        op("pe", lambda h: h.matmul(ps[:, 128 * jd:512], lhsT=kp_[:, k0:k0 + 128], rhs=qa_[:, q0 + 128 * jd:q0 + 512],
                                                                    start=True, stop=True), [kp_tk, qa_tk], [ps_tk])
                                        op("dve", lambda h: h.tensor_tensor(out=ps[:, 128 * jd:128 * (jd + 1)], in0=ps[:, 128 * jd:128 * (jd + 1)],
                                                                            in1=cdg[:, hd, :], op=ALU.add), [ps_tk, cdg_tk], [ps_tk])
                                    op("act", lambda h: h.activation(out=PT[pb][:], in_=ps[:], func=AF.Exp, scale=0.125), [ps_tk], [PT_tk[pb]])

                                def back(kc=kc, pb=pb, ji=ji, nk_=nk_, U=U, U_tk=U_tk, Z=Z, Z_tk=Z_tk, m=m, q0=q0, sb_=sb_, pset=pset):
                                    op("pe", lambda h: h.matmul(U[:], lhsT=Vb[sb_][:, kc, :], rhs=PT[pb][:], start=(ji == 0), stop=(ji == nk_ - 1)),
                                       [Vb_tk[sb_], PT_tk[pb]], [U_tk])
                                    op("pe", lambda h: h.matmul(Z[:], lhsT=ones_b[:], rhs=PT[pb][:], start=(ji == 0), stop=(ji == nk_ - 1)),
                                       [ones_tk, PT_tk[pb]], [Z_tk])
                                    if ji == nk_ - 1 and m == 0:
                                        op("act", lambda h: h.activation(out=r0[:], in_=Z[:], func=AF.Ln), [Z_tk], [r0_tk])
                                        op("act", lambda h: h.activation(out=r0[:], in_=r0[:], func=AF.Exp, scale=-1.0), [r0_tk], [r0_tk])
                                        op("dve", lambda h: h.tensor_tensor(out=a0[:], in0=U[:], in1=r0[:], op=ALU.mult), [U_tk, r0_tk], [a0_tk])
                                    if ji == nk_ - 1 and m == 1:
                                        op("act", lambda h: h.activation(out=r1[:], in_=Z[:], func=AF.Ln), [Z_tk], [r1_tk])
                                        op("act", lambda h: h.activation(out=r1[:], in_=r1[:], func=AF.Exp, scale=-1.0), [r1_tk], [r1_tk])
                                        op("dve", lambda h: h.tensor_tensor(out=a1[:], in0=U[:], in1=r1[:], op=ALU.mult), [U_tk, r1_tk], [a1_tk])
                                        op("dve", lambda h: h.scalar_tensor_tensor(out=o_t[:], in0=a1[:], scalar=nlam_t[:, l:l + 1], in1=a0[:],
                                                                                   op0=ALU.mult, op1=ALU.add), [a1_tk, a0_tk, nlam_tk], [o_tk])
                                        op("pool", lambda h: h.tensor_tensor(out=sq[:], in0=o_t[:], in1=o_t[:], op=ALU.mult), [o_tk], [sq_tk])

                                        def post2(q0=q0, sb_=sb_):
                                            ssp, ssp_tk = pbanks[4], pb_tk[4]
                                            op("pe", lambda h: h.matmul(ssp[:], lhsT=ones_b[:], rhs=sq[:], start=True, stop=True), [ones_tk, sq_tk], [ssp_tk])
                                            op("act", lambda h: h.activation(out=rs_t[:], in_=ssp[:], func=AF.Ln, bias=eps_t[:], scale=1.0 / 128), [ssp_tk, eps_tk], [rs_tk])
                                            op("act", lambda h: h.activation(out=rs_t[:], in_=rs_t[:], func=AF.Exp, scale=-0.5), [rs_tk], [rs_tk])
                                            op("pool", lambda h: h.tensor_tensor(out=t2[:], in0=o_t[:], in1=rs_t[:], op=ALU.mult), [o_tk, rs_tk], [t2_tk])
                                            op("pool", lambda h: h.tensor_tensor(out=t2[:], in0=t2[:], in1=Zb[sb_][:, q0:q0 + 512], op=ALU.mult), [t2_tk, Zb_tk[sb_]], [t2_tk])
                                            op("pool", lambda h: h.tensor_scalar(out=Yb[sb_][:, q0:q0 + 512], in0=t2[:], scalar1=gdl_t[:, l:l + 1], scalar2=None,
                                                                                 op0=ALU.mult), [t2_tk, gdl_tk], [Yb_tk[sb_]])
                                        pipe.defer(post2, 3)
                                pipe.push(front, back)
                    pipe.flush()
                    dma(YG[4 + hd, :, s0:s0 + L], Yb[sb_][:, 0:L], reads=[Yb_tk[sb_]])
                set_ring(range(7))
                del pbanks[7:], pb_tk[7:]
                S.barrier()

            with contextlib.ExitStack() as st:
                _maybe_skip(st, "C" not in stages)
                pb7 = st.enter_context(nc.psum_tensor(f"pb7_{l}c", [128, 512], F32))
                pbanks[7:] = [pb7]
                pb_tk[7:] = [Tk("pb7")]
                LA = 4
                pipe = Pipe(LA)
                LM = 4096
                PAD = 1024
                QD = [sb(st, f"QD{i}", [128, LM], BF16) for i in range(2)]; QD_tk = [Tk(f"QD{i}") for i in range(2)]
                KD = [sb(st, f"KD{i}", [128, LM + 2 * PAD], BF16) for i in range(2)]; KD_tk = [Tk(f"KD{i}") for i in range(2)]
                ZD = [sb(st, f"ZD{i}", [128, LM], BF16) for i in range(2)]; ZD_tk = [Tk(f"ZD{i}") for i in range(2)]
                YD = [sb(st, f"YD{i}", [128, LM], BF16) for i in range(2)]; YD_tk = [Tk(f"YD{i}") for i in range(2)]
                VD = [sb(st, f"VD{i}", [128, 48, 128], BF16) for i in range(2)]; VD_tk = [Tk(f"VD{i}") for i in range(2)]
                AT = [sb(st, f"AT{i}", [128, 2, LM], F32) for i in range(2)]; AT_tk = [Tk(f"AT{i}") for i in range(2)]
                dlb = sb(st, "dlb", [128, 12, 256], F32); dlb_tk = Tk("dlb")
                NTMP = 4
                tmp = [sb(st, f"tmpc{i}", [128, 256], F32) for i in range(NTMP)]; tmp_tk = [Tk(f"tmpc{i}") for i in range(NTMP)]
                NPT = LA + 2
                PT = [sb(st, f"PTc{i}", [128, 256], BF16) for i in range(NPT)]; PT_tk = [Tk(f"PTc{i}") for i in range(NPT)]
                NRC = 3
                rzc = [sb(st, f"rzc{i}", [128, 512], F32) for i in range(NRC)]; rzc_tk = [Tk(f"rzc{i}") for i in range(NRC)]
                dma(dlb[:], dilb_in, writes=[dlb_tk])
                for i in range(2):
                    op("pool", lambda h: h.memset(KD[i][:], 0.0), writes=[KD_tk[i]])
                acc_tks = [pb_tk[3 + i] for i in range(4)]
                set_ring([0, 1, 2, 7])
                pcnt = 0
                rcnt = 0
                cjobs = [(s0, L, p, g) for (s0, L) in SEQS for p in range(2) for g in range(3)]
                hjobs = [(jn, hh) for jn in range(len(cjobs)) for hh in range(2)]

                def c_load(jn):
                    s0, L, p, g = cjobs[jn]
                    lb = jn % 2
                    blk = 2 * g + p
                    dma(QD[lb][:, 0:L], FM[B_QC + blk, :, s0:s0 + L], writes=[QD_tk[lb]])
                    dma(KD[lb][:, PAD:PAD + L], FM[B_KC + blk, :, s0:s0 + L], writes=[KD_tk[lb]])
                    if L < LM:
                        op("pool", lambda h: h.memset(KD[lb][:, PAD + L:PAD + L + PAD], 0.0), writes=[KD_tk[lb]])
                    if g == 0:
                        zi = (jn // 3) % 2
                        dma(ZD[zi][:, 0:L], FM[B_ZC + p, :, s0:s0 + L], writes=[ZD_tk[zi]])

                def v_prep(hn):
                    jn, hh = hjobs[hn]
                    s0, L, p, g = cjobs[jn]
                    d = DIL[g][1]
                    M = L // d
                    nch = M // 128 + 1
                    hc = 4 * g + 2 * p + hh
                    vc0 = 0 if hh == 0 else 64
                    oc0 = 64 - vc0
                    vt, vt_tk = VD[hn % 2], VD_tk[hn % 2]
                    vt4 = vt[:, 0:d * nch, :].rearrange("p (r j) e -> p r j e", r=d)
                    op("pool", lambda h: h.memset(vt[:, 0:d * nch, :], 0.0), writes=[vt_tk])
                    if nch > 2:
                        op("pool", lambda h: h.memset(vt4[:, :, 1:nch - 1, oc0:oc0 + 64], 1.0), writes=[vt_tk])
                    op("pool", lambda h: h.memset(vt4[64:128, :, 0, oc0:oc0 + 64], 1.0), writes=[vt_tk])
                    op("pool", lambda h: h.memset(vt4[0:64, :, nch - 1, oc0:oc0 + 64], 1.0), writes=[vt_tk])
                    vsrc = VT[2][s0:s0 + L, hc * 64:(hc + 1) * 64]
                    for r in range(d):
                        if nch > 2:
                            src = vsrc[64 * d:64 * d + 128 * (nch - 2) * d, :].rearrange("(j k r) e -> r k j e", k=128, r=d)[r]
                            dma(vt4[:, r, 1:nch - 1, vc0:vc0 + 64], src, writes=[vt_tk])
                        src = vsrc[0:64 * d, :].rearrange("(k r) e -> r k e", r=d)[r]
                        dma(vt4[64:128, r, 0, vc0:vc0 + 64], src, writes=[vt_tk])
                        src = vsrc[(M - 64) * d:M * d, :].rearrange("(k r) e -> r k e", r=d)[r]
                        dma(vt4[0:64, r, nch - 1, vc0:vc0 + 64], src, writes=[vt_tk])

                c_load(0)
                v_prep(0)
                for hn, (jn, hh) in enumerate(hjobs):
                    s0, L, p, g = cjobs[jn]
                    d = DIL[g][1]
                    M = L // d
                    nsub = M // 128
                    nch = nsub + 1
                    lb = jn % 2
                    ai = (jn // 3) % 2
                    if hh == 0 and jn + 1 < len(cjobs):
                        c_load(jn + 1)
                    hc = 4 * g + 2 * p + hh
                    ub = 64 * hh
                    vt, vt_tk = VD[hn % 2], VD_tk[hn % 2]
                    npush = 0
                    for r in range(d):
                        for j in range(nch):
                            if npush == LA + 1 and hn + 1 < len(hjobs):
                                v_prep(hn + 1)
                            npush += 1
                            tb = pcnt % NTMP
                            pb = pcnt % NPT
                            pcnt += 1

                            def front(r=r, j=j, d=d, nsub=nsub, lb=lb, ub=ub, hc=hc, tb=tb, pb=pb):
                                kstart = PAD + (128 * j - 64) * d + r
                                qlo = max(j - 1, 0)
                                c_lo = 0 if j - 1 >= 0 else 128
                                c_hi = 256 if j <= nsub - 1 else 128
                                ps, ps_tk = next_ps()
                                qa0 = (128 * qlo) * d + r
                                nq = c_hi - c_lo
                                op("pe", lambda h: h.matmul(ps[:, c_lo:c_hi], lhsT=KD[lb][ub:ub + 64, kstart:kstart + 127 * d + 1:d],
                                                            rhs=QD[lb][ub:ub + 64, qa0:qa0 + (nq - 1) * d + 1:d], start=True, stop=True),
                                   [KD_tk[lb], QD_tk[lb]], [ps_tk])
                                op("dve", lambda h: h.scalar_tensor_tensor(out=tmp[tb][:, c_lo:c_hi], in0=ps[:, c_lo:c_hi], scalar=0.125,
                                                                           in1=dlb[:, hc, c_lo:c_hi], op0=ALU.mult, op1=ALU.add),
                                   [ps_tk, dlb_tk], [tmp_tk[tb]])
                                op("act", lambda h: h.activation(out=PT[pb][:, c_lo:c_hi], in_=tmp[tb][:, c_lo:c_hi], func=AF.Exp),
                                   [tmp_tk[tb]], [PT_tk[pb]])

                            def back(r=r, j=j, d=d, nsub=nsub, nch=nch, pb=pb, vt=vt, vt_tk=vt_tk, hh=hh, g=g, ai=ai):
                                if j - 1 >= 0:
                                    sl = (j - 1) % 4
                                    op("pe", lambda h: h.matmul(pbanks[3 + sl][:, 0:128], lhsT=vt[:, r * nch + j, :], rhs=PT[pb][:, 0:128],
                                                                start=False, stop=True), [vt_tk, PT_tk[pb]], [acc_tks[sl]])
                                    a0_ = (128 * (j - 1)) * d + r
                                    dst = AT[ai][:, hh, a0_:a0_ + 127 * d + 1:d]
                                    if g == 0:
                                        op("act", lambda h: h.copy(out=dst, in_=pbanks[3 + sl][:, 0:128]), [acc_tks[sl]], [AT_tk[ai]])
                                    else:
                                        op("dve", lambda h: h.tensor_tensor(out=dst, in0=pbanks[3 + sl][:, 0:128], in1=dst, op=ALU.add),
                                           [acc_tks[sl], AT_tk[ai]], [AT_tk[ai]])
                                if j <= nsub - 1:
                                    sl = j % 4
                                    op("pe", lambda h: h.matmul(pbanks[3 + sl][:, 0:128], lhsT=vt[:, r * nch + j, :], rhs=PT[pb][:, 128:256],
                                                                start=True, stop=False), [vt_tk, PT_tk[pb]], [acc_tks[sl]])
                            pipe.push(front, back)
                    if g == 2 and hh == 1:
                        pipe.flush()
                        last_job = (hn == len(hjobs) - 1)
                        kk = 0
                        for hh2 in range(2):
                            for c0 in range(0, L, 512):
                                ri = rcnt % NRC
                                rcnt += 1

                                def piece(hh2=hh2, c0=c0, ri=ri, ai=ai):
                                    ub2 = 64 * hh2
                                    zb2 = 64 - ub2
                                    op("act", lambda h: h.activation(out=rzc[ri][ub2:ub2 + 64, :], in_=AT[ai][zb2:zb2 + 64, hh2, c0:c0 + 512], func=AF.Ln),
                                       [AT_tk[ai]], [rzc_tk[ri]])
                                    op("act", lambda h: h.activation(out=rzc[ri][ub2:ub2 + 64, :], in_=rzc[ri][ub2:ub2 + 64, :], func=AF.Exp, scale=-1.0),
                                       [rzc_tk[ri]], [rzc_tk[ri]])
                                    op("pool", lambda h: h.tensor_tensor(out=rzc[ri][ub2:ub2 + 64, :], in0=rzc[ri][ub2:ub2 + 64, :], in1=ZD[ai][ub2:ub2 + 64, c0:c0 + 512],
                                                                         op=ALU.mult), [rzc_tk[ri], ZD_tk[ai]], [rzc_tk[ri]])
                                    op("pool", lambda h: h.tensor_tensor(out=YD[ai][ub2:ub2 + 64, c0:c0 + 512], in0=AT[ai][ub2:ub2 + 64, hh2, c0:c0 + 512],
                                                                         in1=rzc[ri][ub2:ub2 + 64, :], op=ALU.mult), [AT_tk[ai], rzc_tk[ri]], [YD_tk[ai]])
                                if last_job:
                                    piece()
                                else:
                                    pipe.defer(piece, 2 + 2 * kk)
                                kk += 1

                        def store(p=p, s0=s0, L=L, ai=ai):
                            dma(YG[8 + p, :, s0:s0 + L], YD[ai][:, 0:L], reads=[YD_tk[ai]])
                        if last_job:
                            store()
                        else:
                            pipe.defer(store, 2 + 2 * kk + 6)
                pipe.flush()
                set_ring(range(7))
                del pbanks[7:], pb_tk[7:]
                S.barrier()

            with contextlib.ExitStack() as st:
                _maybe_skip(st, "6" not in stages)
                WG = sb(st, "WG", [128, 8, 3072], BF16); WG_tk = Tk("WG")
                WBR = sb(st, "WBR", [128, 10, D], BF16); WBR_tk = Tk("WBR")
                WO = sb(st, "WO", [128, 8, D], BF16); WO_tk = Tk("WO")
                with contextlib.ExitStack() as st2:
                    stgs = make_stg(st2)
                    load_cast(stgs, WG, WG_tk, w_in[l, :, GL:INW].rearrange("(c p) n -> p c n", p=128), 8, 3072, l)
                    load_cast(stgs, WBR[:, 0:4], WBR_tk, w_bra[l].rearrange("(c p) n -> p c n", p=128), 4, D, None)
                    load_cast(stgs, WBR[:, 4:8], WBR_tk, w_brb[l].rearrange("(c p) n -> p c n", p=128), 4, D, None)
                    load_cast(stgs, WBR[:, 8:10], WBR_tk, w_brc[l].rearrange("(c p) n -> p c n", p=128), 2, D, None)
                    load_cast(stgs, WO, WO_tk, w_out[l].rearrange("(c p) n -> p c n", p=128), 8, D, None)
                    S.barrier()
                hT = [sb(st, f"hT6{i}", [128, 8, 512], BF16) for i in range(2)]; hT_tk = [Tk(f"hT6{i}") for i in range(2)]
                yg = [sb(st, f"yg6{i}", [128, 10, 512], BF16) for i in range(2)]; yg_tk = [Tk(f"yg6{i}") for i in range(2)]
                xt = [sb(st, f"x6{i}", [128, D], F32) for i in range(8)]; xt_tk = [Tk(f"x6{i}") for i in range(8)]
                xo = [sb(st, f"xo6{i}", [128, D], F32) for i in range(2)]; xo_tk = [Tk(f"xo6{i}") for i in range(2)]
                gt = [sb(st, f"gt6{i}", [128, 512], F32) for i in range(3)]; gt_tk = [Tk(f"gt6{i}") for i in range(3)]
                mt = [sb(st, f"mt6{i}", [128, 512], F32) for i in range(3)]; mt_tk = [Tk(f"mt6{i}") for i in range(3)]
                mg = [sb(st, f"mg6{i}", [128, 8, 512], BF16) for i in range(2)]; mg_tk = [Tk(f"mg6{i}") for i in range(2)]
                junk = sb(st, "junk6", [128, D], BF16); junk_tk = Tk("junk6")
                ss = [sb(st, f"ss6{i}", [128, 1], F32) for i in range(2)]; ss_tk = [Tk(f"ss6{i}") for i in range(2)]
                rstd = [sb(st, f"rstd6{i}", [128, 1], F32) for i in range(2)]; rstd_tk = [Tk(f"rstd6{i}") for i in range(2)]
                NT = NTOK // 512
                KBR = [(0, 4), (4, 4), (8, 2)]
                xcnt = 0
                ocnt = 0

                def s6_load(T):
                    b = T % 2
                    dma(hT[b][:], HT[:, :, T * 512:(T + 1) * 512].rearrange("c p t -> p c t"), writes=[hT_tk[b]])
                    dma(yg[b][:], YG[:, :, T * 512:(T + 1) * 512].rearrange("c p t -> p c t"), writes=[yg_tk[b]])

                s6_load(0)
                for T in range(NT):
                    b = T % 2
                    if T + 1 < NT:
                        s6_load(T + 1)
                    for s_ in range(4):
                        t0 = T * 512 + s_ * 128
                        dma(xt[(4 * T + s_) % 8][:], x_src[t0:t0 + 128, :], writes=[xt_tk[(4 * T + s_) % 8]])
                    for c in range(8):
                        for br in range(3):
                            gps, gps_tk = next_ps()
                            for k in range(8):
                                op("pe", lambda h: h.matmul(gps[:], lhsT=WG[:, k, br * D + c * 128:br * D + (c + 1) * 128], rhs=hT[b][:, k, :],
                                                            start=(k == 0), stop=(k == 7)), [WG_tk, hT_tk[b]], [gps_tk])
                            op("act", lambda h: h.activation(out=gt[br][:], in_=gps[:], func=AF.Sigmoid, bias=bg_t[:, l, br * 8 + c:br * 8 + c + 1]),
                               [gps_tk, bg_tk], [gt_tk[br]])
                            pps, pps_tk = next_ps()
                            k0, nk = KBR[br]
                            for k in range(nk):
                                op("pe", lambda h: h.matmul(pps[:], lhsT=WBR[:, k0 + k, c * 128:(c + 1) * 128], rhs=yg[b][:, k0 + k, :],
                                                            start=(k == 0), stop=(k == nk - 1)), [WBR_tk, yg_tk[b]], [pps_tk])
                            op("dve", lambda h: h.tensor_tensor(out=mt[br][:], in0=pps[:], in1=gt[br][:], op=ALU.mult), [pps_tk, gt_tk[br]], [mt_tk[br]])
                        op("pool", lambda h: h.tensor_tensor(out=mt[0][:], in0=mt[0][:], in1=mt[1][:], op=ALU.add), [mt_tk[0], mt_tk[1]], [mt_tk[0]])
                        op("pool", lambda h: h.tensor_tensor(out=mg[b][:, c, :], in0=mt[0][:], in1=mt[2][:], op=ALU.add), [mt_tk[0], mt_tk[2]], [mg_tk[b]])
                    for s_ in range(4):
                        t0 = T * 512 + s_ * 128
                        xb = (4 * T + s_) % 8
                        ob = ocnt % 2
                        ocnt += 1
                        for half in range(2):
                            ops_, ops_tk = next_ps()
                            for k in range(8):
                                op("pe", lambda h: h.matmul(ops_[:], lhsT=mg[b][:, k, s_ * 128:(s_ + 1) * 128], rhs=WO[:, k, half * 512:(half + 1) * 512],
                                                            start=(k == 0), stop=(k == 7)), [mg_tk[b], WO_tk], [ops_tk])
                            op("dve", lambda h: h.tensor_tensor(out=xo[ob][:, half * 512:(half + 1) * 512], in0=ops_[:], in1=xt[xb][:, half * 512:(half + 1) * 512],
                                                                op=ALU.add), [ops_tk, xt_tk[xb]], [xo_tk[ob]])
                        if l < depth - 1:
                            dma(XS[t0:t0 + 128, :], xo[ob][:], reads=[xo_tk[ob]])
                        else:
                            b2 = ocnt % 2
                            rms_rstd(st, xo[ob][:], xo_tk[ob], junk[:], junk_tk, ss[b2][:], ss_tk[b2], rstd[b2][:], rstd_tk[b2])
                            op("dve", lambda h: h.scalar_tensor_tensor(out=xo[ob][:], in0=xo[ob][:], scalar=rstd[b2][:], in1=gf_t[:],
                                                                        op0=ALU.mult, op1=ALU.mult), [xo_tk[ob], rstd_tk[b2], gf_tk], [xo_tk[ob]])
                            dma(y_out[t0:t0 + 128, :], xo[ob][:], reads=[xo_tk[ob]])
                S.barrier()
        S.barrier()
        import sys as _sys
        print("[build] nsem", S.nsem, {n: e.count for n, e in S.E.items()}, file=_sys.stderr)
    return nc


def _consts():
    bf = ml_dtypes.bfloat16
    ident = np.eye(128, dtype=np.float32).astype(bf)
    pos = np.arange(4096)
    qaug = np.zeros((5, 4096), np.float32)
    qaug[0] = (pos % 512) // 16
    qaug[1] = pos % 16
    qaug[2] = (pos // 512) * 512
    qaug[3] = 1.0
    qaug[4] = 1.0
    kaug = np.zeros((4, 2, 5, 4096), np.float32)
    for h in range(4):
        s = 2.0 ** (-8.0 * (h + 1) / 4)
        for si, sg in enumerate((1.0, -1.0)):
            kaug[h, si, 0] = -16.0 * s * 8.0 * sg
            kaug[h, si, 1] = -s * 8.0 * sg
            kaug[h, si, 2] = -s * 8.0 * sg
            kaug[h, si, 3] = sg * s * 8.0 * (pos % 128)
            kaug[h, si, 4] = sg * s * 8.0 * ((pos // 128) * 128)
    cdiag = np.zeros((128, 4, 128), np.float32)
    kk = np.arange(128)[:, None]
    qq = np.arange(128)[None, :]
    for h in range(4):
        s = 2.0 ** (-8.0 * (h + 1) / 4)
        cdiag[:, h, :] = np.where(qq < kk, 16.0 * s * (qq - kk), 0.0)
    dilb = np.zeros((128, 12, 256), np.float32)
    kk = np.arange(128)[:, None]
    qq = np.arange(256)[None, :]
    rel = qq - kk - 64
    for hc in range(12):
        s = np.float32(2.0 ** (-8.0 * (hc + 1) / 12))
        d = DIL[hc // 4][1]
        dilb[:, hc, :] = np.where(np.abs(rel) <= 64, -(s * np.float32(d) * np.abs(rel).astype(np.float32)), NEG)
    return dict(ident=ident, qaug=qaug.astype(bf), kaug=kaug.astype(bf), cdiag=cdiag, dilb=dilb)


def _trev(rpb):
    kc = np.arange(64)[:, None]
    qc = np.arange(64)[None, :]
    qstart = np.clip(qc - 8, 0, 48)
    ok = (kc >= qstart) & (kc < qstart + 16)
    cidx = np.clip(kc - qc + 15, 0, 30)
    t = rpb[:, :, :, cidx]
    t = np.where(ok[None, None, None], t, np.float32(NEG))
    t = t[:, :, ::-1]
    return np.ascontiguousarray(np.transpose(t, (0, 1, 3, 2, 4))).astype(np.float32)


_NC_CACHE = {}


def kernel(x_prompt, x_sample, g_norm, w_in, b_gate, rpb, lam_qk, g_diff, w_br_a, w_br_b, w_br_c, w_out, g_final, _depth=DEPTH, _debug=False, _stages="2ABC6", _ncores=8, _trace=False):
    f32 = lambda a: np.ascontiguousarray(np.asarray(a, dtype=np.float32))
    x_prompt, x_sample = f32(x_prompt), f32(x_sample)
    shared = dict(
        w_in=f32(w_in), w_br_a=f32(w_br_a), w_br_b=f32(w_br_b), w_br_c=f32(w_br_c), w_out=f32(w_out),
        gk=f32(np.transpose(np.asarray(g_norm).reshape(DEPTH, 8, 128), (0, 2, 1))),
        bg=f32(np.transpose(np.asarray(b_gate).reshape(DEPTH, 24, 128), (0, 2, 1))),
        gd=f32(np.asarray(g_diff).reshape(DEPTH, 128).T),
        lq=f32(np.asarray(lam_qk).reshape(DEPTH, 256)),
        gf=f32(g_final),
        trev=_trev(f32(rpb)),
    )
    shared.update(_consts())
    key = (_depth, _debug, _stages)
    if key not in _NC_CACHE:
        _NC_CACHE[key] = build_nc(_depth, _debug, _stages)
    nc = _NC_CACHE[key]
    in_maps = []
    for c in range(_ncores):
        xc = np.concatenate([x_prompt[2 * c], x_prompt[2 * c + 1], x_sample[c]], axis=0)
        m = dict(shared)
        m["x"] = np.ascontiguousarray(xc)
        in_maps.append(m)
    res = run_bass_kernel_spmd(nc, in_maps, core_ids=list(range(_ncores)), **({'trace': True} if _trace else {}))
    if _debug or _trace:
        return res
    y_prompt = np.empty((16, 2048, D), np.float32)
    y_sample = np.empty((8, 4096, D), np.float32)
    for c in range(8):
        y = res.results[c]["y"]
        y_prompt[2 * c] = y[0:2048]
        y_prompt[2 * c + 1] = y[2048:4096]
        y_sample[c] = y[4096:8192]
    return (y_prompt, y_sample)
```
